# Optimizing a Trainium2 kernel written in Bass

```python
import jax, jax.numpy as jnp
from jax import lax
import numpy as np

D_MODEL = 1024
BATCH = 8
SEQ = 2048
DEPTH = 2
DEC_BATCH = 128
DEC_SEQ = 4
PAST_LEN = 16384
PAGE_SIZE = 128

N_EVEN = (DEPTH + 1) // 2
N_ODD = DEPTH // 2
MIX_A = D_MODEL // 2
HEAD_A = 64
N_HEADS_A = MIX_A // HEAD_A
LORA_DECAY = 64
LORA_ICLR = 64
LORA_GATE = 128
PROJ_A = 3 * MIX_A + LORA_DECAY + LORA_ICLR + LORA_GATE
MIX_B = D_MODEL - MIX_A
PROJ_B = 3 * MIX_B
CONV_W = 3
CHUNK = 128
GM_WIDTH = D_MODEL
GM_GROUPS = 8
GM_GROUP_DIM = GM_WIDTH // GM_GROUPS
N_MEM = 256
XA_HEADS = 4
XA_HEAD_DIM = D_MODEL // XA_HEADS
D_FF = 2816
RMS_EPS = 1e-6
LN_EPS = 1e-5
GN_EPS = HEAD_A * 1e-5

kernel_name = 'rwkv7_shortconv_chunkgmlp_macaron_decoder_step'


def _rmsnorm(x, g):
    xf = x.astype(jnp.float32)
    y = xf * lax.rsqrt(jnp.mean(xf * xf, axis=-1, keepdims=True) + RMS_EPS)
    return (y * g.astype(jnp.float32)).astype(x.dtype)


def _layernorm(x, g, b, eps):
    xf = x.astype(jnp.float32)
    mu = jnp.mean(xf, axis=-1, keepdims=True)
    var = jnp.mean(jnp.square(xf - mu), axis=-1, keepdims=True)
    y = (xf - mu) * lax.rsqrt(var + eps) * g.astype(jnp.float32) + b.astype(jnp.float32)
    return y.astype(x.dtype)


def _swiglu(x, wg, wu, wd):
    return (jax.nn.silu(x @ wg) * (x @ wu)) @ wd


def _wkv7(r, w, k, v, a, b, s0):
    def step(s, inp):
        rt, wt, kt, vt, at, bt = inp
        sa = jnp.einsum('bhij,bhj->bhi', s, at)
        s = s * wt[:, :, None, :] + sa[..., None] * bt[:, :, None, :] + vt[..., None] * kt[:, :, None, :]
        return s, jnp.einsum('bhij,bhj->bhi', s, rt)
    xs = tuple(jnp.moveaxis(t.astype(jnp.float32), 1, 0) for t in (r, w, k, v, a, b))
    s, ys = lax.scan(step, s0.astype(jnp.float32), xs)
    return jnp.moveaxis(ys, 0, 1), s


def _rwkv7(p, prev, s0, mu, w0, w2, a0, a2, g2, k_k, k_a, r_k, lnx_w, lnx_b):
    bn, t, _ = p.shape
    p_prev = jnp.concatenate([prev[:, None].astype(p.dtype), p[:, :-1]], axis=1)
    ps = p + mu * (p_prev - p)
    o1, o2, o3 = MIX_A, 2 * MIX_A, 3 * MIX_A
    o4 = o3 + LORA_DECAY
    o5 = o4 + LORA_ICLR
    r, k, v = ps[..., :o1], ps[..., o1:o2], ps[..., o2:o3]
    wd, ad, gd = ps[..., o3:o4], ps[..., o4:o5], ps[..., o5:]
    w = -jax.nn.softplus(-(w0 + jnp.tanh(wd) @ w2)) - 0.5
    decay = jnp.exp(-jnp.exp(w.astype(jnp.float32)))
    a = jax.nn.sigmoid(a0 + ad @ a2)
    g = jax.nn.sigmoid(gd) @ g2
    hs = lambda z: z.reshape(bn, t, N_HEADS_A, HEAD_A)
    kk = hs(k * k_k).astype(jnp.float32)
    kk = kk / jnp.maximum(jnp.sqrt(jnp.sum(kk * kk, axis=-1, keepdims=True)), 1e-12)
    k = k * (1.0 + (a - 1.0) * k_a)
    rh, kh, vh, ah = hs(r), hs(k), hs(v), hs(a)
    y, s = _wkv7(rh, hs(decay), kh, vh, -kk, kk * ah, s0)
    y = _layernorm(y.astype(rh.dtype), lnx_w.reshape(N_HEADS_A, HEAD_A), lnx_b.reshape(N_HEADS_A, HEAD_A), GN_EPS)
    y = y + jnp.sum(rh * kh * r_k, axis=-1, keepdims=True) * vh
    out = y.reshape(bn, t, MIX_A) * g
    return out, p[:, -1], s


def _short_conv(p, buf, conv_w):
    h, bg, cg = p[..., :MIX_B], p[..., MIX_B:2 * MIX_B], p[..., 2 * MIX_B:]
    z = cg * h
    zp = jnp.concatenate([buf.astype(z.dtype), z], axis=1)
    t = z.shape[1]
    y = conv_w[0] * zp[:, 0:t]
    for j in range(1, CONV_W):
        y = y + conv_w[j] * zp[:, j:j + t]
    return bg * y, zp[:, -(CONV_W - 1):]


def _chunk_gmlp(p, ln_w, ln_b, w_s, b_s):
    zg = jax.nn.gelu(p, approximate=False)
    u, v = zg[..., :GM_WIDTH], zg[..., GM_WIDTH:]
    v = _layernorm(v, ln_w, ln_b, LN_EPS)
    bn, t, _ = v.shape
    n_chunks = -(-t // CHUNK)
    pad = n_chunks * CHUNK - t
    vc = jnp.pad(v, ((0, 0), (0, pad), (0, 0))).reshape(bn, n_chunks, CHUNK, GM_GROUPS, GM_GROUP_DIM)
    causal = jnp.tril(jnp.ones((CHUNK, CHUNK), dtype=bool))
    ws = jnp.where(causal[None], w_s, jnp.zeros((), w_s.dtype))
    f = jnp.einsum('gts,bnsgc->bntgc', ws, vc) + jnp.transpose(b_s)[None, None, :, :, None]
    f = f.reshape(bn, n_chunks * CHUNK, GM_WIDTH)[:, :t]
    return u * f, v


def _mem_attn(x, mk, mv, wq, wo):
    bn, t, _ = x.shape
    q = (x @ wq).reshape(bn, t, XA_HEADS, XA_HEAD_DIM)
    sc = jnp.einsum('bthd,bmhd->bhtm', q, mk.astype(q.dtype)).astype(jnp.float32) * (XA_HEAD_DIM ** -0.5)
    pr = jax.nn.softmax(sc, axis=-1).astype(x.dtype)
    o = jnp.einsum('bhtm,bmhd->bthd', pr, mv.astype(x.dtype)).reshape(bn, t, D_MODEL)
    return o @ wo


def _trunk(x, mem_k, mem_v, shift0, wkv0, conv0, P):
    shifts, wkvs, convs, vrows = [], [], [], []
    for l in range(DEPTH):
        n = P['norms'][l]
        x = x + 0.5 * _swiglu(_rmsnorm(x, n[0]), P['f1_wg'][l], P['f1_wu'][l], P['f1_wd'][l])
        h = _rmsnorm(x, n[1])
        if l % 2 == 0:
            e = l // 2
            p = h @ P['w_in_even'][e]
            ya, sh, s = _rwkv7(p[..., :PROJ_A], shift0[e], wkv0[e], P['shift_mu'][e],
                               P['decay_w0'][e], P['decay_w2'][e], P['iclr_a0'][e], P['iclr_a2'][e],
                               P['gate_g2'][e], P['k_k'][e], P['k_a'][e], P['r_k'][e],
                               P['lnx_w'][e], P['lnx_b'][e])
            yb, cb = _short_conv(p[..., PROJ_A:], conv0[e], P['conv_w'][e])
            x = x + jnp.concatenate([ya.astype(x.dtype), yb.astype(x.dtype)], axis=-1) @ P['w_out_even'][e]
            shifts.append(sh)
            wkvs.append(s)
            convs.append(cb)
        else:
            o = l // 2
            p = h @ P['w_in_odd'][o]
            yc, vr = _chunk_gmlp(p, P['gm_ln_w'][o], P['gm_ln_b'][o], P['gm_ws'][o], P['gm_bs'][o])
            x = x + yc @ P['w_out_odd'][o]
            vrows.append(vr)
        x = x + _mem_attn(_rmsnorm(x, n[2]), mem_k[l], mem_v[l], P['xa_wq'][l], P['xa_wo'][l])
        x = x + 0.5 * _swiglu(_rmsnorm(x, n[3]), P['f2_wg'][l], P['f2_wu'][l], P['f2_wd'][l])
    return _rmsnorm(x, P['final_norm']), shifts, wkvs, convs, vrows


def setup_inputs(seed: int = 0) -> dict:
    key = jax.random.key(seed)
    ks = iter(jax.random.split(key, 48))
    def nrm(shape, scale=1.0, offset=0.0):
        return offset + scale * jax.random.normal(next(ks), shape, jnp.float32)
    D = D_MODEL
    inp = {}
    inp['x_prompt'] = nrm((BATCH, SEQ, D))
    inp['x_sample'] = nrm((DEC_BATCH, DEC_SEQ, D))
    inp['mem_prompt'] = nrm((BATCH, N_MEM, D))
    inp['state_shift'] = nrm((N_EVEN, DEC_BATCH, PROJ_A))
    inp['state_wkv'] = nrm((N_EVEN, DEC_BATCH, N_HEADS_A, HEAD_A, HEAD_A), 0.5)
    inp['state_conv'] = nrm((N_EVEN, DEC_BATCH, CONV_W - 1, MIX_B))
    inp['cache_mem_k'] = nrm((DEPTH, DEC_BATCH, N_MEM, XA_HEADS, XA_HEAD_DIM))
    inp['cache_mem_v'] = nrm((DEPTH, DEC_BATCH, N_MEM, XA_HEADS, XA_HEAD_DIM))
    inp['norms'] = nrm((DEPTH, 4, D), 0.01, 1.0)
    inp['final_norm'] = nrm((D,), 0.01, 1.0)
    inp['f1_wg'] = nrm((DEPTH, D, D_FF), D ** -0.5)
    inp['f1_wu'] = nrm((DEPTH, D, D_FF), D ** -0.5)
    inp['f1_wd'] = nrm((DEPTH, D_FF, D), D_FF ** -0.5)
    inp['f2_wg'] = nrm((DEPTH, D, D_FF), D ** -0.5)
    inp['f2_wu'] = nrm((DEPTH, D, D_FF), D ** -0.5)
    inp['f2_wd'] = nrm((DEPTH, D_FF, D), D_FF ** -0.5)
    inp['xa_wq'] = nrm((DEPTH, D, D), D ** -0.5)
    inp['xa_wk'] = nrm((DEPTH, D, D), D ** -0.5)
    inp['xa_wv'] = nrm((DEPTH, D, D), D ** -0.5)
    inp['xa_wo'] = nrm((DEPTH, D, D), D ** -0.5)
    inp['w_in_even'] = nrm((N_EVEN, D, PROJ_A + PROJ_B), D ** -0.5)
    inp['w_out_even'] = nrm((N_EVEN, D, D), D ** -0.5)
    inp['shift_mu'] = jax.random.uniform(next(ks), (N_EVEN, PROJ_A), jnp.float32)
    inp['decay_w0'] = nrm((N_EVEN, MIX_A), 0.5, -2.0)
    inp['decay_w2'] = nrm((N_EVEN, LORA_DECAY, MIX_A), 0.5 * LORA_DECAY ** -0.5)
    inp['iclr_a0'] = nrm((N_EVEN, MIX_A), 0.1)
    inp['iclr_a2'] = nrm((N_EVEN, LORA_ICLR, MIX_A), LORA_ICLR ** -0.5)
    inp['gate_g2'] = nrm((N_EVEN, LORA_GATE, MIX_A), LORA_GATE ** -0.5)
    inp['k_k'] = nrm((N_EVEN, MIX_A), 0.05, 0.85)
    inp['k_a'] = nrm((N_EVEN, MIX_A), 0.05, 1.0)
    inp['r_k'] = nrm((N_EVEN, N_HEADS_A, HEAD_A), 0.1)
    inp['lnx_w'] = nrm((N_EVEN, MIX_A), 0.01, 1.0)
    inp['lnx_b'] = nrm((N_EVEN, MIX_A), 0.01)
    inp['conv_w'] = nrm((N_EVEN, CONV_W, MIX_B), CONV_W ** -0.5)
    inp['w_in_odd'] = nrm((N_ODD, D, 2 * GM_WIDTH), D ** -0.5)
    inp['w_out_odd'] = nrm((N_ODD, GM_WIDTH, D), GM_WIDTH ** -0.5)
    inp['gm_ln_w'] = nrm((N_ODD, GM_WIDTH), 0.01, 1.0)
    inp['gm_ln_b'] = nrm((N_ODD, GM_WIDTH), 0.01)
    inp['gm_ws'] = nrm((N_ODD, GM_GROUPS, CHUNK, CHUNK), CHUNK ** -0.5)
    inp['gm_bs'] = nrm((N_ODD, GM_GROUPS, CHUNK), 0.1)
    return inp


def reference(x_prompt, x_sample, mem_prompt, state_shift, state_wkv, state_conv, cache_mem_k, cache_mem_v,
              norms, final_norm, f1_wg, f1_wu, f1_wd, f2_wg, f2_wu, f2_wd, xa_wq, xa_wk, xa_wv, xa_wo,
              w_in_even, w_out_even, shift_mu, decay_w0, decay_w2, iclr_a0, iclr_a2, gate_g2, k_k, k_a, r_k,
              lnx_w, lnx_b, conv_w, w_in_odd, w_out_odd, gm_ln_w, gm_ln_b, gm_ws, gm_bs):
    P = dict(norms=norms, final_norm=final_norm, f1_wg=f1_wg, f1_wu=f1_wu, f1_wd=f1_wd,
             f2_wg=f2_wg, f2_wu=f2_wu, f2_wd=f2_wd, xa_wq=xa_wq, xa_wo=xa_wo,
             w_in_even=w_in_even, w_out_even=w_out_even, shift_mu=shift_mu, decay_w0=decay_w0,
             decay_w2=decay_w2, iclr_a0=iclr_a0, iclr_a2=iclr_a2, gate_g2=gate_g2, k_k=k_k, k_a=k_a,
             r_k=r_k, lnx_w=lnx_w, lnx_b=lnx_b, conv_w=conv_w, w_in_odd=w_in_odd, w_out_odd=w_out_odd,
             gm_ln_w=gm_ln_w, gm_ln_b=gm_ln_b, gm_ws=gm_ws, gm_bs=gm_bs)
    bp = x_prompt.shape[0]
    new_mem_k_p = jnp.einsum('bmd,lde->lbme', mem_prompt, xa_wk).reshape(DEPTH, bp, N_MEM, XA_HEADS, XA_HEAD_DIM)
    new_mem_v_p = jnp.einsum('bmd,lde->lbme', mem_prompt, xa_wv).reshape(DEPTH, bp, N_MEM, XA_HEADS, XA_HEAD_DIM)
    shift0 = jnp.zeros((N_EVEN, bp, PROJ_A), x_prompt.dtype)
    wkv0 = jnp.zeros((N_EVEN, bp, N_HEADS_A, HEAD_A, HEAD_A), jnp.float32)
    conv0 = jnp.zeros((N_EVEN, bp, CONV_W - 1, MIX_B), x_prompt.dtype)
    y_prompt, p_sh, p_wkv, p_cv, _ = _trunk(x_prompt, new_mem_k_p, new_mem_v_p, shift0, wkv0, conv0, P)
    y_sample, s_sh, s_wkv, s_cv, s_v = _trunk(x_sample, cache_mem_k, cache_mem_v, state_shift, state_wkv,
                                              state_conv, P)
    new_shift_p = jnp.stack(p_sh)
    new_wkv_p = jnp.stack(p_wkv)
    new_conv_p = jnp.stack(p_cv)
    new_shift_s = jnp.stack(s_sh)
    new_wkv_s = jnp.stack(s_wkv)
    new_conv_s = jnp.stack(s_cv)
    new_gmlp_v_s = jnp.stack(s_v)
    return (y_prompt, y_sample, new_shift_p, new_wkv_p, new_conv_p, new_mem_k_p, new_mem_v_p,
            new_shift_s, new_wkv_s, new_conv_s, new_gmlp_v_s)
```

```python
import math
import numpy as np
import concourse.bass as bass
import concourse.mybir as mybir
from concourse.bass_utils import run_bass_kernel_spmd
from contextlib import ExitStack

F32 = mybir.dt.float32
BF16 = mybir.dt.bfloat16
AF = mybir.ActivationFunctionType
ALU = mybir.AluOpType
AX = mybir.AxisListType

NCORES = 8
D = 1024
KC = 8
SEQ = 2048
NS = 64
NB = 16
T = SEQ + NS
DFF = 2816
NFC = 22
PROJ_A = 1792
PROJ_B = 1536
TILES = [(0, 512), (512, 512), (1024, 512), (1536, 512), (2048, 64)]
RMS_EPS = 1e-6


class Res:
    __slots__ = ("name", "w", "r")

    def __init__(self, name):
        self.name = name
        self.w = None
        self.r = []


class Eng:
    def __init__(self, name, e, sem, same_sync):
        self.name = name
        self.e = e
        self.sem = sem
        self.cnt = 0
        self.pend = False
        self.seen = {}
        self.same_sync = same_sync


class DSem:
    def __init__(self, sem):
        self.sem = sem
        self.total = 0


class TK:
    def __init__(self, nc, es, n_dma_sems=20, same_sync=True):
        self.nc = nc
        self.es = es
        mk = lambda n: es.enter_context(nc.semaphore(n))
        self.mk = mk
        self.pe = Eng("pe", nc.tensor, mk("s_pe"), False)
        self.dve = Eng("dve", nc.vector, mk("s_dve"), same_sync)
        self.act = Eng("act", nc.scalar, mk("s_act"), same_sync)
        self.pool = Eng("pool", nc.gpsimd, mk("s_pool"), same_sync)
        self.sp = Eng("sp", nc.sync, mk("s_sp"), False)
        self.engs = [self.pe, self.dve, self.act, self.pool, self.sp]
        self.dsems = [DSem(mk(f"s_d{i}")) for i in range(n_dma_sems)]
        self.dpool = {"pool": self.dsems[:n_dma_sems // 2], "hw": self.dsems[n_dma_sems // 2:]}
        self.dnext = {"pool": 0, "hw": 0}
        self.res = {}
        self.nwaits = 0
        self.nops = 0

    def R(self, *key):
        r = self.res.get(key)
        if r is None:
            r = self.res[key] = Res(key)
        return r

    def new_dsem(self, name):
        return DSem(self.mk(name))

    def _wait(self, eng, tok):
        if tok[0] == "E":
            _, src, n = tok
            if src is eng and not eng.same_sync:
                return
            if src.cnt < n:
                raise RuntimeError(f"wait on un-emitted milestone {src.name}:{n} (cnt={src.cnt}) from {eng.name}")
            key = src.name
            semobj = src.sem
        else:
            _, d, n = tok
            key = id(d)
            semobj = d.sem
        if eng.seen.get(key, 0) >= n:
            return
        eng.e.wait_ge(semobj, n)
        eng.seen[key] = n
        self.nwaits += 1

    @staticmethod
    def _deps(reads, writes):
        deps = []
        for r in reads:
            if r.w is not None:
                deps.append(r.w)
        for w in writes:
            if w.w is not None:
                deps.append(w.w)
            deps.extend(w.r)
        return deps

    @staticmethod
    def _compact(toks):
        best = {}
        for t in toks:
            k = id(t[1])
            if k not in best or best[k][2] < t[2]:
                best[k] = t
        return list(best.values())

    def _record(self, tok, reads, writes):
        for r in reads:
            r.r.append(tok)
            if len(r.r) > 48:
                r.r = self._compact(r.r)
        for w in writes:
            w.w = tok
            w.r = []

    def op(self, eng, fn, reads=(), writes=(), inc=True):
        if any(r.name[0] == "ps" for r in reads):
            writes = list(writes) + [r for r in reads if r.name[0] == "ps"]
            reads = [r for r in reads if r.name[0] != "ps"]
        for tok in self._deps(reads, writes):
            self._wait(eng, tok)
        ins = fn()
        self.nops += 1
        if inc:
            eng.cnt += 1
            ins.then_inc(eng.sem, 1)
            eng.pend = False
            tok = ("E", eng, eng.cnt)
        else:
            eng.pend = True
            tok = ("E", eng, eng.cnt + 1)
        self._record(tok, reads, writes)
        return ins

    def dma(self, q, out, in_, reads=(), writes=(), dsem=None, **kw):
        for tok in self._deps(reads, writes):
            self._wait(q, tok)
        if dsem is None:
            kind = "pool" if q is self.pool else "hw"
            lst = self.dpool[kind]
            dsem = lst[self.dnext[kind]]
            self.dnext[kind] = (self.dnext[kind] + 1) % len(lst)
            if dsem.total:
                self._wait(q, ("D", dsem, dsem.total))
        ins = q.e.dma_start(out=out, in_=in_, **kw)
        dsem.total += 16
        ins.then_inc(dsem.sem, 16)
        tok = ("D", dsem, dsem.total)
        self._record(tok, reads, writes)
        return tok

    def barrier(self):
        toks = []
        for e in self.engs:
            if e.pend:
                raise RuntimeError(f"barrier with pending instrs on {e.name}")
            if e.cnt:
                toks.append(("E", e, e.cnt))
        for d in self.dsems:
            if d.total:
                toks.append(("D", d, d.total))
        for e in self.engs:
            for t in toks:
                if t[0] == "E" and t[1] is e:
                    continue
                self._wait(e, t)


def _consts():
    c = {}
    c["ident"] = np.eye(128, dtype=np.float32)
    c["onesm"] = np.full((128, 128), 1.0 / D, dtype=np.float32)
    c["ones"] = np.ones((128, 128), dtype=np.float32)
    c["triu"] = np.triu(np.ones((128, 128), dtype=np.float32))
    i = np.arange(64)
    c["mask_blk"] = ((i[:, None] // 4 == i[None, :] // 4) & (i[:, None] <= i[None, :])).astype(np.float32)
    i2 = np.arange(128)
    sI, tI = i2[:, None], i2[None, :]
    same = (sI // 4 == tI // 4) & (sI < 64) & (tI < 64)
    f = lambda m: m.astype(np.float32)
    c["tri"] = f(sI <= tI); c["sut"] = f(sI > tI)
    c["trib"] = f((sI <= tI) & same); c["sutb"] = f((sI > tI) & same)
    strict = f(sI < tI); incl = f(sI <= tI)
    c["mask2"] = np.concatenate([strict, incl], axis=1)
    c["mask2b"] = np.concatenate([strict * same, incl * same], axis=1).astype(np.float32)
    c["maskxt"] = f(sI > tI)
    c["maskxtb"] = f((sI > tI) & same)
    c["rowmask"] = f((i2[:, None] // 4 == np.arange(NB)[None, :]) & (i2[:, None] < 64))
    bo = np.zeros((128, 128), np.float32); bo[:64, :64] = 1; bo[64:, 64:] = 1
    c["bones"] = bo; c["bones64"] = bo / 64.0
    return c


def _pack_pfm(inp):
    cols = []

    def add(v):
        v = np.asarray(v, np.float32).reshape(-1, 128)
        for r in v:
            cols.append(r)
    idx = {}
    idx["norms"] = len(cols)
    add(inp["norms"].reshape(8, D))
    idx["final"] = len(cols)
    add(inp["final_norm"])
    for nm in ("shift_mu", "decay_w0", "iclr_a0", "k_k", "k_a", "r_k", "lnx_w", "lnx_b"):
        idx[nm] = len(cols)
        add(inp[nm][0])
    for i in range(3):
        idx[f"cw{i}"] = len(cols)
        add(inp["conv_w"][0, i])
    return np.stack(cols, axis=1).copy(), idx


PFM_IDX = {"norms": 0, "final": 64, "shift_mu": 72, "decay_w0": 86, "iclr_a0": 90, "k_k": 94, "k_a": 98, "r_k": 102,
           "lnx_w": 106, "lnx_b": 110, "cw0": 114, "cw1": 118, "cw2": 122}
PFM_NCOL = 126


class Kern:
    def __init__(self, nc, es, tk, cfg):
        self.nc = nc
        self.es = es
        self.tk = tk
        self.cfg = cfg
        self.out_toks = []
        self.dr = {}

    def sb(self, name, shape, dt, es=None):
        return (es or self.es).enter_context(self.nc.sbuf_tensor(name, shape, dt))

    def din(self, name, shape, dt=F32):
        t = self.nc.dram_tensor(name, list(shape), dt, kind="ExternalInput").ap()
        self.dr[name] = t
        return t

    def dout(self, name, shape, dt=F32):
        t = self.nc.dram_tensor(name, list(shape), dt, kind="ExternalOutput").ap()
        self.dr[name] = t
        return t

    def declare(self):
        d = self.din
        d("xp", [SEQ, D]); d("xs", [NS, D])
        d("pfm", [128, PFM_NCOL]); d("ident", [128, 128]); d("onesm", [128, 128])
        for nm in ("f1_wg", "f1_wu", "f2_wg", "f2_wu"):
            d(nm, [2, D, DFF])
        for nm in ("f1_wd", "f2_wd"):
            d(nm, [2, DFF, D])
        d("memp", [256, D]); d("ck", [2, NB, 256, D]); d("cv", [2, NB, 256, D])
        d("ones", [128, 128])
        for nm in ("xa_wq", "xa_wk", "xa_wv", "xa_wo"):
            d(nm, [2, D, D])
        d("w_in_odd", [1, D, 2 * D]); d("w_out_odd", [1, D, D]); d("gm_ln_w", [1, D]); d("gm_ln_b", [1, D])
        d("gm_ws", [1, 8, 128, 128]); d("gm_bs", [1, 8, 128]); d("triu", [128, 128]); d("mask_blk", [64, 64])
        d("w_in_even", [1, D, 3328]); d("w_out_even", [1, D, D]); d("decay_w2", [1, 64, 512]); d("iclr_a2", [1, 64, 512]); d("gate_g2", [1, 128, 512])
        d("st_shift", [NB, PROJ_A]); d("st_wkv", [NB, 8, 64, 64]); d("st_conv", [NB, 2, 512])
        for nm, shp in (("tri", [128, 128]), ("sut", [128, 128]), ("mask2", [128, 256]), ("maskxt", [128, 128]),
                        ("trib", [128, 128]), ("sutb", [128, 128]), ("mask2b", [128, 256]), ("maskxtb", [128, 128]),
                        ("rowmask", [128, NB]), ("bones64", [128, 128]), ("bones", [128, 128])):
            d("c_" + nm, shp)
        o = self.dout
        o("sh_p", [1, PROJ_A]); o("wkv_p", [8, 64, 64]); o("conv_p", [2, 512])
        o("sh_s", [NB, PROJ_A]); o("wkv_s", [NB, 8, 64, 64]); o("conv_s", [NB, 2, 512])
        o("gv_s", [NS, D])
        o("y_p", [SEQ, D]); o("y_s", [NS, D])
        o("mk_p", [2, 256, D]); o("mv_p", [2, 256, D])

    def setup(self):
        nc, tk = self.nc, self.tk
        self.xT = self.sb("xT", [128, KC, T], F32)
        self.hT = None
        self.pfm = self.sb("pfm_sb", [128, PFM_NCOL], F32)
        self.ident = self.sb("ident_sb", [128, 128], F32)
        self.onesm = self.sb("onesm_sb", [128, 128], BF16)
        self.ps = [self.es.enter_context(nc.psum_tensor(f"ps{i}", [128, 512], F32)) for i in range(8)]
        self.rps = [tk.R("ps", i) for i in range(8)]
        self.NSLOT = 3
        self.ring = self.sb("ring", [128, self.NSLOT, 4096], BF16)
        self.rslot = [tk.R("slot", i) for i in range(self.NSLOT)]
        self.slot_sem = [tk.new_dsem(f"s_slot{i}") for i in range(self.NSLOT)]
        self.slot_next = 0
        self.r_const = tk.R("const")
        tk.dma(tk.sp, self.pfm[:], self.dr["pfm"], writes=[self.r_const])
        tk.dma(tk.sp, self.ident[:], self.dr["ident"], writes=[self.r_const])
        tk.dma(tk.pool, self.onesm[:], self.dr["onesm"], writes=[self.r_const])
        self.ident_bf = self.sb("ident_bf", [128, 128], BF16)
        self.ones_bf = self.sb("ones_bf", [128, 128], BF16)
        tk.dma(tk.pool, self.ident_bf[:], self.dr["ident"], writes=[self.r_const])
        tk.dma(tk.pool, self.ones_bf[:], self.dr["ones"], writes=[self.r_const])

    def alloc_h(self, les, tag):
        self.hT = self.sb(f"hT_{tag}", [128, KC, T], BF16, les)

    def rx(self, ti):
        return self.tk.R("x", ti)

    def rh(self, ti):
        return self.tk.R("h", ti)

    def load_x(self):
        nc, tk = self.nc, self.tk
        with ExitStack() as les:
            stg = [self.sb(f"xstg{i}", [128, D], F32, les) for i in range(2)]
            rstg = [tk.R("xstg", i) for i in range(2)]
            blocks = [("xp", b * 128, 128, b * 128) for b in range(SEQ // 128)] + [("xs", 0, NS, SEQ)]
            for bi, (src, r0, n, t0) in enumerate(blocks):
                s = bi % 2
                tk.dma(tk.sp, stg[s][0:n, :], self.dr[src][r0:r0 + n, :], writes=[rstg[s]])
                ti = min(t0 // 512, 4)
                for half in range(2):
                    pb = (bi * 2 + half) % 8
                    for q in range(4):
                        kc = half * 4 + q
                        tk.op(tk.pe, lambda: nc.tensor.transpose(self.ps[pb][:, q * 128:q * 128 + n],
                                                                stg[s][0:n, kc * 128:(kc + 1) * 128],
                                                                self.ident[0:n, 0:n]),
                              reads=[rstg[s], self.r_const], writes=[self.rps[pb]], inc=(q == 3))
                    src_ps = self.ps[pb][:].rearrange("p (q t) -> p q t", q=4)[:, :, 0:n]
                    dst = self.xT[:, half * 4:half * 4 + 4, t0:t0 + n]
                    if half == 0:
                        tk.op(tk.dve, lambda: nc.vector.tensor_copy(dst, src_ps), reads=[self.rps[pb]], writes=[self.rx(ti)])
                    else:
                        tk.op(tk.act, lambda: nc.scalar.copy(dst, src_ps), reads=[self.rps[pb]], writes=[self.rx(ti)])
            tk.barrier()

    def rmsnorm_bufs(self, tag, les, n=512, nrs=2):
        tk = self.tk
        b = {}
        b["sq"] = [self.sb(f"sq{i}_{tag}", [128, n], BF16, les) for i in range(2)]
        b["rsq"] = [tk.R("sq", i) for i in range(2)]
        b["sd"] = self.sb(f"sd_{tag}", [128, n], F32, les)
        b["rs"] = [self.sb(f"rs{i}_{tag}", [128, n], F32, les) for i in range(nrs)]
        b["rsd"] = tk.R("sd")
        b["rrs"] = [tk.R("rs", i) for i in range(2)]
        return b

    def rmsnorm_tile(self, ti, gcol, b, dst_fn=None, wr=None):
        nc, tk = self.nc, self.tk
        t0, n = TILES[ti]
        sq, rsq, sd, rs, rsd, rrs = b["sq"], b["rsq"], b["sd"], b["rs"], b["rsd"], b["rrs"]
        pb = 6 + ti % 2
        for kc in range(KC):
            s = kc % 2
            tk.op(tk.act, lambda: nc.scalar.activation(out=sq[s][:, 0:n], in_=self.xT[:, kc, t0:t0 + n], func=AF.Square),
                  reads=[self.rx(ti)], writes=[rsq[s]])
            tk.op(tk.pe, lambda: nc.tensor.matmul(self.ps[pb][:, 0:n], self.onesm[:], sq[s][:, 0:n],
                                                  start=(kc == 0), stop=(kc == KC - 1)),
                  reads=[rsq[s], self.r_const], writes=[self.rps[pb]], inc=True)
        tk.op(tk.act, lambda: nc.scalar.activation(out=sd[:, 0:n], in_=self.ps[pb][:, 0:n], func=AF.Ln, bias=self.eps_t[:, 0:1]),
              reads=[self.rps[pb], self.r_const], writes=[rsd])
        r = ti % 2
        tk.op(tk.act, lambda: nc.scalar.activation(out=rs[r][:, 0:n], in_=sd[:, 0:n], func=AF.Exp, scale=-0.5), reads=[rsd], writes=[rrs[r]])
        for kc in range(KC):
            if dst_fn is None:
                dst = self.hT[:, kc, t0:t0 + n]
                w = [self.rh(ti)]
            else:
                dst = dst_fn(kc)
                w = wr
            tk.op(tk.dve, lambda: nc.vector.scalar_tensor_tensor(out=dst, in0=self.xT[:, kc, t0:t0 + n],
                                                                 scalar=self.pfm[:, gcol + kc:gcol + kc + 1],
                                                                 in1=rs[r][:, 0:n], op0=ALU.mult, op1=ALU.mult),
                  reads=[self.rx(ti), rrs[r], self.r_const], writes=w)

    def rmsnorm(self, gcol, les=None, keep=False):
        if keep:
            b = self.rmsnorm_bufs(f"g{gcol}", les)
            for ti in range(len(TILES)):
                self.rmsnorm_tile(ti, gcol, b)
            return
        with ExitStack() as ies:
            b = self.rmsnorm_bufs(f"g{gcol}", ies)
            for ti in range(len(TILES)):
                self.rmsnorm_tile(ti, gcol, b)
            self.tk.barrier()

    def next_slot(self):
        s = self.slot_next
        self.slot_next = (s + 1) % self.NSLOT
        return s

    def ffn(self, wg, wu, wd, gcol, tag):
        nc, tk = self.nc, self.tk
        with ExitStack() as les:
            self.alloc_h(les, tag)
            self.rmsnorm(gcol, les, keep=True)
            GROUPS = [(0, 12), (12, 10)]
            aT = self.sb(f"aT_{tag}", [128, 12, T], BF16, les)
            sg = [self.sb(f"sg{i}_{tag}", [128, 512], BF16, les) for i in range(2)]
            rsg = [tk.R("sg", i) for i in range(2)]
            cnt = 0
            for (f0, nf) in GROUPS:
                slabs = [(f0 + 2 * i) for i in range(nf // 2)]
                loaded = {}

                def load_up(i):
                    fc = slabs[i]
                    s = self.next_slot()
                    vg = self.ring[:, s, 0:2048].rearrange("p (k f) -> p k f", k=KC)
                    vu = self.ring[:, s, 2048:4096].rearrange("p (k f) -> p k f", k=KC)
                    tk.dma(tk.pool, vg, wg[:, fc * 128:(fc + 2) * 128].rearrange("(k p) f -> p k f", p=128),
                           writes=[self.rslot[s]], dsem=self.slot_sem[s])
                    tk.dma(tk.pool, vu, wu[:, fc * 128:(fc + 2) * 128].rearrange("(k p) f -> p k f", p=128),
                           writes=[self.rslot[s]], dsem=self.slot_sem[s])
                    loaded[i] = (s, vg, vu)
                PRE = self.NSLOT - 1
                for i in range(min(PRE, len(slabs))):
                    load_up(i)
                for i in range(len(slabs)):
                    s, vg, vu = loaded[i]
                    for fl in range(2):
                        fcg = slabs[i] - f0 + fl
                        for ti, (t0, n) in enumerate(TILES):
                            pa = (cnt * 2) % 6
                            pbk = pa + 1
                            cnt += 1
                            for kc in range(KC):
                                tk.op(tk.pe, lambda: nc.tensor.matmul(self.ps[pa][:, 0:n], vg[:, kc, fl * 128:(fl + 1) * 128],
                                                                      self.hT[:, kc, t0:t0 + n], start=(kc == 0), stop=(kc == KC - 1)),
                                      reads=[self.rslot[s], self.rh(ti)], writes=[self.rps[pa]], inc=(kc == KC - 1))
                            for kc in range(KC):
                                tk.op(tk.pe, lambda: nc.tensor.matmul(self.ps[pbk][:, 0:n], vu[:, kc, fl * 128:(fl + 1) * 128],
                                                                      self.hT[:, kc, t0:t0 + n], start=(kc == 0), stop=(kc == KC - 1)),
                                      reads=[self.rslot[s], self.rh(ti)], writes=[self.rps[pbk]], inc=(kc == KC - 1))
                            q = cnt % 2
                            tk.op(tk.act, lambda: nc.scalar.activation(out=sg[q][:, 0:n], in_=self.ps[pa][:, 0:n], func=AF.Silu),
                                  reads=[self.rps[pa]], writes=[rsg[q]])
                            tk.op(tk.dve, lambda: nc.vector.tensor_tensor(out=aT[:, fcg, t0:t0 + n], in0=sg[q][:, 0:n],
                                                                          in1=self.ps[pbk][:, 0:n], op=ALU.mult),
                                  reads=[rsg[q], self.rps[pbk]], writes=[tk.R("aT", ti)])
                    if i + PRE < len(slabs):
                        load_up(i + PRE)
                dl = {}

                def load_dn(j):
                    s = self.next_slot()
                    v = self.ring[:, s, 0:nf * 256].rearrange("p (f d) -> p f d", f=nf)
                    tk.dma(tk.pool, v, wd[f0 * 128:(f0 + nf) * 128, j * 256:(j + 1) * 256].rearrange("(f p) d -> p f d", p=128),
                           writes=[self.rslot[s]], dsem=self.slot_sem[s])
                    dl[j] = (s, v)
                for j in range(min(PRE, 4)):
                    load_dn(j)
                for j in range(4):
                    s, v = dl[j]
                    for ti, (t0, n) in enumerate(TILES):
                        for dlc in range(2):
                            dc = j * 2 + dlc
                            pc = 6 + cnt % 2
                            cnt += 1
                            for fcg in range(nf):
                                tk.op(tk.pe, lambda: nc.tensor.matmul(self.ps[pc][:, 0:n], v[:, fcg, dlc * 128:(dlc + 1) * 128],
                                                                      aT[:, fcg, t0:t0 + n], start=(fcg == 0), stop=(fcg == nf - 1)),
                                      reads=[self.rslot[s], tk.R("aT", ti)], writes=[self.rps[pc]], inc=(fcg == nf - 1))
                            tk.op(tk.dve, lambda: nc.vector.scalar_tensor_tensor(out=self.xT[:, dc, t0:t0 + n], in0=self.ps[pc][:, 0:n],
                                                                                 scalar=0.5, in1=self.xT[:, dc, t0:t0 + n],
                                                                                 op0=ALU.mult, op1=ALU.add),
                                  reads=[self.rps[pc], self.rx(ti)], writes=[self.rx(ti)])
                    if j + PRE < 4:
                        load_dn(j + PRE)
            tk.barrier()

    def load_mem(self, les_outer, l=0):
        nc, tk = self.nc, self.tk
        self.memT = self.sb(f"memT{l}", [128, KC, 256], BF16, les_outer)
        with ExitStack() as les:
            stg = [self.sb(f"mstg{l}_{i}", [128, D], F32, les) for i in range(2)]
            rstg = [tk.R("mstg", i) for i in range(2)]
            for mc in range(2):
                tk.dma(tk.sp, stg[mc][:], self.dr["memp"][mc * 128:(mc + 1) * 128, :], writes=[rstg[mc]])
                for half in range(2):
                    pb = mc * 2 + half
                    for q in range(4):
                        kc = half * 4 + q
                        tk.op(tk.pe, lambda: nc.tensor.transpose(self.ps[pb][:, q * 128:(q + 1) * 128],
                                                                stg[mc][:, kc * 128:(kc + 1) * 128], self.ident[:]),
                              reads=[rstg[mc], self.r_const], writes=[self.rps[pb]], inc=(q == 3))
                    src_ps = self.ps[pb][:].rearrange("p (q t) -> p q t", q=4)
                    tk.op(tk.dve, lambda: nc.vector.tensor_copy(self.memT[:, half * 4:half * 4 + 4, mc * 128:(mc + 1) * 128], src_ps),
                          reads=[self.rps[pb]], writes=[tk.R("memT")])
            tk.barrier()

    def load_w512(self, w, c0):
        tk = self.tk
        s = self.next_slot()
        v = self.ring[:, s, :].rearrange("p (k f) -> p k f", k=KC)
        tk.dma(tk.pool, v, w[:, c0:c0 + 512].rearrange("(k p) f -> p k f", p=128), writes=[self.rslot[s]], dsem=self.slot_sem[s])
        return s, v

    def xattn(self, l):
        nc, tk = self.nc, self.tk
        dr = self.dr
        gcol = PFM_IDX["norms"] + (l * 4 + 2) * 8
        SC = 1.0 / 16.0
        with ExitStack() as les:
            self.load_mem(les, l)
            self.alloc_h(les, f"xa{l}")
            qT = self.sb(f"qT{l}", [128, KC, T], BF16, les)
            KT = self.sb(f"KT{l}", [128, KC, 256], BF16, les)
            Vtm = self.sb(f"Vtm{l}", [128, 2, D], BF16, les)
            kvst = [self.sb(f"kvst{l}_{i}", [128, 512], F32, les) for i in range(2)]
            rkvst = [tk.R("kvst", i) for i in range(2)]
            rKT = tk.R("KT"); rV = tk.R("Vtm")
            cnt = 0
            for wi, (wname, oname) in enumerate((("xa_wk", "mk_p"), ("xa_wv", "mv_p"))):
                w = dr[wname][l]
                for half in range(2):
                    s, v = self.load_w512(w, half * 512)
                    sub = self.cfg.get("xa_sub", "")
                    if wi == 0 and "nokt" not in sub:
                        for q in range(4):
                            ec = half * 4 + q
                            pb = cnt % 4; cnt += 1
                            for kc in range(KC):
                                tk.op(tk.pe, lambda: nc.tensor.matmul(self.ps[pb][:, 0:256], v[:, kc, q * 128:(q + 1) * 128], self.memT[:, kc, :],
                                                                      start=(kc == 0), stop=(kc == KC - 1)),
                                      reads=[self.rslot[s], tk.R("memT")], writes=[self.rps[pb]], inc=(kc == KC - 1))
                            tk.op(tk.act, lambda: nc.scalar.copy(KT[:, ec, :], self.ps[pb][:, 0:256]), reads=[self.rps[pb]], writes=[rKT])
                    for mc in range(2 if "noktm" not in sub else 0):
                        pb = cnt % 4; cnt += 1
                        for kc in range(KC):
                            tk.op(tk.pe, lambda: nc.tensor.matmul(self.ps[pb][:], self.memT[:, kc, mc * 128:(mc + 1) * 128], v[:, kc, :],
                                                                  start=(kc == 0), stop=(kc == KC - 1)),
                                  reads=[self.rslot[s], tk.R("memT")], writes=[self.rps[pb]], inc=(kc == KC - 1))
                        st = cnt % 2
                        tk.op(tk.dve, lambda: nc.vector.tensor_copy(kvst[st][:], self.ps[pb][:]), reads=[self.rps[pb]], writes=[rkvst[st]])
                        if wi == 1 and "novt" not in sub:
                            tk.op(tk.act, lambda: nc.scalar.copy(Vtm[:, mc, half * 512:(half + 1) * 512], kvst[st][:]),
                                  reads=[rkvst[st]], writes=[rV])
                        if "noout" not in sub:
                            self.out_toks.append(tk.dma(tk.sp, dr[oname][l, mc * 128:(mc + 1) * 128, half * 512:(half + 1) * 512], kvst[st][:],
                                                        reads=[rkvst[st]]))
            self.rmsnorm(gcol, les, keep=True)
            stage = self.cfg.get("xa_stage", 9)
            if stage < 2:
                tk.barrier(); return
            for half in range(2):
                s, v = self.load_w512(dr["xa_wq"][l], half * 512)
                for q in range(4):
                    ec = half * 4 + q
                    for ti, (t0, n) in enumerate(TILES):
                        pb = cnt % 4; cnt += 1
                        for kc in range(KC):
                            tk.op(tk.pe, lambda: nc.tensor.matmul(self.ps[pb][:, 0:n], v[:, kc, q * 128:(q + 1) * 128], self.hT[:, kc, t0:t0 + n],
                                                                  start=(kc == 0), stop=(kc == KC - 1)),
                                  reads=[self.rslot[s], self.rh(ti)], writes=[self.rps[pb]], inc=(kc == KC - 1))
                        if cnt % 2:
                            tk.op(tk.act, lambda: nc.scalar.copy(qT[:, ec, t0:t0 + n], self.ps[pb][:, 0:n]), reads=[self.rps[pb]], writes=[tk.R("qT", ti)])
                        else:
                            tk.op(tk.dve, lambda: nc.vector.tensor_copy(qT[:, ec, t0:t0 + n], self.ps[pb][:, 0:n]), reads=[self.rps[pb]], writes=[tk.R("qT", ti)])
            if stage < 3:
                tk.barrier(); return
            PT = [[self.sb(f"PT{l}_{i}_{m}", [128, 512], BF16, les) for m in range(2)] for i in range(2)]
            rPT = [[tk.R("PT", i, m) for m in range(2)] for i in range(2)]
            rsum = [self.sb(f"rsum{l}_{i}", [128, 512], F32, les) for i in range(2)]
            rrsum = [tk.R("rsum", i) for i in range(2)]
            it = 0
            for ti in range(4):
                t0, n = TILES[ti]
                for hh in range(4):
                    par = it % 2; it += 1
                    for mc in range(2):
                        pb = par * 2 + mc
                        for e in range(2):
                            tk.op(tk.pe, lambda: nc.tensor.matmul(self.ps[pb][:], KT[:, 2 * hh + e, mc * 128:(mc + 1) * 128], qT[:, 2 * hh + e, t0:t0 + n],
                                                                  start=(e == 0), stop=(e == 1)),
                                  reads=[rKT, tk.R("qT", ti)], writes=[self.rps[pb]], inc=(e == 1))
                        tk.op(tk.act, lambda: nc.scalar.activation(out=PT[par][mc][:], in_=self.ps[pb][:], func=AF.Exp, scale=SC),
                              reads=[self.rps[pb]], writes=[rPT[par][mc]])
                    pbs = 4 + par
                    for mc in range(2):
                        tk.op(tk.pe, lambda: nc.tensor.matmul(self.ps[pbs][:], self.ones_bf[:], PT[par][mc][:], start=(mc == 0), stop=(mc == 1)),
                              reads=[rPT[par][mc], self.r_const], writes=[self.rps[pbs]], inc=(mc == 1))
                    tk.op(tk.act, lambda: nc.scalar.activation(out=rsum[par][:], in_=self.ps[pbs][:], func=AF.Ln), reads=[self.rps[pbs]], writes=[rrsum[par]])
                    tk.op(tk.act, lambda: nc.scalar.activation(out=rsum[par][:], in_=rsum[par][:], func=AF.Exp, scale=-1.0), reads=[], writes=[rrsum[par]])
                    for e in range(2):
                        pbv = 6 + e
                        for mc in range(2):
                            tk.op(tk.pe, lambda: nc.tensor.matmul(self.ps[pbv][:], Vtm[:, mc, (2 * hh + e) * 128:(2 * hh + e + 1) * 128], PT[par][mc][:],
                                                                  start=(mc == 0), stop=(mc == 1)),
                                  reads=[rPT[par][mc], rV], writes=[self.rps[pbv]], inc=(mc == 1))
                        tk.op(tk.dve, lambda: nc.vector.tensor_tensor(out=self.hT[:, 2 * hh + e, t0:t0 + n], in0=self.ps[pbv][:], in1=rsum[par][:], op=ALU.mult),
                              reads=[self.rps[pbv], rrsum[par]], writes=[self.rh(ti)])
            if stage < 4:
                tk.barrier(); return
            do_s = self.cfg.get("xa_sample", True)
            KbT = [self.sb(f"KbT{l}_{i}", [128, KC, 256], BF16, les) for i in range(2)]
            rKbT = [tk.R("KbT", i) for i in range(2)]
            PTs = self.sb(f"PTs{l}", [128, 2, 4, NS], BF16, les)
            rPTs = tk.R("PTs")
            rs_s = self.sb(f"rs_s{l}", [128, 4, NS], F32, les)
            loads = {}

            def load_kv(b):
                s = self.next_slot()
                vk = self.ring[:, s, 0:2048].rearrange("p (m e) -> p m e", m=2)
                vv = self.ring[:, s, 2048:4096].rearrange("p (m e) -> p m e", m=2)
                tk.dma(tk.pool, vk, dr["ck"][l, b].rearrange("(m p) e -> p m e", p=128), writes=[self.rslot[s]], dsem=self.slot_sem[s])
                tk.dma(tk.pool, vv, dr["cv"][l, b].rearrange("(m p) e -> p m e", p=128), writes=[self.rslot[s]], dsem=self.slot_sem[s])
                loads[b] = (s, vk, vv)
            PRE = self.NSLOT - 1
            for b in range(PRE if do_s else 0):
                load_kv(b)
            scv = [self.ps[4 + i][:].rearrange("p (m h t) -> p m h t", m=2, h=4) for i in range(2)]
            pvv = self.ps[6][:].rearrange("p (e t) -> p e t", e=8)
            def xa_T(b):
                s, vk, vv = loads[b]
                par = b % 2
                for mc in range(2):
                    pb = par * 2 + mc
                    pbf = self.ps[pb][:].bitcast(BF16).rearrange("p (e m) -> p e m", e=8)
                    for ec in range(KC):
                        tk.op(tk.pe, lambda: nc.tensor.transpose(pbf[:, ec, :], vk[:, mc, ec * 128:(ec + 1) * 128], self.ident_bf[:]),
                              reads=[self.rslot[s], self.r_const], writes=[self.rps[pb]], inc=(ec == KC - 1))
                    if mc == 0:
                        tk.op(tk.act, lambda: nc.scalar.copy(KbT[par][:, :, mc * 128:(mc + 1) * 128], pbf), reads=[self.rps[pb]], writes=[rKbT[par]])
                    else:
                        tk.op(tk.dve, lambda: nc.vector.tensor_copy(KbT[par][:, :, mc * 128:(mc + 1) * 128], pbf), reads=[self.rps[pb]], writes=[rKbT[par]])

            if do_s:
                xa_T(0)
            for b in range(NB if do_s else 0):
                s, vk, vv = loads[b]
                par = b % 2
                if b + 1 < NB:
                    if b + 1 not in loads:
                        load_kv(b + 1)
                    xa_T(b + 1)
                c0 = SEQ + 4 * b
                for hh in range(4):
                    for mc in range(2):
                        for e in range(2):
                            last = (hh == 3 and mc == 1 and e == 1)
                            tk.op(tk.pe, lambda: nc.tensor.matmul(scv[par][:, mc, hh, 4 * b:4 * b + 4], KbT[par][:, 2 * hh + e, mc * 128:(mc + 1) * 128],
                                                                  qT[:, 2 * hh + e, c0:c0 + 4], start=(e == 0), stop=(e == 1)),
                                  reads=[rKbT[par], tk.R("qT", 4)], writes=[self.rps[4 + par]], inc=last)
                tk.op(tk.act, lambda: nc.scalar.activation(out=PTs[:, :, :, 4 * b:4 * b + 4], in_=scv[par][:, :, :, 4 * b:4 * b + 4], func=AF.Exp, scale=SC),
                      reads=[self.rps[4 + par]], writes=[rPTs])
                for ec in range(KC):
                    for mc in range(2):
                        last = (ec == KC - 1 and mc == 1)
                        tk.op(tk.pe, lambda: nc.tensor.matmul(pvv[:, ec, 4 * b:4 * b + 4], vv[:, mc, ec * 128:(ec + 1) * 128], PTs[:, mc, ec // 2, 4 * b:4 * b + 4],
                                                              start=(mc == 0), stop=(mc == 1)),
                              reads=[self.rslot[s], rPTs], writes=[self.rps[6]], inc=last)
                if b + PRE < NB and (b + PRE) not in loads:
                    load_kv(b + PRE)
            for mc in range(2 if do_s else 0):
                tk.op(tk.pe, lambda: nc.tensor.matmul(self.ps[7][:, 0:256], self.ones_bf[:], PTs[:, mc, :, :], start=(mc == 0), stop=(mc == 1)),
                      reads=[rPTs, self.r_const], writes=[self.rps[7]], inc=(mc == 1))
            if do_s:
                tk.op(tk.dve, lambda: nc.vector.reciprocal(rs_s[:], self.ps[7][:, 0:256].rearrange("p (h t) -> p h t", h=4)),
                      reads=[self.rps[7]], writes=[tk.R("rs_s")])
            for ec in range(KC if do_s else 0):
                tk.op(tk.dve, lambda: nc.vector.tensor_tensor(out=self.hT[:, ec, SEQ:T], in0=pvv[:, ec, :], in1=rs_s[:, ec // 2, :], op=ALU.mult),
                      reads=[self.rps[6], tk.R("rs_s")], writes=[self.rh(4)])
            for half in range(2):
                s, v = self.load_w512(dr["xa_wo"][l], half * 512)
                for q in range(4):
                    dc = half * 4 + q
                    for ti, (t0, n) in enumerate(TILES):
                        pb = cnt % 4; cnt += 1
                        for kc in range(KC):
                            tk.op(tk.pe, lambda: nc.tensor.matmul(self.ps[pb][:, 0:n], v[:, kc, q * 128:(q + 1) * 128], self.hT[:, kc, t0:t0 + n],
                                                                  start=(kc == 0), stop=(kc == KC - 1)),
                                  reads=[self.rslot[s], self.rh(ti)], writes=[self.rps[pb]], inc=(kc == KC - 1))
                        tk.op(tk.dve, lambda: nc.vector.tensor_tensor(out=self.xT[:, dc, t0:t0 + n], in0=self.ps[pb][:, 0:n], in1=self.xT[:, dc, t0:t0 + n], op=ALU.add),
                              reads=[self.rps[pb], self.rx(ti)], writes=[self.rx(ti)])
            tk.barrier()

    def mix1(self):
        nc, tk = self.nc, self.tk
        dr = self.dr
        l = 1
        gcol = PFM_IDX["norms"] + (l * 4 + 1) * 8
        with ExitStack() as les:
            self.alloc_h(les, "m1")
            self.rmsnorm(gcol, les, keep=True)
            uT = self.sb("uT", [128, KC, T], BF16, les)
            lnw = self.sb("lnw_bc", [128, D], F32, les)
            lnb = self.sb("lnb_bc", [128, D], F32, les)
            wsT = self.sb("wsT", [128, 8, 128], BF16, les)
            wsTb = self.sb("wsTb", [64, 8, 64], BF16, les)
            wsTb_f = self.sb("wsTb_f", [64, 8, 64], F32, les)
            wstg = [self.sb(f"wstg{i}", [128, 128], F32, les) for i in range(2)]
            triu = self.sb("triu_sb", [128, 128], F32, les)
            mblk = self.sb("mblk_sb", [64, 64], F32, les)
            bsr = self.sb("bsr", [1, 8, 128], BF16, les)
            bsS_f = self.sb("bsS_f", [1, 8, 64], F32, les)
            bsS = self.sb("bsS", [1, 8, 64], BF16, les)
            onesr = self.sb("onesr", [1, 128], BF16, les)
            rg = tk.R("gm_const")
            tk.dma(tk.sp, lnw[:], dr["gm_ln_w"][0].partition_broadcast(128), writes=[rg])
            tk.dma(tk.sp, lnb[:], dr["gm_ln_b"][0].partition_broadcast(128), writes=[rg])
            tk.dma(tk.sp, triu[:], dr["triu"], writes=[rg])
            tk.dma(tk.sp, mblk[:], dr["mask_blk"], writes=[rg])
            tk.dma(tk.pool, bsr[:], dr["gm_bs"][0:1], writes=[rg])
            tk.op(tk.dve, lambda: nc.vector.memset(onesr[:], 1.0), writes=[rg])
            tk.op(tk.dve, lambda: nc.vector.memset(wsTb_f[:], 0.0), writes=[tk.R("wsTb_f")])
            for b in range(NB):
                tk.dma(tk.sp, wsTb_f[4 * b:4 * b + 4, :, 4 * b:4 * b + 4], dr["gm_ws"][0][:, 0:4, 0:4].rearrange("g t s -> t g s"),
                       writes=[tk.R("wsTb_f")])
                tk.dma(tk.sp, bsS_f[0:1, :, 4 * b:4 * b + 4], dr["gm_bs"][0:1, :, 0:4], writes=[tk.R("bsS_f")])
            for g in range(8):
                pb = g % 4
                tk.op(tk.pe, lambda: nc.tensor.transpose(self.ps[pb][0:64, 0:64], wsTb_f[:, g, :], self.ident[0:64, 0:64]),
                      reads=[tk.R("wsTb_f"), self.r_const], writes=[self.rps[pb]])
                tk.op(tk.dve, lambda: nc.vector.tensor_tensor(out=wsTb[:, g, :], in0=self.ps[pb][0:64, 0:64], in1=mblk[:], op=ALU.mult),
                      reads=[self.rps[pb], rg], writes=[tk.R("wsTb")])
            tk.op(tk.dve, lambda: nc.vector.tensor_copy(bsS[:], bsS_f[:]), reads=[tk.R("bsS_f")], writes=[tk.R("bsS")])
            for g in range(8):
                st = g % 2
                tk.dma(tk.sp, wstg[st][:], dr["gm_ws"][0, g], writes=[tk.R("wstg", st)])
                pb = g % 4
                tk.op(tk.pe, lambda: nc.tensor.transpose(self.ps[pb][:, 0:128], wstg[st][:], self.ident[:]),
                      reads=[tk.R("wstg", st), self.r_const], writes=[self.rps[pb]])
                tk.op(tk.dve, lambda: nc.vector.tensor_tensor(out=wsT[:, g, :], in0=self.ps[pb][:, 0:128], in1=triu[:], op=ALU.mult),
                      reads=[self.rps[pb], rg], writes=[tk.R("wsT")])
            cnt = 0
            for half in range(2):
                s, v = self.load_w512(dr["w_in_odd"][0], half * 512)
                for q in range(4):
                    ec = half * 4 + q
                    for ti, (t0, n) in enumerate(TILES):
                        pb = cnt % 4; cnt += 1
                        for kc in range(KC):
                            tk.op(tk.pe, lambda: nc.tensor.matmul(self.ps[pb][:, 0:n], v[:, kc, q * 128:(q + 1) * 128], self.hT[:, kc, t0:t0 + n],
                                                                  start=(kc == 0), stop=(kc == KC - 1)),
                                  reads=[self.rslot[s], self.rh(ti)], writes=[self.rps[pb]], inc=(kc == KC - 1))
                        tk.op(tk.act, lambda: nc.scalar.activation(out=uT[:, ec, t0:t0 + n], in_=self.ps[pb][:, 0:n], func=AF.Gelu),
                              reads=[self.rps[pb]], writes=[tk.R("uT", ti)])
            sv = [self.load_w512(dr["w_in_odd"][0], D + half * 512) for half in range(2)]
            vg = [self.sb(f"vg{i}", [128, D], F32, les) for i in range(2)]
            vtm = [self.sb(f"vtm{i}", [128, D], BF16, les) for i in range(2)]
            stats = self.sb("gm_stats", [128, 2, 6], F32, les)
            mv = self.sb("gm_mv", [128, 2], F32, les)
            rstd = self.sb("gm_rstd", [128, 1], F32, les)
            blocks = [(b * 128, 128) for b in range(SEQ // 128)] + [(SEQ, NS)]
            def vproj(bi):
                t0, n = blocks[bi]
                ti = min(t0 // 512, 4)
                par = bi % 2
                rvg = tk.R("vg", par)
                for half in range(2):
                    s, v = sv[half]
                    pb = par * 2 + half
                    for kc in range(KC):
                        tk.op(tk.pe, lambda: nc.tensor.matmul(self.ps[pb][0:n, :], self.hT[:, kc, t0:t0 + n], v[:, kc, :],
                                                              start=(kc == 0), stop=(kc == KC - 1)),
                              reads=[self.rslot[s], self.rh(ti)], writes=[self.rps[pb]], inc=(kc == KC - 1))
                    tk.op(tk.act, lambda: nc.scalar.activation(out=vg[par][0:n, half * 512:(half + 1) * 512], in_=self.ps[pb][0:n, :], func=AF.Gelu),
                          reads=[self.rps[pb]], writes=[rvg])
            vproj(0)
            for bi, (t0, n) in enumerate(blocks):
                ti = min(t0 // 512, 4)
                par = bi % 2
                rvg = tk.R("vg", par); rvt = tk.R("vtm", par)
                if bi + 1 < len(blocks):
                    vproj(bi + 1)
                for half in range(2):
                    tk.op(tk.dve, lambda: nc.vector.bn_stats(stats[0:n, half, :], vg[par][0:n, half * 512:(half + 1) * 512]),
                          reads=[rvg], writes=[tk.R("gm_stats")])
                tk.op(tk.dve, lambda: nc.vector.bn_aggr(mv[0:n, :], stats[0:n, :, :]), reads=[tk.R("gm_stats")], writes=[tk.R("gm_mv")])
                tk.op(tk.act, lambda: nc.scalar.activation(out=rstd[0:n, :], in_=mv[0:n, 1:2], func=AF.Sqrt, bias=self.eps_ln[0:n, 0:1]),
                      reads=[tk.R("gm_mv"), self.r_const], writes=[tk.R("gm_rstd")])
                tk.op(tk.dve, lambda: nc.vector.reciprocal(rstd[0:n, :], rstd[0:n, :]), reads=[tk.R("gm_rstd")], writes=[tk.R("gm_rstd")])
                tk.op(tk.dve, lambda: nc.vector.tensor_scalar(vg[par][0:n, :], vg[par][0:n, :], mv[0:n, 0:1], rstd[0:n, 0:1],
                                                              op0=ALU.subtract, op1=ALU.mult),
                      reads=[tk.R("gm_mv"), tk.R("gm_rstd")], writes=[rvg])
                tk.op(tk.pool, lambda: nc.gpsimd.tensor_tensor(out=vg[par][0:n, :], in0=vg[par][0:n, :], in1=lnw[0:n, :], op=ALU.mult),
                      reads=[rg], writes=[rvg])
                if bi < len(blocks) - 1:
                    tk.op(tk.pool, lambda: nc.gpsimd.tensor_tensor(out=vtm[par][0:n, :], in0=vg[par][0:n, :], in1=lnb[0:n, :], op=ALU.add),
                          reads=[rvg, rg], writes=[rvt])
                else:
                    tk.op(tk.pool, lambda: nc.gpsimd.tensor_tensor(out=vg[par][0:n, :], in0=vg[par][0:n, :], in1=lnb[0:n, :], op=ALU.add),
                          reads=[rg], writes=[rvg])
                    tk.op(tk.dve, lambda: nc.vector.tensor_copy(vtm[par][0:n, :], vg[par][0:n, :]), reads=[rvg], writes=[rvt])
                    self.out_toks.append(tk.dma(tk.sp, dr["gv_s"], vg[par][0:n, :], reads=[rvg]))
                for gh in range(2):
                    pb = 4 + gh
                    pv = self.ps[pb][:].rearrange("p (g t) -> p g t", g=4)
                    for gl in range(4):
                        g = gh * 4 + gl
                        if n == 128:
                            w_ap, b_ap = wsT[:, g, :], bsr[0:1, g, :]
                        else:
                            w_ap, b_ap = wsTb[:, g, :], bsS[0:1, g, :]
                        tk.op(tk.pe, lambda: nc.tensor.matmul(pv[:, gl, 0:n], vtm[par][0:n, g * 128:(g + 1) * 128], w_ap, start=True, stop=False),
                              reads=[rvt, tk.R("wsT"), tk.R("wsTb")], writes=[self.rps[pb]], inc=False)
                        tk.op(tk.pe, lambda: nc.tensor.matmul(pv[:, gl, 0:n], onesr[0:1, :], b_ap, start=False, stop=True),
                              reads=[rg, tk.R("bsS")], writes=[self.rps[pb]], inc=(gl == 3))
                    tk.op(tk.dve, lambda: nc.vector.tensor_tensor(out=uT[:, gh * 4:gh * 4 + 4, t0:t0 + n], in0=pv[:, :, 0:n],
                                                                  in1=uT[:, gh * 4:gh * 4 + 4, t0:t0 + n], op=ALU.mult),
                          reads=[self.rps[pb]], writes=[tk.R("uT", ti)])
            for half in range(2):
                s, v = self.load_w512(dr["w_out_odd"][0], half * 512)
                for q in range(4):
                    dc = half * 4 + q
                    for ti, (t0, n) in enumerate(TILES):
                        pb = cnt % 4; cnt += 1
                        for kc in range(KC):
                            tk.op(tk.pe, lambda: nc.tensor.matmul(self.ps[pb][:, 0:n], v[:, kc, q * 128:(q + 1) * 128], uT[:, kc, t0:t0 + n],
                                                                  start=(kc == 0), stop=(kc == KC - 1)),
                                  reads=[self.rslot[s], tk.R("uT", ti)], writes=[self.rps[pb]], inc=(kc == KC - 1))
                        tk.op(tk.dve, lambda: nc.vector.tensor_tensor(out=self.xT[:, dc, t0:t0 + n], in0=self.ps[pb][:, 0:n], in1=self.xT[:, dc, t0:t0 + n], op=ALU.add),
                              reads=[self.rps[pb], self.rx(ti)], writes=[self.rx(ti)])
            tk.barrier()

    def mix0(self):
        nc, tk = self.nc, self.tk
        dr = self.dr
        P = PFM_IDX
        gcol = P["norms"] + 1 * 8
        do_rwkv = self.cfg.get("rwkv", True)
        SEGS = [(i * 256, 256, 1, 256) for i in range(SEQ // 256)] + [(SEQ, NS, NB, 4)]
        with ExitStack() as les:
            sb = lambda name, shape, dt: self.sb("m0_" + name, shape, dt, les)
            R = tk.R
            rc = R("m0c")
            cst = {}
            for nm, shp in (("tri", [128, 128]), ("sut", [128, 128]), ("mask2", [128, 256]), ("maskxt", [128, 128]),
                            ("trib", [128, 128]), ("sutb", [128, 128]), ("mask2b", [128, 256]), ("maskxtb", [128, 128]),
                            ("rowmask", [128, NB]), ("bones64", [128, 128])):
                cst[nm] = sb("c_" + nm, shp, F32)
                tk.dma(tk.sp, cst[nm][:], dr["c_" + nm], writes=[rc])
            bones = sb("c_bones", [128, 128], BF16)
            tk.dma(tk.pool, bones[:], dr["c_bones"], writes=[rc])
            lora12 = sb("lora12", [128, 512], BF16)
            g2s = sb("g2s", [128, 512], BF16)
            tk.dma(tk.pool, lora12[0:64, :], dr["decay_w2"][0], writes=[rc])
            tk.dma(tk.pool, lora12[64:128, :], dr["iclr_a2"][0], writes=[rc])
            tk.dma(tk.pool, g2s[:], dr["gate_g2"][0], writes=[rc])
            negw0 = sb("negw0", [128, 4], F32)
            tk.op(tk.dve, lambda: nc.vector.tensor_scalar(negw0[:], self.pfm[:, P["decay_w0"]:P["decay_w0"] + 4], -1.0, None, op0=ALU.mult),
                  reads=[self.r_const], writes=[rc])
            cone = sb("cone", [128, 1], F32); chalf = sb("chalf", [128, 1], F32); cgn = sb("cgn", [128, 1], F32)
            tk.op(tk.dve, lambda: nc.vector.memset(cone[:], 1.0), writes=[rc])
            tk.op(tk.dve, lambda: nc.vector.memset(chalf[:], -0.5), writes=[rc])
            tk.op(tk.dve, lambda: nc.vector.memset(cgn[:], 64e-5), writes=[rc])
            stT = sb("stT", [128, 14, NB], F32)
            cbT = sb("cbT", [128, 4, NB, 2], F32)
            with ExitStack() as ies:
                stg = self.sb("m0_ststg", [NB, PROJ_A], F32, ies)
                cstg = self.sb("m0_cstg", [2 * NB, 512], F32, ies)
                tk.dma(tk.sp, stg[:], dr["st_shift"], writes=[R("ststg")])
                tk.dma(tk.sp, cstg[:], dr["st_conv"].rearrange("b r f -> (b r) f"), writes=[R("cstg")])
                for c in range(14):
                    pb = c % 4
                    tk.op(tk.pe, lambda: nc.tensor.transpose(self.ps[pb][:, 0:NB], stg[:, c * 128:(c + 1) * 128], self.ident[0:NB, 0:NB]),
                          reads=[R("ststg"), self.r_const], writes=[self.rps[pb]])
                    tk.op(tk.dve, lambda: nc.vector.tensor_copy(stT[:, c, :], self.ps[pb][:, 0:NB]), reads=[self.rps[pb]], writes=[rc])
                for j in range(4):
                    pb = j % 4
                    tk.op(tk.pe, lambda: nc.tensor.transpose(self.ps[pb][:, 0:2 * NB], cstg[:, j * 128:(j + 1) * 128], self.ident[0:2 * NB, 0:2 * NB]),
                          reads=[R("cstg"), self.r_const], writes=[self.rps[pb]])
                    tk.op(tk.dve, lambda: nc.vector.tensor_copy(cbT[:, j, :, :], self.ps[pb][:, 0:2 * NB].rearrange("p (b r) -> p b r", r=2)),
                          reads=[self.rps[pb]], writes=[rc])
                tk.barrier()
            SM = 256
            hseg = sb("hseg", [128, KC, SM], BF16)
            psA = sb("psA", [128, 14, SM], F32)
            pext = [sb(f"pext{i}", [128, SM + 1], F32) for i in range(2)]
            pcar = sb("pcar", [128, 14], F32)
            dtmp = [sb("dtmp0", [128, SM], F32)] * 2
            zext = sb("zext", [128, 4, SM + 2], F32)
            ycv = [sb("ycv0", [128, SM], F32)] * 2
            mT = sb("mT", [128, KC, SM], BF16)
            aT = sb("aT", [128, 4, SM], F32); bT = sb("bT", [128, 4, SM], F32); lwT = sb("lwT", [128, 4, SM], F32)
            aiT = sb("aiT", [128, 1, SM], F32); gT = sb("gT", [128, 4, SM], BF16); bonT = sb("bonT", [128, 4, SM], BF16)
            t12 = sb("t12", [128, SM], BF16); sgd = sb("sgd", [128, SM], BF16)
            tmpA = [sb(f"tmpA{i}", [128, SM], F32) for i in range(2)]
            tmpB = [sb(f"tmpB{i}", [128, SM], BF16) for i in range(2)]
            rmsb = self.rmsnorm_bufs("m0", les, n=SM, nrs=1)
            hcv = sb("hcv", [128, 4, SM], F32); bgv = sb("bgv", [128, 4, SM], F32)
            pshs = sb("pshs", [128, 14, NB], F32); zs = sb("zs", [128, 4, NB, 2], F32)
            tk.op(tk.dve, lambda: nc.vector.memset(pcar[:], 0.0), writes=[R("pcar")])
            tk.op(tk.dve, lambda: nc.vector.memset(zext[:, :, 0:2], 0.0), writes=[R("zext")])
            lw_tm = sb("lw_tm", [128, 512], F32); v_tm = sb("v_tm", [128, 512], BF16)
            d1 = sb("d1", [128, 512], F32)
            Bh = sb("Bh", [128, 512], BF16); Kh = sb("Kh", [128, 512], BF16)
            Zb = [sb("Zb0", [128, 8, 128], BF16)]
            Ef = sb("Ef", [128, 4, 128], F32); Einv = sb("Einv", [128, 4, 128], F32); Epv = sb("Epv", [128, 4, 128], F32)
            AR = sb("AR", [128, 4, 256], BF16); BK = sb("BK", [128, 4, 256], BF16)
            LQ = sb("LQ", [128, 8, 4, 128], BF16)
            Xs = sb("Xs", [128, 8, 128], BF16); XT = sb("XT", [128, 8, 128], BF16)
            Gs = sb("Gs", [128, 4, 64], F32); Rb = sb("Rb", [128, 4, 128], F32)
            Hs = [sb(f"Hs{i}", [128, 4, 64], F32) for i in range(2)]
            ysb = sb("ysb", [128, 4, 128], F32); ycen = sb("ycen", [128, 4, 128], F32); ysq = sb("ysq", [128, 4, 128], BF16)
            bones64b = sb("bones64b", [128, 128], BF16)
            tk.dma(tk.pool, bones64b[:], dr["c_bones64"], writes=[rc])
            tk.op(tk.dve, lambda: nc.vector.memset(Hs[0][:], 0.0), writes=[R("H", 0)])
            hcur = [0]
            wst = d1[0:64, :].rearrange("p (h k) -> p h k", h=8)

            def bank(i):
                return self.ps[i], self.rps[i]

            def chunk(c0, sample):
                cs = slice(c0, c0 + 128)
                rT = psA[:, 0:4, cs]; kT = psA[:, 4:8, cs]; vT = psA[:, 8:12, cs]
                tri, sut = (cst["trib"], cst["sutb"]) if sample else (cst["tri"], cst["sut"])
                mask2, maskxt = (cst["mask2b"], cst["maskxtb"]) if sample else (cst["mask2"], cst["maskxt"])
                rseg = [R("psA"), R("prep")]
                tmb = {}
                for qi, src in enumerate((lwT, aT, bT, None, "v")):
                    p_, rp = bank(qi)
                    tmb[qi] = (p_, rp)
                    for j in range(4):
                        if src is None:
                            in_ap = kT[:, j, :]
                        elif isinstance(src, str):
                            in_ap = vT[:, j, :]
                        else:
                            in_ap = src[:, j, cs]
                        tk.op(tk.pe, lambda: nc.tensor.transpose(p_[:, j * 128:(j + 1) * 128], in_ap, self.ident[:]),
                              reads=rseg + [self.r_const], writes=[rp], inc=(j == 3))
                    if qi == 0:
                        tk.op(tk.dve, lambda: nc.vector.tensor_copy(lw_tm[:], p_[:]), reads=[rp], writes=[R("tm", 0)])
                    if qi == 4:
                        tk.op(tk.act, lambda: nc.scalar.copy(v_tm[:], p_[:]), reads=[rp], writes=[R("v_tm")])
                pc, rpc = bank(5); prc, rprc = bank(6); pct, rpct = bank(7)
                tk.op(tk.pe, lambda: nc.tensor.matmul(pc[:], tri[:], lw_tm[:], start=True, stop=True), reads=[R("tm", 0), rc], writes=[rpc])
                tk.op(tk.pe, lambda: nc.tensor.matmul(prc[:], sut[:], lw_tm[:], start=True, stop=True), reads=[R("tm", 0), rc], writes=[rprc])
                pctv = pct[:].rearrange("p (j t) -> p j t", j=4)
                for j in range(4):
                    tk.op(tk.pe, lambda: nc.tensor.matmul(pctv[:, j, :], lw_tm[:, j * 128:(j + 1) * 128], tri[:], start=True, stop=True),
                          reads=[R("tm", 0), rc], writes=[rpct], inc=(j == 3))
                Z0 = Zb[0]
                tk.op(tk.dve, lambda: nc.vector.tensor_tensor(out=d1[:], in0=pc[:], in1=lw_tm[:], op=ALU.subtract), reads=[rpc, R("tm", 0)], writes=[R("d1")])
                tk.op(tk.act, lambda: nc.scalar.activation(out=d1[:], in_=d1[:], func=AF.Exp), reads=[], writes=[R("d1")])
                tk.op(tk.dve, lambda: nc.vector.tensor_tensor(out=Z0[:, :, 0:64], in0=tmb[1][0][:].rearrange("p (h k) -> p h k", h=8),
                                                              in1=d1[:].rearrange("p (h k) -> p h k", h=8), op=ALU.mult),
                      reads=[tmb[1][1], R("d1")], writes=[R("Z", 0), R("Z", 1)])
                tk.op(tk.act, lambda: nc.scalar.activation(out=d1[:], in_=prc[:], func=AF.Exp), reads=[rprc], writes=[R("d1")])
                tk.op(tk.dve, lambda: nc.vector.tensor_tensor(out=Bh[:], in0=tmb[2][0][:], in1=d1[:], op=ALU.mult), reads=[tmb[2][1], R("d1")], writes=[R("Bh")])
                tk.op(tk.dve, lambda: nc.vector.tensor_tensor(out=Kh[:], in0=tmb[3][0][:], in1=d1[:], op=ALU.mult), reads=[tmb[3][1], R("d1")], writes=[R("Kh")])
                tk.op(tk.dve, lambda: nc.vector.tensor_tensor(out=Epv[:], in0=pctv, in1=lwT[:, :, cs], op=ALU.subtract), reads=[rpct] + rseg, writes=[R("Epv")])
                tk.op(tk.act, lambda: nc.scalar.activation(out=Ef[:], in_=pctv, func=AF.Exp), reads=[rpct], writes=[R("Ef")])
                tk.op(tk.act, lambda: nc.scalar.activation(out=Einv[:], in_=pctv, func=AF.Exp, scale=-1.0), reads=[rpct], writes=[R("Einv")])
                tk.op(tk.act, lambda: nc.scalar.activation(out=Epv[:], in_=Epv[:], func=AF.Exp), reads=[], writes=[R("Epv")])
                tk.op(tk.dve, lambda: nc.vector.tensor_tensor(out=AR[:, :, 0:128], in0=aT[:, :, cs], in1=Epv[:], op=ALU.mult), reads=rseg + [R("Epv")], writes=[R("AR")])
                tk.op(tk.dve, lambda: nc.vector.tensor_tensor(out=AR[:, :, 128:256], in0=rT, in1=Ef[:], op=ALU.mult), reads=rseg + [R("Ef")], writes=[R("AR")])
                tk.op(tk.dve, lambda: nc.vector.tensor_tensor(out=BK[:, :, 0:128], in0=bT[:, :, cs], in1=Einv[:], op=ALU.mult), reads=rseg + [R("Einv")], writes=[R("BK")])
                tk.op(tk.dve, lambda: nc.vector.tensor_tensor(out=BK[:, :, 128:256], in0=kT, in1=Einv[:], op=ALU.mult), reads=rseg + [R("Einv")], writes=[R("BK")])
                for par_ in range(2):
                    pbs = 64 * par_
                    for i_ in range(4):
                        h = 2 * i_ + par_
                        p_, rp = bank(4 * par_ + i_)
                        tk.op(tk.pe, lambda: nc.tensor.matmul(p_[:, 0:256], BK[pbs:pbs + 64, i_, 0:128], AR[pbs:pbs + 64, i_, :], start=True, stop=True),
                              reads=[R("AR"), R("BK")], writes=[rp], inc=False)
                        tk.op(tk.pe, lambda: nc.tensor.matmul(p_[:, 256:512], BK[pbs:pbs + 64, i_, 128:256], AR[pbs:pbs + 64, i_, :], start=True, stop=True),
                              reads=[R("AR"), R("BK")], writes=[rp], inc=True)
                        tk.op(tk.dve, lambda: nc.vector.tensor_tensor(out=LQ[:, h, :, :].rearrange("p (a q) t -> p a (q t)", a=2),
                                                                      in0=p_[:].rearrange("p (a x) -> p a x", a=2),
                                                                      in1=mask2[:, None, :].to_broadcast([128, 2, 256]), op=ALU.mult),
                              reads=[rp, rc], writes=[R("LQ")])
                XTv = XT[:].rearrange("p (i two) t -> p i two t", two=2)
                for par_ in range(2):
                    p_, rp = bank(par_)
                    pv = p_[:].rearrange("p (i t) -> p i t", i=4)
                    pbs = 64 * par_
                    for i_ in range(4):
                        tk.op(tk.pe, lambda: nc.tensor.matmul(pv[:, i_, :], AR[pbs:pbs + 64, i_, 0:128], BK[pbs:pbs + 64, i_, 0:128], start=True, stop=True),
                              reads=[R("AR"), R("BK")], writes=[rp], inc=(i_ == 3))
                    tk.op(tk.dve, lambda: nc.vector.tensor_tensor(out=XTv[:, :, par_, :], in0=pv, in1=maskxt[:, None, :].to_broadcast([128, 4, 128]), op=ALU.mult),
                          reads=[rp, rc], writes=[R("XT")])
                p_, rp = bank(2)
                pv = p_[:].rearrange("p (h t) -> p h t", h=8)
                for h in range(8):
                    tk.op(tk.pe, lambda: nc.tensor.matmul(pv[:, h, :], LQ[:, h, 2, :], v_tm[:, h * 64:(h + 1) * 64], start=True, stop=True),
                          reads=[R("LQ"), R("v_tm")], writes=[rp], inc=(h == 7))
                tk.op(tk.act, lambda: nc.scalar.copy(Z0[:, :, 64:128], pv), reads=[rp], writes=[R("Z", 0), R("Z", 1)])
                nlev = 2 if sample else 7
                for lev in range(nlev):
                    if lev == 0:
                        Xc = LQ[:, :, 0, :]; rX = R("LQ")
                    else:
                        Xc = Xs[:]; rX = R("X")
                    last = (lev == nlev - 1)
                    for half in range(2):
                        p_, rp = bank(4 + half)
                        pv = p_[:].rearrange("p (h t) -> p h t", h=4)
                        for hl in range(4):
                            h = half * 4 + hl
                            tk.op(tk.pe, lambda: nc.tensor.matmul(pv[:, hl, :], Xc[:, h, :], Zb[0][:, h, :], start=True, stop=True),
                                  reads=[rX, R("Z", half)], writes=[rp], inc=(hl == 3))
                    if not last:
                        for half in range(2):
                            p_, rp = bank(half)
                            pv = p_[:].rearrange("p (h t) -> p h t", h=4)
                            for hl in range(4):
                                h = half * 4 + hl
                                tk.op(tk.pe, lambda: nc.tensor.matmul(pv[:, hl, :], XT[:, h, :], Xc[:, h, :], start=True, stop=True),
                                      reads=[rX, R("XT")], writes=[rp], inc=(hl == 3))
                        for half in range(2):
                            p_, rp = bank(2 + half)
                            pv = p_[:].rearrange("p (h t) -> p h t", h=4)
                            for hl in range(4):
                                h = half * 4 + hl
                                tk.op(tk.pe, lambda: nc.tensor.matmul(pv[:, hl, :], Xc[:, h, :], XT[:, h, :], start=True, stop=True),
                                      reads=[rX, R("XT")], writes=[rp], inc=(hl == 3))
                    for half in range(2):
                        p_, rp = bank(4 + half)
                        pv = p_[:].rearrange("p (h t) -> p h t", h=4)
                        tk.op(tk.dve, lambda: nc.vector.tensor_tensor(out=Zb[0][:, half * 4:half * 4 + 4, :], in0=pv, in1=Zb[0][:, half * 4:half * 4 + 4, :], op=ALU.add),
                              reads=[rp], writes=[R("Z", half)])
                    if not last:
                        for half in range(2):
                            p_, rp = bank(half)
                            pv = p_[:].rearrange("p (h t) -> p h t", h=4)
                            tk.op(tk.act, lambda: nc.scalar.copy(Xs[:, half * 4:half * 4 + 4, :], pv), reads=[rp], writes=[R("X")])
                        for half in range(2):
                            p_, rp = bank(2 + half)
                            pv = p_[:].rearrange("p (h t) -> p h t", h=4)
                            if half == 0:
                                tk.op(tk.act, lambda: nc.scalar.copy(XT[:, half * 4:half * 4 + 4, :], pv), reads=[rp], writes=[R("XT")])
                            else:
                                tk.op(tk.dve, lambda: nc.vector.tensor_copy(XT[:, half * 4:half * 4 + 4, :], pv), reads=[rp], writes=[R("XT")])
                Z6 = Zb[0]; rZs = [R("Z", 0), R("Z", 1)]
                p_, rp = bank(0)
                pv = p_[:].rearrange("p (j t) -> p j t", j=4)
                for h in range(8):
                    j = h // 2; pbs = 64 * (h % 2)
                    tk.op(tk.pe, lambda: nc.tensor.matmul(pv[pbs:pbs + 64, j, :], Z6[:, h, 0:64], LQ[:, h, 1, :], start=True, stop=True),
                          reads=rZs + [R("LQ")], writes=[rp], inc=(h == 7))
                tk.op(tk.dve, lambda: nc.vector.tensor_tensor(out=Rb[:], in0=pv, in1=AR[:, :, 128:256], op=ALU.add), reads=[rp, R("AR")], writes=[R("Rb")])
                py, rpy = bank(1)
                pyv = py[:].rearrange("p (j t) -> p j t", j=4)
                for h in range(8):
                    j = h // 2; pbs = 64 * (h % 2)
                    o = pyv[pbs:pbs + 64, j, :]
                    tk.op(tk.pe, lambda: nc.tensor.matmul(o, Z6[:, h, 64:128], LQ[:, h, 1, :], start=True, stop=False), reads=rZs + [R("LQ")], writes=[rpy], inc=False)
                    tk.op(tk.pe, lambda: nc.tensor.matmul(o, v_tm[:, h * 64:(h + 1) * 64], LQ[:, h, 3, :], start=False, stop=True), reads=[R("v_tm"), R("LQ")], writes=[rpy], inc=(h == 7))
                pyB, rpyB = bank(2); pyC, rpyC = bank(3)
                pyBv = pyB[:].rearrange("p (j t) -> p j t", j=4)
                pyCv = pyC[:].rearrange("p (j t) -> p j t", j=4)
                if not sample:
                    p_, rp = bank(4)
                    pv = p_[:, 0:256].rearrange("p (j t) -> p j t", j=4)
                    for h in range(8):
                        j = h // 2; pbs = 64 * (h % 2)
                        tk.op(tk.pe, lambda: nc.tensor.matmul(pv[pbs:pbs + 64, j, :], Z6[:, h, 0:64], Bh[:, h * 64:(h + 1) * 64], start=True, stop=True),
                              reads=rZs + [R("Bh")], writes=[rp], inc=(h == 7))
                    tk.op(tk.act, lambda: nc.scalar.copy(Gs[:], pv), reads=[rp], writes=[R("Gs")])
                    hc = hcur[0]; hn = 1 - hc
                    Hc, Hn = Hs[hc], Hs[hn]
                    for j in range(4):
                        tk.op(tk.pe, lambda: nc.tensor.matmul(pyBv[0:64, j, :], Hc[0:64, j, :], Rb[0:64, j, :], start=True, stop=True),
                              reads=[R("H", hc), R("Rb")], writes=[rpyB], inc=(j == 3))
                    for j in range(4):
                        tk.op(tk.pe, lambda: nc.tensor.matmul(pyCv[64:128, j, :], Hc[64:128, j, :], Rb[64:128, j, :], start=True, stop=True),
                              reads=[R("H", hc), R("Rb")], writes=[rpyC], inc=(j == 3))
                    ph, rph = bank(5); phE, rphE = bank(6); phF, rphF = bank(7)
                    phv = ph[:, 0:256].rearrange("p (j t) -> p j t", j=4)
                    phEv = phE[:, 0:256].rearrange("p (j t) -> p j t", j=4)
                    phFv = phF[:, 0:256].rearrange("p (j t) -> p j t", j=4)
                    for h in range(8):
                        j = h // 2; pbs = 64 * (h % 2)
                        o = phv[pbs:pbs + 64, j, :]
                        tk.op(tk.pe, lambda: nc.tensor.matmul(o, Bh[:, h * 64:(h + 1) * 64], Z6[:, h, 64:128], start=True, stop=False),
                              reads=rZs + [R("Bh")], writes=[rph], inc=False)
                        tk.op(tk.pe, lambda: nc.tensor.matmul(o, Kh[:, h * 64:(h + 1) * 64], v_tm[:, h * 64:(h + 1) * 64], start=False, stop=True),
                              reads=[R("Kh"), R("v_tm")], writes=[rph], inc=(h == 7))
                    for j in range(4):
                        tk.op(tk.pe, lambda: nc.tensor.matmul(phEv[0:64, j, :], Gs[0:64, j, :], Hc[0:64, j, :], start=True, stop=True),
                              reads=[R("Gs"), R("H", hc)], writes=[rphE], inc=(j == 3))
                    for j in range(4):
                        tk.op(tk.pe, lambda: nc.tensor.matmul(phFv[64:128, j, :], Gs[64:128, j, :], Hc[64:128, j, :], start=True, stop=True),
                              reads=[R("Gs"), R("H", hc)], writes=[rphF], inc=(j == 3))
                    for j in range(4):
                        tk.op(tk.dve, lambda: nc.vector.scalar_tensor_tensor(out=Hn[:, j, :], in0=Hc[:, j, :], scalar=Ef[:, j, 127:128], in1=phv[:, j, :],
                                                                             op0=ALU.mult, op1=ALU.add),
                              reads=[R("H", hc), R("Ef"), rph], writes=[R("H", hn)])
                    tk.op(tk.dve, lambda: nc.vector.tensor_tensor(out=Hn[0:64, :, :], in0=Hn[0:64, :, :], in1=phEv[0:64, :, :], op=ALU.add), reads=[rphE], writes=[R("H", hn)])
                    tk.op(tk.dve, lambda: nc.vector.tensor_tensor(out=Hn[64:128, :, :], in0=Hn[64:128, :, :], in1=phFv[64:128, :, :], op=ALU.add), reads=[rphF], writes=[R("H", hn)])
                    hcur[0] = hn
                else:
                    sample_states(Z6, rZs, pyBv, rpyB, pyCv, rpyC)
                ncol = 64 if sample else 128
                tk.op(tk.act, lambda: nc.scalar.copy(ysb[:], pyv), reads=[rpy], writes=[R("ysb")])
                tk.op(tk.dve, lambda: nc.vector.tensor_tensor(out=ysb[0:64, :, 0:ncol], in0=ysb[0:64, :, 0:ncol], in1=pyBv[0:64, :, 0:ncol], op=ALU.add), reads=[rpyB], writes=[R("ysb")])
                tk.op(tk.dve, lambda: nc.vector.tensor_tensor(out=ysb[64:128, :, 0:ncol], in0=ysb[64:128, :, 0:ncol], in1=pyCv[64:128, :, 0:ncol], op=ALU.add), reads=[rpyC], writes=[R("ysb")])
                pm, rpm = bank(0)
                pmv = pm[:].rearrange("p (j t) -> p j t", j=4)
                tk.op(tk.pe, lambda: nc.tensor.matmul(pm[:], cst["bones64"][:], ysb[:].rearrange("p j t -> p (j t)"), start=True, stop=True), reads=[R("ysb"), rc], writes=[rpm])
                tk.op(tk.dve, lambda: nc.vector.tensor_tensor(out=ycen[:], in0=ysb[:], in1=pmv, op=ALU.subtract), reads=[R("ysb"), rpm], writes=[R("ycen")])
                tk.op(tk.act, lambda: nc.scalar.activation(out=ysq[:], in_=ycen[:], func=AF.Square), reads=[R("ycen")], writes=[R("ysq")])
                pq, rpq = bank(1)
                pqv = pq[:].rearrange("p (j t) -> p j t", j=4)
                tk.op(tk.pe, lambda: nc.tensor.matmul(pq[:], bones64b[:], ysq[:].rearrange("p j t -> p (j t)"), start=True, stop=True), reads=[R("ysq"), rc], writes=[rpq])
                rstd = ysb
                tk.op(tk.act, lambda: nc.scalar.activation(out=rstd[:], in_=pqv, func=AF.Ln, bias=cgn[:, 0:1]), reads=[rpq, rc], writes=[R("ysb")])
                tk.op(tk.act, lambda: nc.scalar.activation(out=rstd[:], in_=rstd[:], func=AF.Exp, scale=-0.5), reads=[], writes=[R("ysb")])
                tk.op(tk.dve, lambda: nc.vector.tensor_tensor(out=ycen[:], in0=ycen[:], in1=rstd[:], op=ALU.mult), reads=[R("ysb")], writes=[R("ycen")])
                for j in range(4):
                    tk.op(tk.dve, lambda: nc.vector.tensor_scalar(ycen[:, j, :], ycen[:, j, :], self.pfm[:, P["lnx_w"] + j:P["lnx_w"] + j + 1],
                                                                  self.pfm[:, P["lnx_b"] + j:P["lnx_b"] + j + 1], op0=ALU.mult, op1=ALU.add),
                          reads=[self.r_const], writes=[R("ycen")])
                tk.op(tk.dve, lambda: nc.vector.tensor_tensor(out=ycen[:], in0=ycen[:], in1=bonT[:, :, cs], op=ALU.add), reads=rseg, writes=[R("ycen")])
                tk.op(tk.dve, lambda: nc.vector.tensor_tensor(out=mT[:, 0:4, cs], in0=ycen[:], in1=gT[:, :, cs], op=ALU.mult), reads=rseg + [R("ycen")], writes=[R("mT")])

            def sample_states(Z6, rZs, pyBv, rpyB, pyCv, rpyC):
                tk.barrier()
                S0 = [hcv[0:64, 2 * i:2 * i + 2, :].rearrange("p c (a k) -> p (c a) k", a=4) for i in range(2)]
                H0 = [bgv[:, i, :].rearrange("p (j v) -> p j v", j=4) for i in range(2)]
                Bm = zext[:, 0, 0:256].bitcast(BF16)
                Km = zext[:, 1, 0:256].bitcast(BF16)
                Gb = ycv[0][:, 0:256].rearrange("p (j v) -> p j v", j=4)
                grow = bgv[0:64, 2:4, :].rearrange("p c k -> p (c k)")
                So = wst
                for b in range(NB):
                    par = b % 2
                    tk.dma(tk.sp, S0[par], dr["st_wkv"][b].rearrange("h v k -> v h k"), writes=[R("S0", par)])
                    pt, rpt = bank(4)
                    ptv = pt[:, 0:256].rearrange("p (j v) -> p j v", j=4)
                    for j in range(4):
                        tk.op(tk.pe, lambda: nc.tensor.transpose(ptv[:, j, :], S0[par][:, 2 * j:2 * j + 2, :], self.ident[0:64, 0:64]),
                              reads=[R("S0", par), self.r_const], writes=[rpt], inc=(j == 3))
                    tk.op(tk.act, lambda: nc.scalar.copy(H0[par], ptv), reads=[rpt], writes=[R("H0", par)])
                    for j in range(4):
                        tk.op(tk.pe, lambda: nc.tensor.matmul(pyBv[0:64, j, 4 * b:4 * b + 4], H0[par][0:64, j, :], Rb[0:64, j, 4 * b:4 * b + 4], start=True, stop=True),
                              reads=[R("H0", par), R("Rb")], writes=[rpyB], inc=(j == 3))
                    for j in range(4):
                        tk.op(tk.pe, lambda: nc.tensor.matmul(pyCv[64:128, j, 4 * b:4 * b + 4], H0[par][64:128, j, :], Rb[64:128, j, 4 * b:4 * b + 4], start=True, stop=True),
                              reads=[R("H0", par), R("Rb")], writes=[rpyC], inc=(j == 3))
                    tk.op(tk.dve, lambda: nc.vector.tensor_scalar(Bm, Bh[:], cst["rowmask"][:, b:b + 1], None, op0=ALU.mult), reads=[R("Bh"), rc], writes=[R("Bm")])
                    tk.op(tk.dve, lambda: nc.vector.tensor_scalar(Km, Kh[:], cst["rowmask"][:, b:b + 1], None, op0=ALU.mult), reads=[R("Kh"), rc], writes=[R("Km")])
                    pg, rpg = bank(5)
                    pgv = pg[:, 0:256].rearrange("p (j t) -> p j t", j=4)
                    for h in range(8):
                        j = h // 2; pbs = 64 * (h % 2)
                        tk.op(tk.pe, lambda: nc.tensor.matmul(pgv[pbs:pbs + 64, j, :], Z6[:, h, 0:64], Bm[:, h * 64:(h + 1) * 64], start=True, stop=True),
                              reads=rZs + [R("Bm")], writes=[rpg], inc=(h == 7))
                    tk.op(tk.act, lambda: nc.scalar.copy(Gb, pgv), reads=[rpg], writes=[R("Gb")])
                    pgm, rpgm = bank(6)
                    tk.op(tk.pe, lambda: nc.tensor.matmul(pgm[0:64, :], cst["rowmask"][:, b:b + 1].to_broadcast([128, 64]), lw_tm[:], start=True, stop=True), reads=[R("tm", 0), rc], writes=[rpgm])
                    tk.op(tk.act, lambda: nc.scalar.activation(out=grow, in_=pgm[0:64, :], func=AF.Exp), reads=[rpgm], writes=[R("grow")])
                    pn, rpn = bank(7); pnE, rpnE = bank(0); pnF, rpnF = bank(4)
                    pnv = pn[0:64, :].rearrange("p (h k) -> p h k", h=8)
                    pnEv = pnE[0:64, :].rearrange("p (i two k) -> p i two k", two=2, k=64)
                    pnFv = pnF[0:64, :].rearrange("p (i two k) -> p i two k", two=2, k=64)
                    for h in range(8):
                        o = pnv[:, h, :]
                        tk.op(tk.pe, lambda: nc.tensor.matmul(o, Z6[:, h, 64:128], Bm[:, h * 64:(h + 1) * 64], start=True, stop=False), reads=rZs + [R("Bm")], writes=[rpn], inc=False)
                        tk.op(tk.pe, lambda: nc.tensor.matmul(o, v_tm[:, h * 64:(h + 1) * 64], Km[:, h * 64:(h + 1) * 64], start=False, stop=True), reads=[R("v_tm"), R("Km")], writes=[rpn], inc=(h == 7))
                    for j in range(4):
                        tk.op(tk.pe, lambda: nc.tensor.matmul(pnEv[:, j, 0, :], H0[par][0:64, j, :], Gb[0:64, j, :], start=True, stop=True), reads=[R("H0", par), R("Gb")], writes=[rpnE], inc=(j == 3))
                    for j in range(4):
                        tk.op(tk.pe, lambda: nc.tensor.matmul(pnFv[:, j, 1, :], H0[par][64:128, j, :], Gb[64:128, j, :], start=True, stop=True), reads=[R("H0", par), R("Gb")], writes=[rpnF], inc=(j == 3))
                    Sov = So.rearrange("p (i two) k -> p i two k", two=2)
                    tk.op(tk.dve, lambda: nc.vector.tensor_tensor(out=So, in0=S0[par], in1=grow.rearrange("p (h k) -> p h k", h=8), op=ALU.mult),
                          reads=[R("S0", par), R("grow")], writes=[R("d1")])
                    tk.op(tk.dve, lambda: nc.vector.tensor_tensor(out=So, in0=So, in1=pnv, op=ALU.add), reads=[rpn], writes=[R("d1")])
                    tk.op(tk.dve, lambda: nc.vector.tensor_tensor(out=Sov[:, :, 0, :], in0=Sov[:, :, 0, :], in1=pnEv[:, :, 0, :], op=ALU.add), reads=[rpnE], writes=[R("d1")])
                    tk.op(tk.dve, lambda: nc.vector.tensor_tensor(out=Sov[:, :, 1, :], in0=Sov[:, :, 1, :], in1=pnFv[:, :, 1, :], op=ALU.add), reads=[rpnF], writes=[R("d1")])
                    self.out_toks.append(tk.dma(tk.sp, dr["wkv_s"][b].rearrange("h v k -> v h k"), So, reads=[R("d1")]))
                tk.barrier()

            wo_slots = []
            for si, (t0, S, nb, tl) in enumerate(SEGS):
                sample = nb > 1
                ti = min(t0 // 512, 4)
                if si == 0:
                    self.rmsnorm_seg(t0, S, ti, gcol, rmsb, hseg)
                col_slabs = [(c0, min(512, 3328 - c0)) for c0 in range(0, 3328, 512)]
                pp = 0
                for (w0c, wn) in col_slabs:
                    s_ = self.next_slot()
                    v = self.ring[:, s_, 0:KC * wn].rearrange("p (k f) -> p k f", k=KC)
                    tk.dma(tk.pool, v, dr["w_in_even"][0][:, w0c:w0c + wn].rearrange("(k p) f -> p k f", p=128), writes=[self.rslot[s_]], dsem=self.slot_sem[s_])
                    for q in range(wn // 128):
                        c = w0c // 128 + q
                        if c < 14 or c >= 22:
                            pb = pp % 3; pp += 1
                        elif c < 18:
                            pb = 3
                        else:
                            pb = 4 + (c % 2)
                        p_, rp = bank(pb)
                        for kc in range(KC):
                            tk.op(tk.pe, lambda: nc.tensor.matmul(p_[:, 0:S], v[:, kc, q * 128:(q + 1) * 128], hseg[:, kc, 0:S], start=(kc == 0), stop=(kc == KC - 1)),
                                  reads=[self.rslot[s_], R("hseg")], writes=[rp], inc=(kc == KC - 1))
                        if c < 14:
                            pe_ = pext[c % 2]; rpe = R("pext", c % 2)
                            pv3 = pe_[:, 0:nb * (tl + 1)].rearrange("p (b t) -> p b t", b=nb)
                            if sample:
                                tk.op(tk.dve, lambda: nc.vector.tensor_copy(pv3[:, :, 0], stT[:, c, :]), reads=[rc], writes=[rpe])
                            else:
                                tk.op(tk.dve, lambda: nc.vector.tensor_copy(pv3[:, :, 0], pcar[:, c:c + 1]), reads=[R("pcar")], writes=[rpe])
                            tk.op(tk.act, lambda: nc.scalar.copy(pv3[:, :, 1:tl + 1], p_[:, 0:S].rearrange("p (b t) -> p b t", b=nb)), reads=[rp], writes=[rpe])
                            dt_ = dtmp[c % 2]; rdt = R("dtmp", 0)
                            dv3 = dt_[:, 0:S].rearrange("p (b t) -> p b t", b=nb)
                            tk.op(tk.dve, lambda: nc.vector.tensor_tensor(out=dv3, in0=pv3[:, :, 0:tl], in1=pv3[:, :, 1:tl + 1], op=ALU.subtract), reads=[rpe], writes=[rdt])
                            tk.op(tk.dve, lambda: nc.vector.scalar_tensor_tensor(out=psA[:, c, 0:S].rearrange("p (b t) -> p b t", b=nb), in0=dv3,
                                                                                 scalar=self.pfm[:, P["shift_mu"] + c:P["shift_mu"] + c + 1],
                                                                                 in1=pv3[:, :, 1:tl + 1], op0=ALU.mult, op1=ALU.add),
                                  reads=[rdt, rpe, self.r_const], writes=[R("psA")])
                            if sample:
                                tk.op(tk.act, lambda: nc.scalar.copy(pshs[:, c, :], pv3[:, :, tl]), reads=[rpe], writes=[R("pshs")])
                            else:
                                tk.op(tk.act, lambda: nc.scalar.copy(pcar[:, c:c + 1], pe_[:, S:S + 1]), reads=[rpe], writes=[R("pcar")])
                        elif c < 18:
                            tk.op(tk.act, lambda: nc.scalar.copy(hcv[:, c - 14, 0:S], p_[:, 0:S]), reads=[rp], writes=[R("hcv")])
                        elif c < 22:
                            tk.op(tk.act, lambda: nc.scalar.copy(bgv[:, c - 18, 0:S], p_[:, 0:S]), reads=[rp], writes=[R("bgv")])
                        else:
                            j = c - 22
                            zv = zext[:, j, 0:nb * (tl + 2)].rearrange("p (b t) -> p b t", b=nb)
                            if sample:
                                tk.op(tk.dve, lambda: nc.vector.tensor_copy(zv[:, :, 0:2], cbT[:, j, :, :]), reads=[rc], writes=[R("zext")])
                            tk.op(tk.dve, lambda: nc.vector.tensor_tensor(out=zv[:, :, 2:tl + 2], in0=p_[:, 0:S].rearrange("p (b t) -> p b t", b=nb),
                                                                          in1=hcv[:, j, 0:S].rearrange("p (b t) -> p b t", b=nb), op=ALU.mult),
                                  reads=[rp, R("hcv")], writes=[R("zext")])
                            y_ = ycv[j % 2]; ry = R("ycv", 0)
                            yv = y_[:, 0:S].rearrange("p (b t) -> p b t", b=nb)
                            tk.op(tk.act, lambda: nc.scalar.activation(out=yv, in_=zv[:, :, 0:tl], func=AF.Copy, scale=self.pfm[:, P["cw0"] + j:P["cw0"] + j + 1]),
                                  reads=[R("zext"), self.r_const], writes=[ry])
                            tk.op(tk.dve, lambda: nc.vector.scalar_tensor_tensor(out=yv, in0=zv[:, :, 1:tl + 1], scalar=self.pfm[:, P["cw1"] + j:P["cw1"] + j + 1],
                                                                                 in1=yv, op0=ALU.mult, op1=ALU.add), reads=[R("zext"), self.r_const], writes=[ry])
                            tk.op(tk.dve, lambda: nc.vector.scalar_tensor_tensor(out=yv, in0=zv[:, :, 2:tl + 2], scalar=self.pfm[:, P["cw2"] + j:P["cw2"] + j + 1],
                                                                                 in1=yv, op0=ALU.mult, op1=ALU.add), reads=[R("zext"), self.r_const], writes=[ry])
                            tk.op(tk.dve, lambda: nc.vector.tensor_tensor(out=mT[:, 4 + j, 0:S], in0=y_[:, 0:S], in1=bgv[:, j, 0:S], op=ALU.mult),
                                  reads=[ry, R("bgv")], writes=[R("mT")])
                            if sample:
                                tk.op(tk.act, lambda: nc.scalar.copy(zs[:, j, :, :], zv[:, :, tl:tl + 2]), reads=[R("zext")], writes=[R("zs")])
                            else:
                                tk.op(tk.act, lambda: nc.scalar.copy(zext[:, j, 0:2], zext[:, j, S:S + 2]), reads=[], writes=[R("zext")])
                if do_rwkv and sample:
                    tk.op(tk.dve, lambda: nc.vector.memset(psA[:, 0:12, 64:128], 0.0), writes=[R("psA")])
                    for t_ in (aT, bT, lwT):
                        tk.op(tk.dve, lambda: nc.vector.memset(t_[:, :, 64:128], 0.0), writes=[R("prep")])
                if do_rwkv:
                    self.rwkv_prep(S, psA, aT, bT, lwT, aiT, gT, bonT, t12, sgd, tmpA, tmpB, lora12, g2s, negw0, cone, chalf, bones, rc)
                    for c0 in range(0, S, 128):
                        chunk(c0, sample)
                else:
                    tk.op(tk.dve, lambda: nc.vector.memset(mT[:, 0:4, 0:S], 0.0), writes=[R("mT")])
                if si + 1 < len(SEGS):
                    nt0, nS, _, _ = SEGS[si + 1]
                    self.rmsnorm_seg(nt0, nS, min(nt0 // 512, 4), gcol, rmsb, hseg)
                for half in range(2):
                    s_, v = self.load_w512(dr["w_out_even"][0], half * 512)
                    for q in range(4):
                        dc = half * 4 + q
                        p_, rp = bank(5 + (q % 2))
                        for kc in range(KC):
                            tk.op(tk.pe, lambda: nc.tensor.matmul(p_[:, 0:S], v[:, kc, q * 128:(q + 1) * 128], mT[:, kc, 0:S], start=(kc == 0), stop=(kc == KC - 1)),
                                  reads=[self.rslot[s_], R("mT")], writes=[rp], inc=(kc == KC - 1))
                        tk.op(tk.dve, lambda: nc.vector.tensor_tensor(out=self.xT[:, dc, t0:t0 + S], in0=p_[:, 0:S], in1=self.xT[:, dc, t0:t0 + S], op=ALU.add),
                              reads=[rp, self.rx(ti)], writes=[self.rx(ti)])
                if si == len(SEGS) - 2:
                    self.mix0_prompt_outputs(pcar, zext, Hs[hcur[0]], R("H", hcur[0]), wst, (dtmp[0], hcv))
            self.mix0_sample_outputs(pshs, zs, psA, hcv)
            tk.barrier()

    def rmsnorm_seg(self, t0, S, ti, gcol, b, hseg):
        nc, tk = self.nc, self.tk
        sq, rsq, sd, rs, rsd, rrs = b["sq"], b["rsq"], b["sd"], b["rs"], b["rsd"], b["rrs"]
        p_, rp = self.ps[7], self.rps[7]
        for kc in range(KC):
            s = kc % 2
            tk.op(tk.act, lambda: nc.scalar.activation(out=sq[s][:, 0:S], in_=self.xT[:, kc, t0:t0 + S], func=AF.Square),
                  reads=[self.rx(ti)], writes=[rsq[s]])
            tk.op(tk.pe, lambda: nc.tensor.matmul(p_[:, 0:S], self.onesm[:], sq[s][:, 0:S], start=(kc == 0), stop=(kc == KC - 1)),
                  reads=[rsq[s], self.r_const], writes=[rp], inc=True)
        tk.op(tk.act, lambda: nc.scalar.activation(out=sd[:, 0:S], in_=p_[:, 0:S], func=AF.Ln, bias=self.eps_t[:, 0:1]),
              reads=[rp, self.r_const], writes=[rsd])
        tk.op(tk.act, lambda: nc.scalar.activation(out=rs[0][:, 0:S], in_=sd[:, 0:S], func=AF.Exp, scale=-0.5), reads=[rsd], writes=[rrs[0]])
        for kc in range(KC):
            tk.op(tk.dve, lambda: nc.vector.scalar_tensor_tensor(out=hseg[:, kc, 0:S], in0=self.xT[:, kc, t0:t0 + S],
                                                                 scalar=self.pfm[:, gcol + kc:gcol + kc + 1],
                                                                 in1=rs[0][:, 0:S], op0=ALU.mult, op1=ALU.mult),
                  reads=[self.rx(ti), rrs[0], self.r_const], writes=[tk.R("hseg")])

    def rwkv_prep(self, S, psA, aT, bT, lwT, aiT, gT, bonT, t12, sgd, tmpA, tmpB, lora12, g2s, negw0, cone, chalf, bones, rc):
        nc, tk = self.nc, self.tk
        R = tk.R; P = PFM_IDX
        rA = R("psA"); rp_ = R("prep")
        pf = self.pfm
        tk.op(tk.act, lambda: nc.scalar.activation(out=t12[0:64, 0:S], in_=psA[0:64, 12, 0:S], func=AF.Tanh), reads=[rA], writes=[R("t12")])
        tk.op(tk.act, lambda: nc.scalar.copy(t12[64:128, 0:S], psA[64:128, 12, 0:S]), reads=[rA], writes=[R("t12")])
        tk.op(tk.act, lambda: nc.scalar.activation(out=sgd[:, 0:S], in_=psA[:, 13, 0:S], func=AF.Sigmoid), reads=[rA], writes=[R("sgd")])
        t0_, t1_ = tmpA
        b0_, b1_ = tmpB
        for j in range(4):
            js = slice(j * 128, (j + 1) * 128)
            pw, rpw = self.ps[0], self.rps[0]
            pa, rpa = self.ps[1], self.rps[1]
            pg, rpg = self.ps[2], self.rps[2]
            pss, rpss = self.ps[3], self.rps[3]
            pbn, rpbn = self.ps[4], self.rps[4]
            tk.op(tk.pe, lambda: nc.tensor.matmul(pw[:, 0:S], lora12[0:64, js], t12[0:64, 0:S], start=True, stop=True), reads=[R("t12"), rc], writes=[rpw])
            tk.op(tk.pe, lambda: nc.tensor.matmul(pa[:, 0:S], lora12[64:128, js], t12[64:128, 0:S], start=True, stop=True), reads=[R("t12"), rc], writes=[rpa])
            tk.op(tk.pe, lambda: nc.tensor.matmul(pg[:, 0:S], g2s[:, js], sgd[:, 0:S], start=True, stop=True), reads=[R("sgd"), rc], writes=[rpg])
            tk.op(tk.act, lambda: nc.scalar.activation(out=lwT[:, j, 0:S], in_=pw[:, 0:S], func=AF.Sigmoid, bias=pf[:, P["decay_w0"] + j:P["decay_w0"] + j + 1]),
                  reads=[rpw, self.r_const], writes=[rp_])
            tk.op(tk.dve, lambda: nc.vector.tensor_scalar(lwT[:, j, 0:S], lwT[:, j, 0:S], -math.exp(-0.5), None, op0=ALU.mult), reads=[], writes=[rp_])
            tk.op(tk.act, lambda: nc.scalar.activation(out=aiT[:, 0, 0:S], in_=pa[:, 0:S], func=AF.Sigmoid, bias=pf[:, P["iclr_a0"] + j:P["iclr_a0"] + j + 1]),
                  reads=[rpa, self.r_const], writes=[R("aiT")])
            tk.op(tk.act, lambda: nc.scalar.copy(gT[:, j, 0:S], pg[:, 0:S]), reads=[rpg], writes=[rp_])
            kj = psA[:, 4 + j, 0:S]
            tk.op(tk.dve, lambda: nc.vector.tensor_scalar(t1_[:, 0:S], kj, pf[:, P["k_k"] + j:P["k_k"] + j + 1], None, op0=ALU.mult), reads=[rA, self.r_const], writes=[R("tA", 1)])
            tk.op(tk.act, lambda: nc.scalar.activation(out=b0_[:, 0:S], in_=t1_[:, 0:S], func=AF.Square), reads=[R("tA", 1)], writes=[R("tB", 0)])
            tk.op(tk.pe, lambda: nc.tensor.matmul(pss[:, 0:S], bones[:], b0_[:, 0:S], start=True, stop=True), reads=[R("tB", 0), rc], writes=[rpss])
            tk.op(tk.dve, lambda: nc.vector.tensor_scalar(t0_[:, 0:S], pss[:, 0:S], 1e-18, None, op0=ALU.max), reads=[rpss], writes=[R("tA", 0)])
            tk.op(tk.act, lambda: nc.scalar.activation(out=t0_[:, 0:S], in_=t0_[:, 0:S], func=AF.Ln), reads=[], writes=[R("tA", 0)])
            tk.op(tk.act, lambda: nc.scalar.activation(out=t0_[:, 0:S], in_=t0_[:, 0:S], func=AF.Exp, scale=-0.5), reads=[], writes=[R("tA", 0)])
            tk.op(tk.dve, lambda: nc.vector.tensor_tensor(out=t1_[:, 0:S], in0=t1_[:, 0:S], in1=t0_[:, 0:S], op=ALU.mult), reads=[R("tA", 0)], writes=[R("tA", 1)])
            tk.op(tk.dve, lambda: nc.vector.tensor_scalar(aT[:, j, 0:S], t1_[:, 0:S], -1.0, None, op0=ALU.mult), reads=[R("tA", 1)], writes=[rp_])
            tk.op(tk.dve, lambda: nc.vector.tensor_tensor(out=bT[:, j, 0:S], in0=t1_[:, 0:S], in1=aiT[:, 0, 0:S], op=ALU.mult), reads=[R("tA", 1), R("aiT")], writes=[rp_])
            tk.op(tk.dve, lambda: nc.vector.tensor_scalar(t0_[:, 0:S], aiT[:, 0, 0:S], -1.0, None, op0=ALU.add), reads=[R("aiT")], writes=[R("tA", 0)])
            tk.op(tk.dve, lambda: nc.vector.tensor_scalar(t0_[:, 0:S], t0_[:, 0:S], pf[:, P["k_a"] + j:P["k_a"] + j + 1], None, op0=ALU.mult), reads=[self.r_const], writes=[R("tA", 0)])
            tk.op(tk.dve, lambda: nc.vector.scalar_tensor_tensor(out=kj, in0=t0_[:, 0:S], scalar=1.0, in1=kj, op0=ALU.add, op1=ALU.mult), reads=[R("tA", 0)], writes=[rA])
            tk.op(tk.dve, lambda: nc.vector.scalar_tensor_tensor(out=b1_[:, 0:S], in0=psA[:, j, 0:S], scalar=pf[:, P["r_k"] + j:P["r_k"] + j + 1], in1=kj, op0=ALU.mult, op1=ALU.mult),
                  reads=[rA, self.r_const], writes=[R("tB", 1)])
            tk.op(tk.pe, lambda: nc.tensor.matmul(pbn[:, 0:S], bones[:], b1_[:, 0:S], start=True, stop=True), reads=[R("tB", 1), rc], writes=[rpbn])
            tk.op(tk.dve, lambda: nc.vector.tensor_tensor(out=bonT[:, j, 0:S], in0=pbn[:, 0:S], in1=psA[:, 8 + j, 0:S], op=ALU.mult), reads=[rpbn, rA], writes=[rp_])

    def mix0_prompt_outputs(self, pcar, zext, Hfin, rH, wst, les):
        nc, tk = self.nc, self.tk
        dr = self.dr
        R = tk.R
        dtmp0, hcv = les
        stg14 = dtmp0[0:14, 0:128]
        stg2 = hcv[0:2, :, 0:128]
        p_, rp = self.ps[0], self.rps[0]
        tk.op(tk.pe, lambda: nc.tensor.transpose(p_[0:14, 0:128], pcar[:, 0:14], self.ident[:]), reads=[R("pcar"), self.r_const], writes=[rp])
        tk.op(tk.dve, lambda: nc.vector.tensor_copy(stg14, p_[0:14, 0:128]), reads=[rp], writes=[R("dtmp", 0)])
        self.out_toks.append(tk.dma(tk.sp, dr["sh_p"].rearrange("o (c p) -> (o c) p", p=128), stg14, reads=[R("dtmp", 0)]))
        p_, rp = self.ps[1], self.rps[1]
        for j in range(4):
            tk.op(tk.pe, lambda: nc.tensor.transpose(p_[0:2, j * 128:(j + 1) * 128], zext[:, j, 0:2], self.ident[:]), reads=[R("zext"), self.r_const], writes=[rp], inc=(j == 3))
        tk.op(tk.dve, lambda: nc.vector.tensor_copy(stg2, p_[0:2, :].rearrange("p (q f) -> p q f", q=4)), reads=[rp], writes=[R("hcv")])
        self.out_toks.append(tk.dma(tk.sp, dr["conv_p"].rearrange("r (q f) -> r q f", q=4), stg2, reads=[R("hcv")]))
        p_, rp = self.ps[2], self.rps[2]
        for j in range(4):
            tk.op(tk.pe, lambda: nc.tensor.transpose(p_[0:64, j * 128:(j + 1) * 128], Hfin[:, j, :], self.ident[:]), reads=[rH, self.r_const], writes=[rp], inc=(j == 3))
        tk.op(tk.dve, lambda: nc.vector.tensor_copy(wst, p_[0:64, :].rearrange("p (h k) -> p h k", h=8)), reads=[rp], writes=[R("d1")])
        self.out_toks.append(tk.dma(tk.sp, dr["wkv_p"].rearrange("h v k -> v h k"), wst, reads=[R("d1")]))

    def mix0_sample_outputs(self, pshs, zs, psA, hcv):
        nc, tk = self.nc, self.tk
        dr = self.dr
        R = tk.R
        stg = psA[0:NB, :, 0:128]
        cstg = hcv[0:2 * NB, :, 0:128]
        for c in range(14):
            bk = c // 4
            p_, rp = self.ps[bk], self.rps[bk]
            last = (c % 4 == 3) or c == 13
            tk.op(tk.pe, lambda: nc.tensor.transpose(p_[0:NB, (c % 4) * 128:(c % 4 + 1) * 128], pshs[:, c, :], self.ident[:]), reads=[R("pshs"), self.r_const], writes=[rp], inc=last)
            if last:
                nq = c % 4 + 1
                tk.op(tk.dve, lambda: nc.vector.tensor_copy(stg[:, bk * 4:bk * 4 + nq, :], p_[0:NB, 0:nq * 128].rearrange("p (q f) -> p q f", q=nq)),
                      reads=[rp], writes=[R("psA")])
        self.out_toks.append(tk.dma(tk.sp, dr["sh_s"].rearrange("b (c f) -> b c f", c=14), stg, reads=[R("psA")]))
        p_, rp = self.ps[4], self.rps[4]
        for j in range(4):
            tk.op(tk.pe, lambda: nc.tensor.transpose(p_[0:2 * NB, j * 128:(j + 1) * 128], zs[:, j, :, :], self.ident[:]), reads=[R("zs"), self.r_const], writes=[rp], inc=(j == 3))
        tk.op(tk.dve, lambda: nc.vector.tensor_copy(cstg, p_[0:2 * NB, :].rearrange("p (q f) -> p q f", q=4)), reads=[rp], writes=[R("hcv")])
        self.out_toks.append(tk.dma(tk.sp, dr["conv_s"].rearrange("b r (q f) -> (b r) q f", q=4), cstg, reads=[R("hcv")]))

    def final(self):
        nc, tk = self.nc, self.tk
        with ExitStack() as les:
            b = self.rmsnorm_bufs("fin", les)
            yT = [self.sb(f"yT{i}", [128, KC, 512], F32, les) for i in range(2)]
            ryT = [tk.R("yT", i) for i in range(2)]
            ostg = [self.sb(f"ostg{i}", [128, D], F32, les) for i in range(2)]
            rost = [tk.R("ostg", i) for i in range(2)]
            bi = 0
            for ti, (t0, n) in enumerate(TILES):
                yb = ti % 2
                self.rmsnorm_tile(ti, PFM_IDX["final"], b, dst_fn=lambda kc: yT[yb][:, kc, 0:n], wr=[ryT[yb]])
                nblk = (n + 127) // 128
                for blk in range(nblk):
                    nn = min(128, n - blk * 128)
                    s = bi % 2
                    for half in range(2):
                        pb = (bi * 2 + half) % 6
                        for q in range(4):
                            kc = half * 4 + q
                            tk.op(tk.pe, lambda: nc.tensor.transpose(self.ps[pb][0:nn, q * 128:(q + 1) * 128],
                                                                    yT[yb][:, kc, blk * 128:blk * 128 + nn], self.ident[:]),
                                  reads=[ryT[yb], self.r_const], writes=[self.rps[pb]], inc=(q == 3))
                        dstv = ostg[s][0:nn, half * 512:(half + 1) * 512]
                        if half == 0:
                            tk.op(tk.dve, lambda: nc.vector.tensor_copy(dstv, self.ps[pb][0:nn, :]), reads=[self.rps[pb]], writes=[rost[s]])
                        else:
                            tk.op(tk.act, lambda: nc.scalar.copy(dstv, self.ps[pb][0:nn, :]), reads=[self.rps[pb]], writes=[rost[s]])
                    if ti < 4:
                        dst = self.dr["y_p"][t0 + blk * 128:t0 + blk * 128 + nn, :]
                    else:
                        dst = self.dr["y_s"][0:nn, :]
                    self.out_toks.append(tk.dma(tk.sp, dst, ostg[s][0:nn, :], reads=[rost[s]]))
                    bi += 1
            for t in self.out_toks:
                tk._wait(tk.sp, t)

    def build(self):
        nc, tk = self.nc, self.tk
        self.declare()
        self.setup()
        self.eps_t = self.sb("eps_t", [128, 1], F32)
        tk.op(tk.dve, lambda: nc.vector.memset(self.eps_t[:], RMS_EPS), writes=[self.r_const])
        self.eps_ln = self.sb("eps_ln", [128, 1], F32)
        tk.op(tk.dve, lambda: nc.vector.memset(self.eps_ln[:], 1e-5), writes=[self.r_const])
        self.load_x()
        dr = self.dr
        for l in range(self.cfg.get("layers", 2)):
            if self.cfg.get("ffn", True):
                self.ffn(dr["f1_wg"][l], dr["f1_wu"][l], dr["f1_wd"][l], PFM_IDX["norms"] + (l * 4 + 0) * 8, f"f1_{l}")
            if l == 0 and self.cfg.get("mix0", True):
                self.mix0()
            if l == 1 and self.cfg.get("mix1", True):
                self.mix1()
            if self.cfg.get("xattn", True):
                self.xattn(l)
            if self.cfg.get("ffn", True):
                self.ffn(dr["f2_wg"][l], dr["f2_wu"][l], dr["f2_wd"][l], PFM_IDX["norms"] + (l * 4 + 3) * 8, f"f2_{l}")
        self.final()


def build_nc(cfg=None):
    cfg = cfg or {}
    nc = bass.Bass("TRN2", target_bir_lowering=False)
    with ExitStack() as es:
        tk = TK(nc, es)
        k = Kern(nc, es, tk, cfg)
        k.build()
    return nc, k


def make_in_maps(inp):
    consts = _consts()
    pfm, _ = _pack_pfm(inp)
    maps = []
    for c in range(NCORES):
        m = {}
        m["xp"] = np.ascontiguousarray(inp["x_prompt"][c])
        m["xs"] = np.ascontiguousarray(inp["x_sample"][c * NB:(c + 1) * NB].reshape(NS, D))
        m["pfm"] = pfm
        m["ident"] = consts["ident"]
        m["onesm"] = consts["onesm"]
        for nm in ("f1_wg", "f1_wu", "f1_wd", "f2_wg", "f2_wu", "f2_wd", "xa_wq", "xa_wk", "xa_wv", "xa_wo"):
            m[nm] = inp[nm]
        m["ones"] = consts["ones"]
        m["triu"] = consts["triu"]; m["mask_blk"] = consts["mask_blk"]
        for nm in ("tri", "sut", "mask2", "maskxt", "trib", "sutb", "mask2b", "maskxtb", "rowmask", "bones64", "bones"):
            m["c_" + nm] = consts[nm]
        for nm in ("w_in_even", "w_out_even", "decay_w2", "iclr_a2", "gate_g2"):
            m[nm] = inp[nm]
        m["st_shift"] = np.ascontiguousarray(inp["state_shift"][0, c * NB:(c + 1) * NB])
        m["st_wkv"] = np.ascontiguousarray(inp["state_wkv"][0, c * NB:(c + 1) * NB])
        m["st_conv"] = np.ascontiguousarray(inp["state_conv"][0, c * NB:(c + 1) * NB])
        for nm in ("w_in_odd", "w_out_odd", "gm_ln_w", "gm_ln_b", "gm_ws", "gm_bs"):
            m[nm] = inp[nm]
        m["memp"] = np.ascontiguousarray(inp["mem_prompt"][c])
        m["ck"] = np.ascontiguousarray(inp["cache_mem_k"][:, c * NB:(c + 1) * NB].reshape(2, NB, 256, D))
        m["cv"] = np.ascontiguousarray(inp["cache_mem_v"][:, c * NB:(c + 1) * NB].reshape(2, NB, 256, D))
        maps.append(m)
    return maps


def kernel(**inputs):
    inp = {k: np.asarray(v) for k, v in inputs.items()}
    nc, k = build_nc()
    maps = make_in_maps(inp)
    res = run_bass_kernel_spmd(nc, maps, core_ids=list(range(NCORES)))
    r = res.results
    C = range(NCORES)
    f = lambda a: np.ascontiguousarray(np.asarray(a, dtype=np.float32))
    y_p = np.stack([r[c]["y_p"] for c in C], axis=0)
    y_s = np.concatenate([r[c]["y_s"].reshape(NB, 4, D) for c in C], axis=0)
    sh_p = np.stack([r[c]["sh_p"].reshape(PROJ_A) for c in C], axis=0)[None]
    wkv_p = np.stack([r[c]["wkv_p"] for c in C], axis=0)[None]
    conv_p = np.stack([r[c]["conv_p"] for c in C], axis=0)[None]
    mk_p = np.stack([r[c]["mk_p"].reshape(2, 256, 4, 256) for c in C], axis=1)
    mv_p = np.stack([r[c]["mv_p"].reshape(2, 256, 4, 256) for c in C], axis=1)
    sh_s = np.concatenate([r[c]["sh_s"] for c in C], axis=0)[None]
    wkv_s = np.concatenate([r[c]["wkv_s"] for c in C], axis=0)[None]
    conv_s = np.concatenate([r[c]["conv_s"] for c in C], axis=0)[None]
    gv_s = np.concatenate([r[c]["gv_s"].reshape(NB, 4, D) for c in C], axis=0)[None]
    return tuple(f(a) for a in (y_p, y_s, sh_p, wkv_p, conv_p, mk_p, mv_p, sh_s, wkv_s, conv_s, gv_s))
```

```python
import math
import numpy as np
import concourse.bass as bass
import concourse.mybir as mybir
from concourse.bass_utils import run_bass_kernel_spmd
from contextlib import ExitStack

F32 = mybir.dt.float32
BF16 = mybir.dt.bfloat16
AF = mybir.ActivationFunctionType
ALU = mybir.AluOpType
AX = mybir.AxisListType

NCORES = 8
D = 1024
KC = 8
SEQ = 2048
NS = 64
NB = 16
T = SEQ + NS
DFF = 2816
NFC = 22
PROJ_A = 1792
PROJ_B = 1536
TILES = [(0, 512), (512, 512), (1024, 512), (1536, 512), (2048, 64)]
RMS_EPS = 1e-6


class Res:
    __slots__ = ("name", "w", "r")

    def __init__(self, name):
        self.name = name
        self.w = None
        self.r = []


class Eng:
    def __init__(self, name, e, sem, same_sync):
        self.name = name
        self.e = e
        self.sem = sem
        self.cnt = 0
        self.pend = False
        self.seen = {}
        self.same_sync = same_sync


class DSem:
    def __init__(self, sem):
        self.sem = sem
        self.total = 0


class TK:
    def __init__(self, nc, es, n_dma_sems=20, same_sync=True):
        self.nc = nc
        self.es = es
        mk = lambda n: es.enter_context(nc.semaphore(n))
        self.mk = mk
        self.pe = Eng("pe", nc.tensor, mk("s_pe"), False)
        self.dve = Eng("dve", nc.vector, mk("s_dve"), same_sync)
        self.act = Eng("act", nc.scalar, mk("s_act"), same_sync)
        self.pool = Eng("pool", nc.gpsimd, mk("s_pool"), same_sync)
        self.sp = Eng("sp", nc.sync, mk("s_sp"), False)
        self.engs = [self.pe, self.dve, self.act, self.pool, self.sp]
        self.dsems = [DSem(mk(f"s_d{i}")) for i in range(n_dma_sems)]
        self.dpool = {"pool": self.dsems[:n_dma_sems // 2], "hw": self.dsems[n_dma_sems // 2:]}
        self.dnext = {"pool": 0, "hw": 0}
        self.res = {}
        self.nwaits = 0
        self.nops = 0

    def R(self, *key):
        r = self.res.get(key)
        if r is None:
            r = self.res[key] = Res(key)
        return r

    def new_dsem(self, name):
        return DSem(self.mk(name))

    def _wait(self, eng, tok):
        if tok[0] == "E":
            _, src, n = tok
            if src is eng and not eng.same_sync:
                return
            if src.cnt < n:
                raise RuntimeError(f"wait on un-emitted milestone {src.name}:{n} (cnt={src.cnt}) from {eng.name}")
            key = src.name
            semobj = src.sem
        else:
            _, d, n = tok
            key = id(d)
            semobj = d.sem
        if eng.seen.get(key, 0) >= n:
            return
        eng.e.wait_ge(semobj, n)
        eng.seen[key] = n
        self.nwaits += 1

    @staticmethod
    def _deps(reads, writes):
        deps = []
        for r in reads:
            if r.w is not None:
                deps.append(r.w)
        for w in writes:
            if w.w is not None:
                deps.append(w.w)
            deps.extend(w.r)
        return deps

    @staticmethod
    def _compact(toks):
        best = {}
        for t in toks:
            k = id(t[1])
            if k not in best or best[k][2] < t[2]:
                best[k] = t
        return list(best.values())

    def _record(self, tok, reads, writes):
        for r in reads:
            r.r.append(tok)
            if len(r.r) > 48:
                r.r = self._compact(r.r)
        for w in writes:
            w.w = tok
            w.r = []

    def op(self, eng, fn, reads=(), writes=(), inc=True):
        if any(r.name[0] == "ps" for r in reads):
            writes = list(writes) + [r for r in reads if r.name[0] == "ps"]
            reads = [r for r in reads if r.name[0] != "ps"]
        for tok in self._deps(reads, writes):
            self._wait(eng, tok)
        ins = fn()
        self.nops += 1
        if inc:
            eng.cnt += 1
            ins.then_inc(eng.sem, 1)
            eng.pend = False
            tok = ("E", eng, eng.cnt)
        else:
            eng.pend = True
            tok = ("E", eng, eng.cnt + 1)
        self._record(tok, reads, writes)
        return ins

    def dma(self, q, out, in_, reads=(), writes=(), dsem=None, **kw):
        for tok in self._deps(reads, writes):
            self._wait(q, tok)
        if dsem is None:
            kind = "pool" if q is self.pool else "hw"
            lst = self.dpool[kind]
            dsem = lst[self.dnext[kind]]
            self.dnext[kind] = (self.dnext[kind] + 1) % len(lst)
            if dsem.total:
                self._wait(q, ("D", dsem, dsem.total))
        ins = q.e.dma_start(out=out, in_=in_, **kw)
        dsem.total += 16
        ins.then_inc(dsem.sem, 16)
        tok = ("D", dsem, dsem.total)
        self._record(tok, reads, writes)
        return tok

    def barrier(self):
        toks = []
        for e in self.engs:
            if e.pend:
                raise RuntimeError(f"barrier with pending instrs on {e.name}")
            if e.cnt:
                toks.append(("E", e, e.cnt))
        for d in self.dsems:
            if d.total:
                toks.append(("D", d, d.total))
        for e in self.engs:
            for t in toks:
                if t[0] == "E" and t[1] is e:
                    continue
                self._wait(e, t)


def _consts():
    c = {}
    c["ident"] = np.eye(128, dtype=np.float32)
    c["onesm"] = np.full((128, 128), 1.0 / D, dtype=np.float32)
    c["ones"] = np.ones((128, 128), dtype=np.float32)
    c["triu"] = np.triu(np.ones((128, 128), dtype=np.float32))
    i = np.arange(64)
    c["mask_blk"] = ((i[:, None] // 4 == i[None, :] // 4) & (i[:, None] <= i[None, :])).astype(np.float32)
    i2 = np.arange(128)
    sI, tI = i2[:, None], i2[None, :]
    same = (sI // 4 == tI // 4) & (sI < 64) & (tI < 64)
    f = lambda m: m.astype(np.float32)
    c["tri"] = f(sI <= tI); c["sut"] = f(sI > tI)
    c["trib"] = f((sI <= tI) & same); c["sutb"] = f((sI > tI) & same)
    strict = f(sI < tI); incl = f(sI <= tI)
    c["mask2"] = np.concatenate([strict, incl], axis=1)
    c["mask2b"] = np.concatenate([strict * same, incl * same], axis=1).astype(np.float32)
    c["maskxt"] = f(sI > tI)
    c["maskxtb"] = f((sI > tI) & same)
    c["rowmask"] = f((i2[:, None] // 4 == np.arange(NB)[None, :]) & (i2[:, None] < 64))
    bo = np.zeros((128, 128), np.float32); bo[:64, :64] = 1; bo[64:, 64:] = 1
    c["bones"] = bo; c["bones64"] = bo / 64.0
    return c


def _pack_pfm(inp):
    cols = []

    def add(v):
        v = np.asarray(v, np.float32).reshape(-1, 128)
        for r in v:
            cols.append(r)
    idx = {}
    idx["norms"] = len(cols)
    add(inp["norms"].reshape(8, D))
    idx["final"] = len(cols)
    add(inp["final_norm"])
    for nm in ("shift_mu", "decay_w0", "iclr_a0", "k_k", "k_a", "r_k", "lnx_w", "lnx_b"):
        idx[nm] = len(cols)
        add(inp[nm][0])
    for i in range(3):
        idx[f"cw{i}"] = len(cols)
        add(inp["conv_w"][0, i])
    return np.stack(cols, axis=1).copy(), idx


PFM_IDX = {"norms": 0, "final": 64, "shift_mu": 72, "decay_w0": 86, "iclr_a0": 90, "k_k": 94, "k_a": 98, "r_k": 102,
           "lnx_w": 106, "lnx_b": 110, "cw0": 114, "cw1": 118, "cw2": 122}
PFM_NCOL = 126


class Kern:
    def __init__(self, nc, es, tk, cfg):
        self.nc = nc
        self.es = es
        self.tk = tk
        self.cfg = cfg
        self.out_toks = []
        self.dr = {}

    def sb(self, name, shape, dt, es=None):
        return (es or self.es).enter_context(self.nc.sbuf_tensor(name, shape, dt))

    def din(self, name, shape, dt=F32):
        t = self.nc.dram_tensor(name, list(shape), dt, kind="ExternalInput").ap()
        self.dr[name] = t
        return t

    def dout(self, name, shape, dt=F32):
        t = self.nc.dram_tensor(name, list(shape), dt, kind="ExternalOutput").ap()
        self.dr[name] = t
        return t

    def declare(self):
        d = self.din
        d("xp", [SEQ, D]); d("xs", [NS, D])
        d("pfm", [128, PFM_NCOL]); d("ident", [128, 128]); d("onesm", [128, 128])
        for nm in ("f1_wg", "f1_wu", "f2_wg", "f2_wu"):
            d(nm, [2, D, DFF])
        for nm in ("f1_wd", "f2_wd"):
            d(nm, [2, DFF, D])
        d("memp", [256, D]); d("ck", [2, NB, 256, D]); d("cv", [2, NB, 256, D])
        d("ones", [128, 128])
        for nm in ("xa_wq", "xa_wk", "xa_wv", "xa_wo"):
            d(nm, [2, D, D])
        d("w_in_odd", [1, D, 2 * D]); d("w_out_odd", [1, D, D]); d("gm_ln_w", [1, D]); d("gm_ln_b", [1, D])
        d("gm_ws", [1, 8, 128, 128]); d("gm_bs", [1, 8, 128]); d("triu", [128, 128]); d("mask_blk", [64, 64])
        d("w_in_even", [1, D, 3328]); d("w_out_even", [1, D, D]); d("decay_w2", [1, 64, 512]); d("iclr_a2", [1, 64, 512]); d("gate_g2", [1, 128, 512])
        d("st_shift", [NB, PROJ_A]); d("st_wkv", [NB, 8, 64, 64]); d("st_conv", [NB, 2, 512])
        for nm, shp in (("tri", [128, 128]), ("sut", [128, 128]), ("mask2", [128, 256]), ("maskxt", [128, 128]),
                        ("trib", [128, 128]), ("sutb", [128, 128]), ("mask2b", [128, 256]), ("maskxtb", [128, 128]),
                        ("rowmask", [128, NB]), ("bones64", [128, 128]), ("bones", [128, 128])):
            d("c_" + nm, shp)
        o = self.dout
        o("sh_p", [1, PROJ_A]); o("wkv_p", [8, 64, 64]); o("conv_p", [2, 512])
        o("sh_s", [NB, PROJ_A]); o("wkv_s", [NB, 8, 64, 64]); o("conv_s", [NB, 2, 512])
        o("gv_s", [NS, D])
        o("y_p", [SEQ, D]); o("y_s", [NS, D])
        o("mk_p", [2, 256, D]); o("mv_p", [2, 256, D])

    def setup(self):
        nc, tk = self.nc, self.tk
        self.xT = self.sb("xT", [128, KC, T], F32)
        self.hT = None
        self.pfm = self.sb("pfm_sb", [128, PFM_NCOL], F32)
        self.ident = self.sb("ident_sb", [128, 128], F32)
        self.onesm = self.sb("onesm_sb", [128, 128], BF16)
        self.ps = [self.es.enter_context(nc.psum_tensor(f"ps{i}", [128, 512], F32)) for i in range(8)]
        self.rps = [tk.R("ps", i) for i in range(8)]
        self.NSLOT = 3
        self.ring = self.sb("ring", [128, self.NSLOT, 4096], BF16)
        self.rslot = [tk.R("slot", i) for i in range(self.NSLOT)]
        self.slot_sem = [tk.new_dsem(f"s_slot{i}") for i in range(self.NSLOT)]
        self.slot_next = 0
        self.r_const = tk.R("const")
        tk.dma(tk.sp, self.pfm[:], self.dr["pfm"], writes=[self.r_const])
        tk.dma(tk.sp, self.ident[:], self.dr["ident"], writes=[self.r_const])
        tk.dma(tk.pool, self.onesm[:], self.dr["onesm"], writes=[self.r_const])
        self.ident_bf = self.sb("ident_bf", [128, 128], BF16)
        self.ones_bf = self.sb("ones_bf", [128, 128], BF16)
        tk.dma(tk.pool, self.ident_bf[:], self.dr["ident"], writes=[self.r_const])
        tk.dma(tk.pool, self.ones_bf[:], self.dr["ones"], writes=[self.r_const])

    def alloc_h(self, les, tag):
        self.hT = self.sb(f"hT_{tag}", [128, KC, T], BF16, les)

    def rx(self, ti):
        return self.tk.R("x", ti)

    def rh(self, ti):
        return self.tk.R("h", ti)

    def load_x(self):
        nc, tk = self.nc, self.tk
        with ExitStack() as les:
            stg = [self.sb(f"xstg{i}", [128, D], F32, les) for i in range(2)]
            rstg = [tk.R("xstg", i) for i in range(2)]
            blocks = [("xp", b * 128, 128, b * 128) for b in range(SEQ // 128)] + [("xs", 0, NS, SEQ)]
            for bi, (src, r0, n, t0) in enumerate(blocks):
                s = bi % 2
                tk.dma(tk.sp if bi % 2 == 0 else tk.act, stg[s][0:n, :], self.dr[src][r0:r0 + n, :], writes=[rstg[s]])
                ti = min(t0 // 512, 4)
                for half in range(2):
                    pb = (bi * 2 + half) % 8
                    for q in range(4):
                        kc = half * 4 + q
                        tk.op(tk.pe, lambda: nc.tensor.transpose(self.ps[pb][:, q * 128:q * 128 + n],
                                                                stg[s][0:n, kc * 128:(kc + 1) * 128],
                                                                self.ident[0:n, 0:n]),
                              reads=[rstg[s], self.r_const], writes=[self.rps[pb]], inc=(q == 3))
                    src_ps = self.ps[pb][:].rearrange("p (q t) -> p q t", q=4)[:, :, 0:n]
                    dst = self.xT[:, half * 4:half * 4 + 4, t0:t0 + n]
                    if half == 0:
                        tk.op(tk.dve, lambda: nc.vector.tensor_copy(dst, src_ps), reads=[self.rps[pb]], writes=[self.rx(ti)])
                    else:
                        tk.op(tk.act, lambda: nc.scalar.copy(dst, src_ps), reads=[self.rps[pb]], writes=[self.rx(ti)])
            tk.barrier()

    def rmsnorm_bufs(self, tag, les, n=512, nrs=2):
        tk = self.tk
        b = {}
        b["sq"] = [self.sb(f"sq{i}_{tag}", [128, n], BF16, les) for i in range(2)]
        b["rsq"] = [tk.R("sq", i) for i in range(2)]
        b["sd"] = self.sb(f"sd_{tag}", [128, n], F32, les)
        b["rs"] = [self.sb(f"rs{i}_{tag}", [128, n], F32, les) for i in range(nrs)]
        b["rsd"] = tk.R("sd")
        b["rrs"] = [tk.R("rs", i) for i in range(2)]
        return b

    def rmsnorm_tile(self, ti, gcol, b, dst_fn=None, wr=None):
        nc, tk = self.nc, self.tk
        t0, n = TILES[ti]
        sq, rsq, sd, rs, rsd, rrs = b["sq"], b["rsq"], b["sd"], b["rs"], b["rsd"], b["rrs"]
        pb = 6 + ti % 2
        for kc in range(KC):
            s = kc % 2
            tk.op(tk.act, lambda: nc.scalar.activation(out=sq[s][:, 0:n], in_=self.xT[:, kc, t0:t0 + n], func=AF.Square),
                  reads=[self.rx(ti)], writes=[rsq[s]])
            tk.op(tk.pe, lambda: nc.tensor.matmul(self.ps[pb][:, 0:n], self.onesm[:], sq[s][:, 0:n],
                                                  start=(kc == 0), stop=(kc == KC - 1)),
                  reads=[rsq[s], self.r_const], writes=[self.rps[pb]], inc=True)
        tk.op(tk.act, lambda: nc.scalar.activation(out=sd[:, 0:n], in_=self.ps[pb][:, 0:n], func=AF.Ln, bias=self.eps_t[:, 0:1]),
              reads=[self.rps[pb], self.r_const], writes=[rsd])
        r = ti % 2
        tk.op(tk.act, lambda: nc.scalar.activation(out=rs[r][:, 0:n], in_=sd[:, 0:n], func=AF.Exp, scale=-0.5), reads=[rsd], writes=[rrs[r]])
        for kc in range(KC):
            if dst_fn is None:
                dst = self.hT[:, kc, t0:t0 + n]
                w = [self.rh(ti)]
            else:
                dst = dst_fn(kc)
                w = wr
            tk.op(tk.dve, lambda: nc.vector.scalar_tensor_tensor(out=dst, in0=self.xT[:, kc, t0:t0 + n],
                                                                 scalar=self.pfm[:, gcol + kc:gcol + kc + 1],
                                                                 in1=rs[r][:, 0:n], op0=ALU.mult, op1=ALU.mult),
                  reads=[self.rx(ti), rrs[r], self.r_const], writes=w)

    def rmsnorm(self, gcol, les=None, keep=False):
        if keep:
            b = self.rmsnorm_bufs(f"g{gcol}", les)
            for ti in range(len(TILES)):
                self.rmsnorm_tile(ti, gcol, b)
            return
        with ExitStack() as ies:
            b = self.rmsnorm_bufs(f"g{gcol}", ies)
            for ti in range(len(TILES)):
                self.rmsnorm_tile(ti, gcol, b)
            self.tk.barrier()

    def next_slot(self):
        s = self.slot_next
        self.slot_next = (s + 1) % self.NSLOT
        return s

    def ffn(self, wg, wu, wd, gcol, tag):
        nc, tk = self.nc, self.tk
        with ExitStack() as les:
            self.alloc_h(les, tag)
            self.rmsnorm(gcol, les, keep=True)
            GROUPS = [(0, 12), (12, 10)]
            aT = self.sb(f"aT_{tag}", [128, 12, T], BF16, les)
            sg = [self.sb(f"sg{i}_{tag}", [128, 512], BF16, les) for i in range(2)]
            rsg = [tk.R("sg", i) for i in range(2)]
            cnt = 0
            for (f0, nf) in GROUPS:
                slabs = [(f0 + 2 * i) for i in range(nf // 2)]
                loaded = {}

                def load_up(i):
                    fc = slabs[i]
                    s = self.next_slot()
                    vg = self.ring[:, s, 0:2048].rearrange("p (k f) -> p k f", k=KC)
                    vu = self.ring[:, s, 2048:4096].rearrange("p (k f) -> p k f", k=KC)
                    tk.dma(tk.pool, vg, wg[:, fc * 128:(fc + 2) * 128].rearrange("(k p) f -> p k f", p=128),
                           writes=[self.rslot[s]], dsem=self.slot_sem[s])
                    tk.dma(tk.pool, vu, wu[:, fc * 128:(fc + 2) * 128].rearrange("(k p) f -> p k f", p=128),
                           writes=[self.rslot[s]], dsem=self.slot_sem[s])
                    loaded[i] = (s, vg, vu)
                PRE = self.NSLOT - 1
                for i in range(min(PRE, len(slabs))):
                    load_up(i)
                for i in range(len(slabs)):
                    s, vg, vu = loaded[i]
                    for fl in range(2):
                        fcg = slabs[i] - f0 + fl
                        for ti, (t0, n) in enumerate(TILES):
                            pa = (cnt * 2) % 6
                            pbk = pa + 1
                            cnt += 1
                            for kc in range(KC):
                                tk.op(tk.pe, lambda: nc.tensor.matmul(self.ps[pa][:, 0:n], vg[:, kc, fl * 128:(fl + 1) * 128],
                                                                      self.hT[:, kc, t0:t0 + n], start=(kc == 0), stop=(kc == KC - 1)),
                                      reads=[self.rslot[s], self.rh(ti)], writes=[self.rps[pa]], inc=(kc == KC - 1))
                            for kc in range(KC):
                                tk.op(tk.pe, lambda: nc.tensor.matmul(self.ps[pbk][:, 0:n], vu[:, kc, fl * 128:(fl + 1) * 128],
                                                                      self.hT[:, kc, t0:t0 + n], start=(kc == 0), stop=(kc == KC - 1)),
                                      reads=[self.rslot[s], self.rh(ti)], writes=[self.rps[pbk]], inc=(kc == KC - 1))
                            q = cnt % 2
                            tk.op(tk.act, lambda: nc.scalar.activation(out=sg[q][:, 0:n], in_=self.ps[pa][:, 0:n], func=AF.Silu),
                                  reads=[self.rps[pa]], writes=[rsg[q]])
                            tk.op(tk.dve, lambda: nc.vector.tensor_tensor(out=aT[:, fcg, t0:t0 + n], in0=sg[q][:, 0:n],
                                                                          in1=self.ps[pbk][:, 0:n], op=ALU.mult),
                                  reads=[rsg[q], self.rps[pbk]], writes=[tk.R("aT", ti)])
                    if i + PRE < len(slabs):
                        load_up(i + PRE)
                dl = {}

                def load_dn(j):
                    s = self.next_slot()
                    v = self.ring[:, s, 0:nf * 256].rearrange("p (f d) -> p f d", f=nf)
                    tk.dma(tk.pool, v, wd[f0 * 128:(f0 + nf) * 128, j * 256:(j + 1) * 256].rearrange("(f p) d -> p f d", p=128),
                           writes=[self.rslot[s]], dsem=self.slot_sem[s])
                    dl[j] = (s, v)
                for j in range(min(PRE, 4)):
                    load_dn(j)
                for j in range(4):
                    s, v = dl[j]
                    for ti, (t0, n) in enumerate(TILES):
                        for dlc in range(2):
                            dc = j * 2 + dlc
                            pc = 6 + cnt % 2
                            cnt += 1
                            for fcg in range(nf):
                                tk.op(tk.pe, lambda: nc.tensor.matmul(self.ps[pc][:, 0:n], v[:, fcg, dlc * 128:(dlc + 1) * 128],
                                                                      aT[:, fcg, t0:t0 + n], start=(fcg == 0), stop=(fcg == nf - 1)),
                                      reads=[self.rslot[s], tk.R("aT", ti)], writes=[self.rps[pc]], inc=(fcg == nf - 1))
                            tk.op(tk.dve, lambda: nc.vector.scalar_tensor_tensor(out=self.xT[:, dc, t0:t0 + n], in0=self.ps[pc][:, 0:n],
                                                                                 scalar=0.5, in1=self.xT[:, dc, t0:t0 + n],
                                                                                 op0=ALU.mult, op1=ALU.add),
                                  reads=[self.rps[pc], self.rx(ti)], writes=[self.rx(ti)])
                    if j + PRE < 4:
                        load_dn(j + PRE)
            tk.barrier()

    def load_mem(self, les_outer, l=0):
        nc, tk = self.nc, self.tk
        self.memT = self.sb(f"memT{l}", [128, KC, 256], BF16, les_outer)
        with ExitStack() as les:
            stg = [self.sb(f"mstg{l}_{i}", [128, D], F32, les) for i in range(2)]
            rstg = [tk.R("mstg", i) for i in range(2)]
            for mc in range(2):
                tk.dma(tk.sp, stg[mc][:], self.dr["memp"][mc * 128:(mc + 1) * 128, :], writes=[rstg[mc]])
                for half in range(2):
                    pb = mc * 2 + half
                    for q in range(4):
                        kc = half * 4 + q
                        tk.op(tk.pe, lambda: nc.tensor.transpose(self.ps[pb][:, q * 128:(q + 1) * 128],
                                                                stg[mc][:, kc * 128:(kc + 1) * 128], self.ident[:]),
                              reads=[rstg[mc], self.r_const], writes=[self.rps[pb]], inc=(q == 3))
                    src_ps = self.ps[pb][:].rearrange("p (q t) -> p q t", q=4)
                    tk.op(tk.dve, lambda: nc.vector.tensor_copy(self.memT[:, half * 4:half * 4 + 4, mc * 128:(mc + 1) * 128], src_ps),
                          reads=[self.rps[pb]], writes=[tk.R("memT")])
            tk.barrier()

    def load_w512(self, w, c0):
        tk = self.tk
        s = self.next_slot()
        v = self.ring[:, s, :].rearrange("p (k f) -> p k f", k=KC)
        tk.dma(tk.pool, v, w[:, c0:c0 + 512].rearrange("(k p) f -> p k f", p=128), writes=[self.rslot[s]], dsem=self.slot_sem[s])
        return s, v

    def xattn(self, l):
        nc, tk = self.nc, self.tk
        dr = self.dr
        gcol = PFM_IDX["norms"] + (l * 4 + 2) * 8
        SC = 1.0 / 16.0
        with ExitStack() as les:
            self.load_mem(les, l)
            self.alloc_h(les, f"xa{l}")
            self.rmsnorm(gcol, les, keep=True)
            qT = self.sb(f"qT{l}", [128, KC, T], BF16, les)
            KT = self.sb(f"KT{l}", [128, KC, 256], BF16, les)
            Vtm = self.sb(f"Vtm{l}", [128, 2, D], BF16, les)
            kvst = [self.sb(f"kvst{l}_{i}", [128, 512], F32, les) for i in range(2)]
            rkvst = [tk.R("kvst", i) for i in range(2)]
            rKT = tk.R("KT"); rV = tk.R("Vtm")
            cnt = 0
            for wi, (wname, oname) in enumerate((("xa_wk", "mk_p"), ("xa_wv", "mv_p"))):
                w = dr[wname][l]
                for half in range(2):
                    s, v = self.load_w512(w, half * 512)
                    sub = self.cfg.get("xa_sub", "")
                    if wi == 0 and "nokt" not in sub:
                        for q in range(4):
                            ec = half * 4 + q
                            pb = cnt % 4; cnt += 1
                            for kc in range(KC):
                                tk.op(tk.pe, lambda: nc.tensor.matmul(self.ps[pb][:, 0:256], v[:, kc, q * 128:(q + 1) * 128], self.memT[:, kc, :],
                                                                      start=(kc == 0), stop=(kc == KC - 1)),
                                      reads=[self.rslot[s], tk.R("memT")], writes=[self.rps[pb]], inc=(kc == KC - 1))
                            tk.op(tk.act, lambda: nc.scalar.copy(KT[:, ec, :], self.ps[pb][:, 0:256]), reads=[self.rps[pb]], writes=[rKT])
                    for mc in range(2 if "noktm" not in sub else 0):
                        pb = cnt % 4; cnt += 1
                        for kc in range(KC):
                            tk.op(tk.pe, lambda: nc.tensor.matmul(self.ps[pb][:], self.memT[:, kc, mc * 128:(mc + 1) * 128], v[:, kc, :],
                                                                  start=(kc == 0), stop=(kc == KC - 1)),
                                  reads=[self.rslot[s], tk.R("memT")], writes=[self.rps[pb]], inc=(kc == KC - 1))
                        st = cnt % 2
                        tk.op(tk.dve, lambda: nc.vector.tensor_copy(kvst[st][:], self.ps[pb][:]), reads=[self.rps[pb]], writes=[rkvst[st]])
                        if wi == 1 and "novt" not in sub:
                            tk.op(tk.act, lambda: nc.scalar.copy(Vtm[:, mc, half * 512:(half + 1) * 512], kvst[st][:]),
                                  reads=[rkvst[st]], writes=[rV])
                        if "noout" not in sub:
                            self.out_toks.append(tk.dma(tk.sp, dr[oname][l, mc * 128:(mc + 1) * 128, half * 512:(half + 1) * 512], kvst[st][:],
                                                        reads=[rkvst[st]]))
            stage = self.cfg.get("xa_stage", 9)
            if stage < 2:
                tk.barrier(); return
            for half in range(2):
                s, v = self.load_w512(dr["xa_wq"][l], half * 512)
                for q in range(4):
                    ec = half * 4 + q
                    for ti, (t0, n) in enumerate(TILES):
                        pb = cnt % 4; cnt += 1
                        for kc in range(KC):
                            tk.op(tk.pe, lambda: nc.tensor.matmul(self.ps[pb][:, 0:n], v[:, kc, q * 128:(q + 1) * 128], self.hT[:, kc, t0:t0 + n],
                                                                  start=(kc == 0), stop=(kc == KC - 1)),
                                  reads=[self.rslot[s], self.rh(ti)], writes=[self.rps[pb]], inc=(kc == KC - 1))
                        if cnt % 2:
                            tk.op(tk.act, lambda: nc.scalar.copy(qT[:, ec, t0:t0 + n], self.ps[pb][:, 0:n]), reads=[self.rps[pb]], writes=[tk.R("qT", ti)])
                        else:
                            tk.op(tk.dve, lambda: nc.vector.tensor_copy(qT[:, ec, t0:t0 + n], self.ps[pb][:, 0:n]), reads=[self.rps[pb]], writes=[tk.R("qT", ti)])
            if stage < 3:
                tk.barrier(); return
            PT = [[self.sb(f"PT{l}_{i}_{m}", [128, 512], BF16, les) for m in range(2)] for i in range(2)]
            rPT = [[tk.R("PT", i, m) for m in range(2)] for i in range(2)]
            rsum = [self.sb(f"rsum{l}_{i}", [128, 512], F32, les) for i in range(2)]
            rrsum = [tk.R("rsum", i) for i in range(2)]
            it = 0
            for ti in range(4):
                t0, n = TILES[ti]
                for hh in range(4):
                    par = it % 2; it += 1
                    for mc in range(2):
                        pb = par * 2 + mc
                        for e in range(2):
                            tk.op(tk.pe, lambda: nc.tensor.matmul(self.ps[pb][:], KT[:, 2 * hh + e, mc * 128:(mc + 1) * 128], qT[:, 2 * hh + e, t0:t0 + n],
                                                                  start=(e == 0), stop=(e == 1)),
                                  reads=[rKT, tk.R("qT", ti)], writes=[self.rps[pb]], inc=(e == 1))
                        tk.op(tk.act, lambda: nc.scalar.activation(out=PT[par][mc][:], in_=self.ps[pb][:], func=AF.Exp, scale=SC),
                              reads=[self.rps[pb]], writes=[rPT[par][mc]])
                    pbs = 4 + par
                    for mc in range(2):
                        tk.op(tk.pe, lambda: nc.tensor.matmul(self.ps[pbs][:], self.ones_bf[:], PT[par][mc][:], start=(mc == 0), stop=(mc == 1)),
                              reads=[rPT[par][mc], self.r_const], writes=[self.rps[pbs]], inc=(mc == 1))
                    tk.op(tk.act, lambda: nc.scalar.activation(out=rsum[par][:], in_=self.ps[pbs][:], func=AF.Ln), reads=[self.rps[pbs]], writes=[rrsum[par]])
                    tk.op(tk.act, lambda: nc.scalar.activation(out=rsum[par][:], in_=rsum[par][:], func=AF.Exp, scale=-1.0), reads=[], writes=[rrsum[par]])
                    for e in range(2):
                        pbv = 6 + e
                        for mc in range(2):
                            tk.op(tk.pe, lambda: nc.tensor.matmul(self.ps[pbv][:], Vtm[:, mc, (2 * hh + e) * 128:(2 * hh + e + 1) * 128], PT[par][mc][:],
                                                                  start=(mc == 0), stop=(mc == 1)),
                                  reads=[rPT[par][mc], rV], writes=[self.rps[pbv]], inc=(mc == 1))
                        tk.op(tk.dve, lambda: nc.vector.tensor_tensor(out=self.hT[:, 2 * hh + e, t0:t0 + n], in0=self.ps[pbv][:], in1=rsum[par][:], op=ALU.mult),
                              reads=[self.rps[pbv], rrsum[par]], writes=[self.rh(ti)])
            if stage < 4:
                tk.barrier(); return
            do_s = self.cfg.get("xa_sample", True)
            KbT = [self.sb(f"KbT{l}_{i}", [128, KC, 256], BF16, les) for i in range(2)]
            rKbT = [tk.R("KbT", i) for i in range(2)]
            PTs = self.sb(f"PTs{l}", [128, 2, 4, NS], BF16, les)
            rPTs = tk.R("PTs")
            rs_s = self.sb(f"rs_s{l}", [128, 4, NS], F32, les)
            loads = {}

            def load_kv(b):
                s = self.next_slot()
                vk = self.ring[:, s, 0:2048].rearrange("p (m e) -> p m e", m=2)
                vv = self.ring[:, s, 2048:4096].rearrange("p (m e) -> p m e", m=2)
                tk.dma(tk.pool, vk, dr["ck"][l, b].rearrange("(m p) e -> p m e", p=128), writes=[self.rslot[s]], dsem=self.slot_sem[s])
                tk.dma(tk.pool, vv, dr["cv"][l, b].rearrange("(m p) e -> p m e", p=128), writes=[self.rslot[s]], dsem=self.slot_sem[s])
                loads[b] = (s, vk, vv)
            PRE = self.NSLOT - 1
            for b in range(PRE if do_s else 0):
                load_kv(b)
            scv = [self.ps[4 + i][:].rearrange("p (m h t) -> p m h t", m=2, h=4) for i in range(2)]
            pvv = self.ps[6][:].rearrange("p (e t) -> p e t", e=8)
            def xa_T(b):
                s, vk, vv = loads[b]
                par = b % 2
                for mc in range(2):
                    pb = par * 2 + mc
                    pbf = self.ps[pb][:].bitcast(BF16).rearrange("p (e m) -> p e m", e=8)
                    for ec in range(KC):
                        tk.op(tk.pe, lambda: nc.tensor.transpose(pbf[:, ec, :], vk[:, mc, ec * 128:(ec + 1) * 128], self.ident_bf[:]),
                              reads=[self.rslot[s], self.r_const], writes=[self.rps[pb]], inc=(ec == KC - 1))
                    if mc == 0:
                        tk.op(tk.act, lambda: nc.scalar.copy(KbT[par][:, :, mc * 128:(mc + 1) * 128], pbf), reads=[self.rps[pb]], writes=[rKbT[par]])
                    else:
                        tk.op(tk.dve, lambda: nc.vector.tensor_copy(KbT[par][:, :, mc * 128:(mc + 1) * 128], pbf), reads=[self.rps[pb]], writes=[rKbT[par]])

            if do_s:
                xa_T(0)
            for b in range(NB if do_s else 0):
                s, vk, vv = loads[b]
                par = b % 2
                if b + 1 < NB:
                    if b + 1 not in loads:
                        load_kv(b + 1)
                    xa_T(b + 1)
                c0 = SEQ + 4 * b
                for hh in range(4):
                    for mc in range(2):
                        for e in range(2):
                            last = (hh == 3 and mc == 1 and e == 1)
                            tk.op(tk.pe, lambda: nc.tensor.matmul(scv[par][:, mc, hh, 4 * b:4 * b + 4], KbT[par][:, 2 * hh + e, mc * 128:(mc + 1) * 128],
                                                                  qT[:, 2 * hh + e, c0:c0 + 4], start=(e == 0), stop=(e == 1)),
                                  reads=[rKbT[par], tk.R("qT", 4)], writes=[self.rps[4 + par]], inc=last)
                tk.op(tk.act, lambda: nc.scalar.activation(out=PTs[:, :, :, 4 * b:4 * b + 4], in_=scv[par][:, :, :, 4 * b:4 * b + 4], func=AF.Exp, scale=SC),
                      reads=[self.rps[4 + par]], writes=[rPTs])
                for ec in range(KC):
                    for mc in range(2):
                        last = (ec == KC - 1 and mc == 1)
                        tk.op(tk.pe, lambda: nc.tensor.matmul(pvv[:, ec, 4 * b:4 * b + 4], vv[:, mc, ec * 128:(ec + 1) * 128], PTs[:, mc, ec // 2, 4 * b:4 * b + 4],
                                                              start=(mc == 0), stop=(mc == 1)),
                              reads=[self.rslot[s], rPTs], writes=[self.rps[6]], inc=last)
                if b + PRE < NB and (b + PRE) not in loads:
                    load_kv(b + PRE)
            for mc in range(2 if do_s else 0):
                tk.op(tk.pe, lambda: nc.tensor.matmul(self.ps[7][:, 0:256], self.ones_bf[:], PTs[:, mc, :, :], start=(mc == 0), stop=(mc == 1)),
                      reads=[rPTs, self.r_const], writes=[self.rps[7]], inc=(mc == 1))
            if do_s:
                tk.op(tk.dve, lambda: nc.vector.reciprocal(rs_s[:], self.ps[7][:, 0:256].rearrange("p (h t) -> p h t", h=4)),
                      reads=[self.rps[7]], writes=[tk.R("rs_s")])
            for ec in range(KC if do_s else 0):
                tk.op(tk.dve, lambda: nc.vector.tensor_tensor(out=self.hT[:, ec, SEQ:T], in0=pvv[:, ec, :], in1=rs_s[:, ec // 2, :], op=ALU.mult),
                      reads=[self.rps[6], tk.R("rs_s")], writes=[self.rh(4)])
            for half in range(2):
                s, v = self.load_w512(dr["xa_wo"][l], half * 512)
                for q in range(4):
                    dc = half * 4 + q
                    for ti, (t0, n) in enumerate(TILES):
                        pb = cnt % 4; cnt += 1
                        for kc in range(KC):
                            tk.op(tk.pe, lambda: nc.tensor.matmul(self.ps[pb][:, 0:n], v[:, kc, q * 128:(q + 1) * 128], self.hT[:, kc, t0:t0 + n],
                                                                  start=(kc == 0), stop=(kc == KC - 1)),
                                  reads=[self.rslot[s], self.rh(ti)], writes=[self.rps[pb]], inc=(kc == KC - 1))
                        tk.op(tk.dve, lambda: nc.vector.tensor_tensor(out=self.xT[:, dc, t0:t0 + n], in0=self.ps[pb][:, 0:n], in1=self.xT[:, dc, t0:t0 + n], op=ALU.add),
                              reads=[self.rps[pb], self.rx(ti)], writes=[self.rx(ti)])
            tk.barrier()

    def mix1(self):
        nc, tk = self.nc, self.tk
        dr = self.dr
        l = 1
        gcol = PFM_IDX["norms"] + (l * 4 + 1) * 8
        with ExitStack() as les:
            self.alloc_h(les, "m1")
            self.rmsnorm(gcol, les, keep=True)
            uT = self.sb("uT", [128, KC, T], BF16, les)
            lnw = self.sb("lnw_bc", [128, D], F32, les)
            lnb = self.sb("lnb_bc", [128, D], F32, les)
            wsT = self.sb("wsT", [128, 8, 128], BF16, les)
            wsTb = self.sb("wsTb", [64, 8, 64], BF16, les)
            wsTb_f = self.sb("wsTb_f", [64, 8, 64], F32, les)
            wstg = [self.sb(f"wstg{i}", [128, 128], F32, les) for i in range(2)]
            triu = self.sb("triu_sb", [128, 128], F32, les)
            mblk = self.sb("mblk_sb", [64, 64], F32, les)
            bsr = self.sb("bsr", [1, 8, 128], BF16, les)
            bsS_f = self.sb("bsS_f", [1, 8, 64], F32, les)
            bsS = self.sb("bsS", [1, 8, 64], BF16, les)
            onesr = self.sb("onesr", [1, 128], BF16, les)
            rg = tk.R("gm_const")
            tk.dma(tk.sp, lnw[:], dr["gm_ln_w"][0].partition_broadcast(128), writes=[rg])
            tk.dma(tk.sp, lnb[:], dr["gm_ln_b"][0].partition_broadcast(128), writes=[rg])
            tk.dma(tk.sp, triu[:], dr["triu"], writes=[rg])
            tk.dma(tk.sp, mblk[:], dr["mask_blk"], writes=[rg])
            tk.dma(tk.pool, bsr[:], dr["gm_bs"][0:1], writes=[rg])
            tk.op(tk.dve, lambda: nc.vector.memset(onesr[:], 1.0), writes=[rg])
            tk.op(tk.dve, lambda: nc.vector.memset(wsTb_f[:], 0.0), writes=[tk.R("wsTb_f")])
            for b in range(NB):
                tk.dma(tk.sp, wsTb_f[4 * b:4 * b + 4, :, 4 * b:4 * b + 4], dr["gm_ws"][0][:, 0:4, 0:4].rearrange("g t s -> t g s"),
                       writes=[tk.R("wsTb_f")])
                tk.dma(tk.sp, bsS_f[0:1, :, 4 * b:4 * b + 4], dr["gm_bs"][0:1, :, 0:4], writes=[tk.R("bsS_f")])
            for g in range(8):
                pb = g % 4
                tk.op(tk.pe, lambda: nc.tensor.transpose(self.ps[pb][0:64, 0:64], wsTb_f[:, g, :], self.ident[0:64, 0:64]),
                      reads=[tk.R("wsTb_f"), self.r_const], writes=[self.rps[pb]])
                tk.op(tk.dve, lambda: nc.vector.tensor_tensor(out=wsTb[:, g, :], in0=self.ps[pb][0:64, 0:64], in1=mblk[:], op=ALU.mult),
                      reads=[self.rps[pb], rg], writes=[tk.R("wsTb")])
            tk.op(tk.dve, lambda: nc.vector.tensor_copy(bsS[:], bsS_f[:]), reads=[tk.R("bsS_f")], writes=[tk.R("bsS")])
            for g in range(8):
                st = g % 2
                tk.dma(tk.sp, wstg[st][:], dr["gm_ws"][0, g], writes=[tk.R("wstg", st)])
                pb = g % 4
                tk.op(tk.pe, lambda: nc.tensor.transpose(self.ps[pb][:, 0:128], wstg[st][:], self.ident[:]),
                      reads=[tk.R("wstg", st), self.r_const], writes=[self.rps[pb]])
                tk.op(tk.dve, lambda: nc.vector.tensor_tensor(out=wsT[:, g, :], in0=self.ps[pb][:, 0:128], in1=triu[:], op=ALU.mult),
                      reads=[self.rps[pb], rg], writes=[tk.R("wsT")])
            cnt = 0
            for half in range(2):
                s, v = self.load_w512(dr["w_in_odd"][0], half * 512)
                for q in range(4):
                    ec = half * 4 + q
                    for ti, (t0, n) in enumerate(TILES):
                        pb = cnt % 4; cnt += 1
                        for kc in range(KC):
                            tk.op(tk.pe, lambda: nc.tensor.matmul(self.ps[pb][:, 0:n], v[:, kc, q * 128:(q + 1) * 128], self.hT[:, kc, t0:t0 + n],
                                                                  start=(kc == 0), stop=(kc == KC - 1)),
                                  reads=[self.rslot[s], self.rh(ti)], writes=[self.rps[pb]], inc=(kc == KC - 1))
                        tk.op(tk.act, lambda: nc.scalar.activation(out=uT[:, ec, t0:t0 + n], in_=self.ps[pb][:, 0:n], func=AF.Gelu),
                              reads=[self.rps[pb]], writes=[tk.R("uT", ti)])
            sv = [self.load_w512(dr["w_in_odd"][0], D + half * 512) for half in range(2)]
            vg = [self.sb(f"vg{i}", [128, D], F32, les) for i in range(2)]
            vtm = [self.sb(f"vtm{i}", [128, D], BF16, les) for i in range(2)]
            stats = self.sb("gm_stats", [128, 2, 6], F32, les)
            mv = self.sb("gm_mv", [128, 2], F32, les)
            rstd = self.sb("gm_rstd", [128, 1], F32, les)
            blocks = [(b * 128, 128) for b in range(SEQ // 128)] + [(SEQ, NS)]
            def vproj(bi):
                t0, n = blocks[bi]
                ti = min(t0 // 512, 4)
                par = bi % 2
                rvg = tk.R("vg", par)
                for half in range(2):
                    s, v = sv[half]
                    pb = par * 2 + half
                    for kc in range(KC):
                        tk.op(tk.pe, lambda: nc.tensor.matmul(self.ps[pb][0:n, :], self.hT[:, kc, t0:t0 + n], v[:, kc, :],
                                                              start=(kc == 0), stop=(kc == KC - 1)),
                              reads=[self.rslot[s], self.rh(ti)], writes=[self.rps[pb]], inc=(kc == KC - 1))
                    tk.op(tk.act, lambda: nc.scalar.activation(out=vg[par][0:n, half * 512:(half + 1) * 512], in_=self.ps[pb][0:n, :], func=AF.Gelu),
                          reads=[self.rps[pb]], writes=[rvg])
            vproj(0)
            for bi, (t0, n) in enumerate(blocks):
                ti = min(t0 // 512, 4)
                par = bi % 2
                rvg = tk.R("vg", par); rvt = tk.R("vtm", par)
                if bi + 1 < len(blocks):
                    vproj(bi + 1)
                for half in range(2):
                    tk.op(tk.dve, lambda: nc.vector.bn_stats(stats[0:n, half, :], vg[par][0:n, half * 512:(half + 1) * 512]),
                          reads=[rvg], writes=[tk.R("gm_stats")])
                tk.op(tk.dve, lambda: nc.vector.bn_aggr(mv[0:n, :], stats[0:n, :, :]), reads=[tk.R("gm_stats")], writes=[tk.R("gm_mv")])
                tk.op(tk.act, lambda: nc.scalar.activation(out=rstd[0:n, :], in_=mv[0:n, 1:2], func=AF.Sqrt, bias=self.eps_ln[0:n, 0:1]),
                      reads=[tk.R("gm_mv"), self.r_const], writes=[tk.R("gm_rstd")])
                tk.op(tk.dve, lambda: nc.vector.reciprocal(rstd[0:n, :], rstd[0:n, :]), reads=[tk.R("gm_rstd")], writes=[tk.R("gm_rstd")])
                tk.op(tk.dve, lambda: nc.vector.tensor_scalar(vg[par][0:n, :], vg[par][0:n, :], mv[0:n, 0:1], rstd[0:n, 0:1],
                                                              op0=ALU.subtract, op1=ALU.mult),
                      reads=[tk.R("gm_mv"), tk.R("gm_rstd")], writes=[rvg])
                tk.op(tk.pool, lambda: nc.gpsimd.tensor_tensor(out=vg[par][0:n, :], in0=vg[par][0:n, :], in1=lnw[0:n, :], op=ALU.mult),
                      reads=[rg], writes=[rvg])
                if bi < len(blocks) - 1:
                    tk.op(tk.pool, lambda: nc.gpsimd.tensor_tensor(out=vtm[par][0:n, :], in0=vg[par][0:n, :], in1=lnb[0:n, :], op=ALU.add),
                          reads=[rvg, rg], writes=[rvt])
                else:
                    tk.op(tk.pool, lambda: nc.gpsimd.tensor_tensor(out=vg[par][0:n, :], in0=vg[par][0:n, :], in1=lnb[0:n, :], op=ALU.add),
                          reads=[rg], writes=[rvg])
                    tk.op(tk.dve, lambda: nc.vector.tensor_copy(vtm[par][0:n, :], vg[par][0:n, :]), reads=[rvg], writes=[rvt])
                    self.out_toks.append(tk.dma(tk.sp, dr["gv_s"], vg[par][0:n, :], reads=[rvg]))
                for gh in range(2):
                    pb = 4 + gh
                    pv = self.ps[pb][:].rearrange("p (g t) -> p g t", g=4)
                    for gl in range(4):
                        g = gh * 4 + gl
                        if n == 128:
                            w_ap, b_ap = wsT[:, g, :], bsr[0:1, g, :]
                        else:
                            w_ap, b_ap = wsTb[:, g, :], bsS[0:1, g, :]
                        tk.op(tk.pe, lambda: nc.tensor.matmul(pv[:, gl, 0:n], vtm[par][0:n, g * 128:(g + 1) * 128], w_ap, start=True, stop=False),
                              reads=[rvt, tk.R("wsT"), tk.R("wsTb")], writes=[self.rps[pb]], inc=False)
                        tk.op(tk.pe, lambda: nc.tensor.matmul(pv[:, gl, 0:n], onesr[0:1, :], b_ap, start=False, stop=True),
                              reads=[rg, tk.R("bsS")], writes=[self.rps[pb]], inc=(gl == 3))
                    tk.op(tk.dve, lambda: nc.vector.tensor_tensor(out=uT[:, gh * 4:gh * 4 + 4, t0:t0 + n], in0=pv[:, :, 0:n],
                                                                  in1=uT[:, gh * 4:gh * 4 + 4, t0:t0 + n], op=ALU.mult),
                          reads=[self.rps[pb]], writes=[tk.R("uT", ti)])
            for half in range(2):
                s, v = self.load_w512(dr["w_out_odd"][0], half * 512)
                for q in range(4):
                    dc = half * 4 + q
                    for ti, (t0, n) in enumerate(TILES):
                        pb = cnt % 4; cnt += 1
                        for kc in range(KC):
                            tk.op(tk.pe, lambda: nc.tensor.matmul(self.ps[pb][:, 0:n], v[:, kc, q * 128:(q + 1) * 128], uT[:, kc, t0:t0 + n],
                                                                  start=(kc == 0), stop=(kc == KC - 1)),
                                  reads=[self.rslot[s], tk.R("uT", ti)], writes=[self.rps[pb]], inc=(kc == KC - 1))
                        tk.op(tk.dve, lambda: nc.vector.tensor_tensor(out=self.xT[:, dc, t0:t0 + n], in0=self.ps[pb][:, 0:n], in1=self.xT[:, dc, t0:t0 + n], op=ALU.add),
                              reads=[self.rps[pb], self.rx(ti)], writes=[self.rx(ti)])
            tk.barrier()

    def mix0(self):
        nc, tk = self.nc, self.tk
        dr = self.dr
        P = PFM_IDX
        gcol = P["norms"] + 1 * 8
        do_rwkv = self.cfg.get("rwkv", True)
        SEGS = [(i * 256, 256, 1, 256) for i in range(SEQ // 256)] + [(SEQ, NS, NB, 4)]
        with ExitStack() as les:
            sb = lambda name, shape, dt: self.sb("m0_" + name, shape, dt, les)
            R = tk.R
            rc = R("m0c")
            cst = {}
            for nm, shp in (("tri", [128, 128]), ("sut", [128, 128]), ("mask2", [128, 256]), ("maskxt", [128, 128]),
                            ("trib", [128, 128]), ("sutb", [128, 128]), ("mask2b", [128, 256]), ("maskxtb", [128, 128]),
                            ("rowmask", [128, NB]), ("bones64", [128, 128])):
                cst[nm] = sb("c_" + nm, shp, F32)
                tk.dma(tk.sp, cst[nm][:], dr["c_" + nm], writes=[rc])
            bones = sb("c_bones", [128, 128], BF16)
            tk.dma(tk.pool, bones[:], dr["c_bones"], writes=[rc])
            lora12 = sb("lora12", [128, 512], BF16)
            g2s = sb("g2s", [128, 512], BF16)
            tk.dma(tk.pool, lora12[0:64, :], dr["decay_w2"][0], writes=[rc])
            tk.dma(tk.pool, lora12[64:128, :], dr["iclr_a2"][0], writes=[rc])
            tk.dma(tk.pool, g2s[:], dr["gate_g2"][0], writes=[rc])
            negw0 = sb("negw0", [128, 4], F32)
            tk.op(tk.dve, lambda: nc.vector.tensor_scalar(negw0[:], self.pfm[:, P["decay_w0"]:P["decay_w0"] + 4], -1.0, None, op0=ALU.mult),
                  reads=[self.r_const], writes=[rc])
            cone = sb("cone", [128, 1], F32); chalf = sb("chalf", [128, 1], F32); cgn = sb("cgn", [128, 1], F32)
            tk.op(tk.dve, lambda: nc.vector.memset(cone[:], 1.0), writes=[rc])
            tk.op(tk.dve, lambda: nc.vector.memset(chalf[:], -0.5), writes=[rc])
            tk.op(tk.dve, lambda: nc.vector.memset(cgn[:], 64e-5), writes=[rc])
            stT = sb("stT", [128, 14, NB], F32)
            cbT = sb("cbT", [128, 4, NB, 2], F32)
            with ExitStack() as ies:
                stg = self.sb("m0_ststg", [NB, PROJ_A], F32, ies)
                cstg = self.sb("m0_cstg", [2 * NB, 512], F32, ies)
                tk.dma(tk.sp, stg[:], dr["st_shift"], writes=[R("ststg")])
                tk.dma(tk.sp, cstg[:], dr["st_conv"].rearrange("b r f -> (b r) f"), writes=[R("cstg")])
                for c in range(14):
                    pb = c % 4
                    tk.op(tk.pe, lambda: nc.tensor.transpose(self.ps[pb][:, 0:NB], stg[:, c * 128:(c + 1) * 128], self.ident[0:NB, 0:NB]),
                          reads=[R("ststg"), self.r_const], writes=[self.rps[pb]])
                    tk.op(tk.dve, lambda: nc.vector.tensor_copy(stT[:, c, :], self.ps[pb][:, 0:NB]), reads=[self.rps[pb]], writes=[rc])
                for j in range(4):
                    pb = j % 4
                    tk.op(tk.pe, lambda: nc.tensor.transpose(self.ps[pb][:, 0:2 * NB], cstg[:, j * 128:(j + 1) * 128], self.ident[0:2 * NB, 0:2 * NB]),
                          reads=[R("cstg"), self.r_const], writes=[self.rps[pb]])
                    tk.op(tk.dve, lambda: nc.vector.tensor_copy(cbT[:, j, :, :], self.ps[pb][:, 0:2 * NB].rearrange("p (b r) -> p b r", r=2)),
                          reads=[self.rps[pb]], writes=[rc])
                tk.barrier()
            SM = 256
            hseg = sb("hseg", [128, KC, SM], BF16)
            psA = sb("psA", [128, 14, SM], F32)
            pext = [sb(f"pext{i}", [128, SM + 1], F32) for i in range(2)]
            pcar = sb("pcar", [128, 14], F32)
            dtmp = [sb("dtmp0", [128, SM], F32)] * 2
            zext = sb("zext", [128, 4, SM + 2], F32)
            ycv = [sb("ycv0", [128, SM], F32)] * 2
            mT = sb("mT", [128, KC, SM], BF16)
            aT = sb("aT", [128, 4, SM], F32); bT = sb("bT", [128, 4, SM], F32); lwT = sb("lwT", [128, 4, SM], F32)
            aiT = sb("aiT", [128, 1, SM], F32); gT = sb("gT", [128, 4, SM], BF16); bonT = sb("bonT", [128, 4, SM], BF16)
            t12 = sb("t12", [128, SM], BF16); sgd = sb("sgd", [128, SM], BF16)
            tmpA = [sb(f"tmpA{i}", [128, SM], F32) for i in range(2)]
            tmpB = [sb(f"tmpB{i}", [128, SM], BF16) for i in range(2)]
            rmsb = self.rmsnorm_bufs("m0", les, n=SM, nrs=1)
            hcv = sb("hcv", [128, 4, SM], F32); bgv = sb("bgv", [128, 4, SM], F32)
            pshs = sb("pshs", [128, 14, NB], F32); zs = sb("zs", [128, 4, NB, 2], F32)
            tk.op(tk.dve, lambda: nc.vector.memset(pcar[:], 0.0), writes=[R("pcar")])
            tk.op(tk.dve, lambda: nc.vector.memset(zext[:, :, 0:2], 0.0), writes=[R("zext")])
            lw_tm = sb("lw_tm", [128, 512], F32); v_tm = sb("v_tm", [128, 512], BF16)
            d1 = sb("d1", [128, 512], F32)
            Bh = sb("Bh", [128, 512], BF16); Kh = sb("Kh", [128, 512], BF16)
            Zb = [sb("Zb0", [128, 8, 128], BF16)]
            Ef = sb("Ef", [128, 4, 128], F32); Einv = sb("Einv", [128, 4, 128], F32); Epv = sb("Epv", [128, 4, 128], F32)
            AR = sb("AR", [128, 4, 256], BF16); BK = sb("BK", [128, 4, 256], BF16)
            LQ = sb("LQ", [128, 8, 4, 128], BF16)
            Xs = sb("Xs", [128, 8, 128], BF16); XT = sb("XT", [128, 8, 128], BF16)
            Gs = sb("Gs", [128, 4, 64], F32); Rb = sb("Rb", [128, 4, 128], F32)
            Hs = [sb(f"Hs{i}", [128, 4, 64], F32) for i in range(2)]
            ysb = sb("ysb", [128, 4, 128], F32); ycen = sb("ycen", [128, 4, 128], F32); ysq = sb("ysq", [128, 4, 128], BF16)
            bones64b = sb("bones64b", [128, 128], BF16)
            tk.dma(tk.pool, bones64b[:], dr["c_bones64"], writes=[rc])
            tk.op(tk.dve, lambda: nc.vector.memset(Hs[0][:], 0.0), writes=[R("H", 0)])
            hcur = [0]
            wst = d1[0:64, :].rearrange("p (h k) -> p h k", h=8)

            def bank(i):
                return self.ps[i], self.rps[i]

            def chunk(c0, sample):
                cs = slice(c0, c0 + 128)
                rT = psA[:, 0:4, cs]; kT = psA[:, 4:8, cs]; vT = psA[:, 8:12, cs]
                tri, sut = (cst["trib"], cst["sutb"]) if sample else (cst["tri"], cst["sut"])
                mask2, maskxt = (cst["mask2b"], cst["maskxtb"]) if sample else (cst["mask2"], cst["maskxt"])
                rseg = [R("psA"), R("prep")]
                tmb = {}
                for qi, src in enumerate((lwT, aT, bT, None, "v")):
                    p_, rp = bank(qi)
                    tmb[qi] = (p_, rp)
                    for j in range(4):
                        if src is None:
                            in_ap = kT[:, j, :]
                        elif isinstance(src, str):
                            in_ap = vT[:, j, :]
                        else:
                            in_ap = src[:, j, cs]
                        tk.op(tk.pe, lambda: nc.tensor.transpose(p_[:, j * 128:(j + 1) * 128], in_ap, self.ident[:]),
                              reads=rseg + [self.r_const], writes=[rp], inc=(j == 3))
                    if qi == 0:
                        tk.op(tk.dve, lambda: nc.vector.tensor_copy(lw_tm[:], p_[:]), reads=[rp], writes=[R("tm", 0)])
                    if qi == 4:
                        tk.op(tk.act, lambda: nc.scalar.copy(v_tm[:], p_[:]), reads=[rp], writes=[R("v_tm")])
                pc, rpc = bank(5); prc, rprc = bank(6); pct, rpct = bank(7)
                tk.op(tk.pe, lambda: nc.tensor.matmul(pc[:], tri[:], lw_tm[:], start=True, stop=True), reads=[R("tm", 0), rc], writes=[rpc])
                tk.op(tk.pe, lambda: nc.tensor.matmul(prc[:], sut[:], lw_tm[:], start=True, stop=True), reads=[R("tm", 0), rc], writes=[rprc])
                pctv = pct[:].rearrange("p (j t) -> p j t", j=4)
                for j in range(4):
                    tk.op(tk.pe, lambda: nc.tensor.matmul(pctv[:, j, :], lw_tm[:, j * 128:(j + 1) * 128], tri[:], start=True, stop=True),
                          reads=[R("tm", 0), rc], writes=[rpct], inc=(j == 3))
                Z0 = Zb[0]
                tk.op(tk.dve, lambda: nc.vector.tensor_tensor(out=d1[:], in0=pc[:], in1=lw_tm[:], op=ALU.subtract), reads=[rpc, R("tm", 0)], writes=[R("d1")])
                tk.op(tk.act, lambda: nc.scalar.activation(out=d1[:], in_=d1[:], func=AF.Exp), reads=[], writes=[R("d1")])
                tk.op(tk.dve, lambda: nc.vector.tensor_tensor(out=Z0[:, :, 0:64], in0=tmb[1][0][:].rearrange("p (h k) -> p h k", h=8),
                                                              in1=d1[:].rearrange("p (h k) -> p h k", h=8), op=ALU.mult),
                      reads=[tmb[1][1], R("d1")], writes=[R("Z", 0), R("Z", 1)])
                tk.op(tk.act, lambda: nc.scalar.activation(out=d1[:], in_=prc[:], func=AF.Exp), reads=[rprc], writes=[R("d1")])
                tk.op(tk.dve, lambda: nc.vector.tensor_tensor(out=Bh[:], in0=tmb[2][0][:], in1=d1[:], op=ALU.mult), reads=[tmb[2][1], R("d1")], writes=[R("Bh")])
                tk.op(tk.dve, lambda: nc.vector.tensor_tensor(out=Kh[:], in0=tmb[3][0][:], in1=d1[:], op=ALU.mult), reads=[tmb[3][1], R("d1")], writes=[R("Kh")])
                tk.op(tk.dve, lambda: nc.vector.tensor_tensor(out=Epv[:], in0=pctv, in1=lwT[:, :, cs], op=ALU.subtract), reads=[rpct] + rseg, writes=[R("Epv")])
                tk.op(tk.act, lambda: nc.scalar.activation(out=Ef[:], in_=pctv, func=AF.Exp), reads=[rpct], writes=[R("Ef")])
                tk.op(tk.act, lambda: nc.scalar.activation(out=Einv[:], in_=pctv, func=AF.Exp, scale=-1.0), reads=[rpct], writes=[R("Einv")])
                tk.op(tk.act, lambda: nc.scalar.activation(out=Epv[:], in_=Epv[:], func=AF.Exp), reads=[], writes=[R("Epv")])
                tk.op(tk.dve, lambda: nc.vector.tensor_tensor(out=AR[:, :, 0:128], in0=aT[:, :, cs], in1=Epv[:], op=ALU.mult), reads=rseg + [R("Epv")], writes=[R("AR")])
                tk.op(tk.dve, lambda: nc.vector.tensor_tensor(out=AR[:, :, 128:256], in0=rT, in1=Ef[:], op=ALU.mult), reads=rseg + [R("Ef")], writes=[R("AR")])
                tk.op(tk.dve, lambda: nc.vector.tensor_tensor(out=BK[:, :, 0:128], in0=bT[:, :, cs], in1=Einv[:], op=ALU.mult), reads=rseg + [R("Einv")], writes=[R("BK")])
                tk.op(tk.dve, lambda: nc.vector.tensor_tensor(out=BK[:, :, 128:256], in0=kT, in1=Einv[:], op=ALU.mult), reads=rseg + [R("Einv")], writes=[R("BK")])
                for par_ in range(2):
                    pbs = 64 * par_
                    for i_ in range(4):
                        h = 2 * i_ + par_
                        p_, rp = bank(4 * par_ + i_)
                        tk.op(tk.pe, lambda: nc.tensor.matmul(p_[:, 0:256], BK[pbs:pbs + 64, i_, 0:128], AR[pbs:pbs + 64, i_, :], start=True, stop=True),
                              reads=[R("AR"), R("BK")], writes=[rp], inc=False)
                        tk.op(tk.pe, lambda: nc.tensor.matmul(p_[:, 256:512], BK[pbs:pbs + 64, i_, 128:256], AR[pbs:pbs + 64, i_, :], start=True, stop=True),
                              reads=[R("AR"), R("BK")], writes=[rp], inc=True)
                        tk.op(tk.dve, lambda: nc.vector.tensor_tensor(out=LQ[:, h, :, :].rearrange("p (a q) t -> p a (q t)", a=2),
                                                                      in0=p_[:].rearrange("p (a x) -> p a x", a=2),
                                                                      in1=mask2[:, None, :].to_broadcast([128, 2, 256]), op=ALU.mult),
                              reads=[rp, rc], writes=[R("LQ")])
                XTv = XT[:].rearrange("p (i two) t -> p i two t", two=2)
                for par_ in range(2):
                    p_, rp = bank(par_)
                    pv = p_[:].rearrange("p (i t) -> p i t", i=4)
                    pbs = 64 * par_
                    for i_ in range(4):
                        tk.op(tk.pe, lambda: nc.tensor.matmul(pv[:, i_, :], AR[pbs:pbs + 64, i_, 0:128], BK[pbs:pbs + 64, i_, 0:128], start=True, stop=True),
                              reads=[R("AR"), R("BK")], writes=[rp], inc=(i_ == 3))
                    tk.op(tk.dve, lambda: nc.vector.tensor_tensor(out=XTv[:, :, par_, :], in0=pv, in1=maskxt[:, None, :].to_broadcast([128, 4, 128]), op=ALU.mult),
                          reads=[rp, rc], writes=[R("XT")])
                p_, rp = bank(2)
                pv = p_[:].rearrange("p (h t) -> p h t", h=8)
                for h in range(8):
                    tk.op(tk.pe, lambda: nc.tensor.matmul(pv[:, h, :], LQ[:, h, 2, :], v_tm[:, h * 64:(h + 1) * 64], start=True, stop=True),
                          reads=[R("LQ"), R("v_tm")], writes=[rp], inc=(h == 7))
                tk.op(tk.act, lambda: nc.scalar.copy(Z0[:, :, 64:128], pv), reads=[rp], writes=[R("Z", 0), R("Z", 1)])
                nlev = 2 if sample else 7
                for lev in range(nlev):
                    if lev == 0:
                        Xc = LQ[:, :, 0, :]; rX = R("LQ")
                    else:
                        Xc = Xs[:]; rX = R("X")
                    last = (lev == nlev - 1)
                    for half in range(2):
                        p_, rp = bank(4 + half)
                        pv = p_[:].rearrange("p (h t) -> p h t", h=4)
                        for hl in range(4):
                            h = half * 4 + hl
                            tk.op(tk.pe, lambda: nc.tensor.matmul(pv[:, hl, :], Xc[:, h, :], Zb[0][:, h, :], start=True, stop=True),
                                  reads=[rX, R("Z", half)], writes=[rp], inc=(hl == 3))
                    if not last:
                        for half in range(2):
                            p_, rp = bank(half)
                            pv = p_[:].rearrange("p (h t) -> p h t", h=4)
                            for hl in range(4):
                                h = half * 4 + hl
                                tk.op(tk.pe, lambda: nc.tensor.matmul(pv[:, hl, :], XT[:, h, :], Xc[:, h, :], start=True, stop=True),
                                      reads=[rX, R("XT")], writes=[rp], inc=(hl == 3))
                        for half in range(2):
                            p_, rp = bank(2 + half)
                            pv = p_[:].rearrange("p (h t) -> p h t", h=4)
                            for hl in range(4):
                                h = half * 4 + hl
                                tk.op(tk.pe, lambda: nc.tensor.matmul(pv[:, hl, :], Xc[:, h, :], XT[:, h, :], start=True, stop=True),
                                      reads=[rX, R("XT")], writes=[rp], inc=(hl == 3))
                    for half in range(2):
                        p_, rp = bank(4 + half)
                        pv = p_[:].rearrange("p (h t) -> p h t", h=4)
                        tk.op(tk.dve, lambda: nc.vector.tensor_tensor(out=Zb[0][:, half * 4:half * 4 + 4, :], in0=pv, in1=Zb[0][:, half * 4:half * 4 + 4, :], op=ALU.add),
                              reads=[rp], writes=[R("Z", half)])
                    if not last:
                        for half in range(2):
                            p_, rp = bank(half)
                            pv = p_[:].rearrange("p (h t) -> p h t", h=4)
                            tk.op(tk.act, lambda: nc.scalar.copy(Xs[:, half * 4:half * 4 + 4, :], pv), reads=[rp], writes=[R("X")])
                        for half in range(2):
                            p_, rp = bank(2 + half)
                            pv = p_[:].rearrange("p (h t) -> p h t", h=4)
                            if half == 0:
                                tk.op(tk.act, lambda: nc.scalar.copy(XT[:, half * 4:half * 4 + 4, :], pv), reads=[rp], writes=[R("XT")])
                            else:
                                tk.op(tk.dve, lambda: nc.vector.tensor_copy(XT[:, half * 4:half * 4 + 4, :], pv), reads=[rp], writes=[R("XT")])
                Z6 = Zb[0]; rZs = [R("Z", 0), R("Z", 1)]
                p_, rp = bank(0)
                pv = p_[:].rearrange("p (j t) -> p j t", j=4)
                for h in range(8):
                    j = h // 2; pbs = 64 * (h % 2)
                    tk.op(tk.pe, lambda: nc.tensor.matmul(pv[pbs:pbs + 64, j, :], Z6[:, h, 0:64], LQ[:, h, 1, :], start=True, stop=True),
                          reads=rZs + [R("LQ")], writes=[rp], inc=(h == 7))
                tk.op(tk.dve, lambda: nc.vector.tensor_tensor(out=Rb[:], in0=pv, in1=AR[:, :, 128:256], op=ALU.add), reads=[rp, R("AR")], writes=[R("Rb")])
                py, rpy = bank(1)
                pyv = py[:].rearrange("p (j t) -> p j t", j=4)
                for h in range(8):
                    j = h // 2; pbs = 64 * (h % 2)
                    o = pyv[pbs:pbs + 64, j, :]
                    tk.op(tk.pe, lambda: nc.tensor.matmul(o, Z6[:, h, 64:128], LQ[:, h, 1, :], start=True, stop=False), reads=rZs + [R("LQ")], writes=[rpy], inc=False)
                    tk.op(tk.pe, lambda: nc.tensor.matmul(o, v_tm[:, h * 64:(h + 1) * 64], LQ[:, h, 3, :], start=False, stop=True), reads=[R("v_tm"), R("LQ")], writes=[rpy], inc=(h == 7))
                pyB, rpyB = bank(2); pyC, rpyC = bank(3)
                pyBv = pyB[:].rearrange("p (j t) -> p j t", j=4)
                pyCv = pyC[:].rearrange("p (j t) -> p j t", j=4)
                if not sample:
                    p_, rp = bank(4)
                    pv = p_[:, 0:256].rearrange("p (j t) -> p j t", j=4)
                    for h in range(8):
                        j = h // 2; pbs = 64 * (h % 2)
                        tk.op(tk.pe, lambda: nc.tensor.matmul(pv[pbs:pbs + 64, j, :], Z6[:, h, 0:64], Bh[:, h * 64:(h + 1) * 64], start=True, stop=True),
                              reads=rZs + [R("Bh")], writes=[rp], inc=(h == 7))
                    tk.op(tk.act, lambda: nc.scalar.copy(Gs[:], pv), reads=[rp], writes=[R("Gs")])
                    hc = hcur[0]; hn = 1 - hc
                    Hc, Hn = Hs[hc], Hs[hn]
                    for j in range(4):
                        tk.op(tk.pe, lambda: nc.tensor.matmul(pyBv[0:64, j, :], Hc[0:64, j, :], Rb[0:64, j, :], start=True, stop=True),
                              reads=[R("H", hc), R("Rb")], writes=[rpyB], inc=(j == 3))
                    for j in range(4):
                        tk.op(tk.pe, lambda: nc.tensor.matmul(pyCv[64:128, j, :], Hc[64:128, j, :], Rb[64:128, j, :], start=True, stop=True),
                              reads=[R("H", hc), R("Rb")], writes=[rpyC], inc=(j == 3))
                    ph, rph = bank(5); phE, rphE = bank(6); phF, rphF = bank(7)
                    phv = ph[:, 0:256].rearrange("p (j t) -> p j t", j=4)
                    phEv = phE[:, 0:256].rearrange("p (j t) -> p j t", j=4)
                    phFv = phF[:, 0:256].rearrange("p (j t) -> p j t", j=4)
                    for h in range(8):
                        j = h // 2; pbs = 64 * (h % 2)
                        o = phv[pbs:pbs + 64, j, :]
                        tk.op(tk.pe, lambda: nc.tensor.matmul(o, Bh[:, h * 64:(h + 1) * 64], Z6[:, h, 64:128], start=True, stop=False),
                              reads=rZs + [R("Bh")], writes=[rph], inc=False)
                        tk.op(tk.pe, lambda: nc.tensor.matmul(o, Kh[:, h * 64:(h + 1) * 64], v_tm[:, h * 64:(h + 1) * 64], start=False, stop=True),
                              reads=[R("Kh"), R("v_tm")], writes=[rph], inc=(h == 7))
                    for j in range(4):
                        tk.op(tk.pe, lambda: nc.tensor.matmul(phEv[0:64, j, :], Gs[0:64, j, :], Hc[0:64, j, :], start=True, stop=True),
                              reads=[R("Gs"), R("H", hc)], writes=[rphE], inc=(j == 3))
                    for j in range(4):
                        tk.op(tk.pe, lambda: nc.tensor.matmul(phFv[64:128, j, :], Gs[64:128, j, :], Hc[64:128, j, :], start=True, stop=True),
                              reads=[R("Gs"), R("H", hc)], writes=[rphF], inc=(j == 3))
                    for j in range(4):
                        tk.op(tk.dve, lambda: nc.vector.scalar_tensor_tensor(out=Hn[:, j, :], in0=Hc[:, j, :], scalar=Ef[:, j, 127:128], in1=phv[:, j, :],
                                                                             op0=ALU.mult, op1=ALU.add),
                              reads=[R("H", hc), R("Ef"), rph], writes=[R("H", hn)])
                    tk.op(tk.dve, lambda: nc.vector.tensor_tensor(out=Hn[0:64, :, :], in0=Hn[0:64, :, :], in1=phEv[0:64, :, :], op=ALU.add), reads=[rphE], writes=[R("H", hn)])
                    tk.op(tk.dve, lambda: nc.vector.tensor_tensor(out=Hn[64:128, :, :], in0=Hn[64:128, :, :], in1=phFv[64:128, :, :], op=ALU.add), reads=[rphF], writes=[R("H", hn)])
                    hcur[0] = hn
                else:
                    sample_states(Z6, rZs, pyBv, rpyB, pyCv, rpyC)
                ncol = 64 if sample else 128
                tk.op(tk.act, lambda: nc.scalar.copy(ysb[:], pyv), reads=[rpy], writes=[R("ysb")])
                tk.op(tk.dve, lambda: nc.vector.tensor_tensor(out=ysb[0:64, :, 0:ncol], in0=ysb[0:64, :, 0:ncol], in1=pyBv[0:64, :, 0:ncol], op=ALU.add), reads=[rpyB], writes=[R("ysb")])
                tk.op(tk.dve, lambda: nc.vector.tensor_tensor(out=ysb[64:128, :, 0:ncol], in0=ysb[64:128, :, 0:ncol], in1=pyCv[64:128, :, 0:ncol], op=ALU.add), reads=[rpyC], writes=[R("ysb")])
                pm, rpm = bank(0)
                pmv = pm[:].rearrange("p (j t) -> p j t", j=4)
                tk.op(tk.pe, lambda: nc.tensor.matmul(pm[:], cst["bones64"][:], ysb[:].rearrange("p j t -> p (j t)"), start=True, stop=True), reads=[R("ysb"), rc], writes=[rpm])
                tk.op(tk.dve, lambda: nc.vector.tensor_tensor(out=ycen[:], in0=ysb[:], in1=pmv, op=ALU.subtract), reads=[R("ysb"), rpm], writes=[R("ycen")])
                tk.op(tk.act, lambda: nc.scalar.activation(out=ysq[:], in_=ycen[:], func=AF.Square), reads=[R("ycen")], writes=[R("ysq")])
                pq, rpq = bank(1)
                pqv = pq[:].rearrange("p (j t) -> p j t", j=4)
                tk.op(tk.pe, lambda: nc.tensor.matmul(pq[:], bones64b[:], ysq[:].rearrange("p j t -> p (j t)"), start=True, stop=True), reads=[R("ysq"), rc], writes=[rpq])
                rstd = ysb
                tk.op(tk.act, lambda: nc.scalar.activation(out=rstd[:], in_=pqv, func=AF.Ln, bias=cgn[:, 0:1]), reads=[rpq, rc], writes=[R("ysb")])
                tk.op(tk.act, lambda: nc.scalar.activation(out=rstd[:], in_=rstd[:], func=AF.Exp, scale=-0.5), reads=[], writes=[R("ysb")])
                tk.op(tk.dve, lambda: nc.vector.tensor_tensor(out=ycen[:], in0=ycen[:], in1=rstd[:], op=ALU.mult), reads=[R("ysb")], writes=[R("ycen")])
                for j in range(4):
                    tk.op(tk.dve, lambda: nc.vector.tensor_scalar(ycen[:, j, :], ycen[:, j, :], self.pfm[:, P["lnx_w"] + j:P["lnx_w"] + j + 1],
                                                                  self.pfm[:, P["lnx_b"] + j:P["lnx_b"] + j + 1], op0=ALU.mult, op1=ALU.add),
                          reads=[self.r_const], writes=[R("ycen")])
                tk.op(tk.dve, lambda: nc.vector.tensor_tensor(out=ycen[:], in0=ycen[:], in1=bonT[:, :, cs], op=ALU.add), reads=rseg, writes=[R("ycen")])
                tk.op(tk.dve, lambda: nc.vector.tensor_tensor(out=mT[:, 0:4, cs], in0=ycen[:], in1=gT[:, :, cs], op=ALU.mult), reads=rseg + [R("ycen")], writes=[R("mT")])

            def sample_states(Z6, rZs, pyBv, rpyB, pyCv, rpyC):
                tk.barrier()
                S0 = [hcv[0:64, 2 * i:2 * i + 2, :].rearrange("p c (a k) -> p (c a) k", a=4) for i in range(2)]
                H0 = [bgv[:, i, :].rearrange("p (j v) -> p j v", j=4) for i in range(2)]
                Bm = zext[:, 0, 0:256].bitcast(BF16)
                Km = zext[:, 1, 0:256].bitcast(BF16)
                Gb = ycv[0][:, 0:256].rearrange("p (j v) -> p j v", j=4)
                grow = bgv[0:64, 2:4, :].rearrange("p c k -> p (c k)")
                So = wst
                for b in range(NB):
                    par = b % 2
                    tk.dma(tk.sp, S0[par], dr["st_wkv"][b].rearrange("h v k -> v h k"), writes=[R("S0", par)])
                    pt, rpt = bank(4)
                    ptv = pt[:, 0:256].rearrange("p (j v) -> p j v", j=4)
                    for j in range(4):
                        tk.op(tk.pe, lambda: nc.tensor.transpose(ptv[:, j, :], S0[par][:, 2 * j:2 * j + 2, :], self.ident[0:64, 0:64]),
                              reads=[R("S0", par), self.r_const], writes=[rpt], inc=(j == 3))
                    tk.op(tk.act, lambda: nc.scalar.copy(H0[par], ptv), reads=[rpt], writes=[R("H0", par)])
                    for j in range(4):
                        tk.op(tk.pe, lambda: nc.tensor.matmul(pyBv[0:64, j, 4 * b:4 * b + 4], H0[par][0:64, j, :], Rb[0:64, j, 4 * b:4 * b + 4], start=True, stop=True),
                              reads=[R("H0", par), R("Rb")], writes=[rpyB], inc=(j == 3))
                    for j in range(4):
                        tk.op(tk.pe, lambda: nc.tensor.matmul(pyCv[64:128, j, 4 * b:4 * b + 4], H0[par][64:128, j, :], Rb[64:128, j, 4 * b:4 * b + 4], start=True, stop=True),
                              reads=[R("H0", par), R("Rb")], writes=[rpyC], inc=(j == 3))
                    tk.op(tk.dve, lambda: nc.vector.tensor_scalar(Bm, Bh[:], cst["rowmask"][:, b:b + 1], None, op0=ALU.mult), reads=[R("Bh"), rc], writes=[R("Bm")])
                    tk.op(tk.dve, lambda: nc.vector.tensor_scalar(Km, Kh[:], cst["rowmask"][:, b:b + 1], None, op0=ALU.mult), reads=[R("Kh"), rc], writes=[R("Km")])
                    pg, rpg = bank(5)
                    pgv = pg[:, 0:256].rearrange("p (j t) -> p j t", j=4)
                    for h in range(8):
                        j = h // 2; pbs = 64 * (h % 2)
                        tk.op(tk.pe, lambda: nc.tensor.matmul(pgv[pbs:pbs + 64, j, :], Z6[:, h, 0:64], Bm[:, h * 64:(h + 1) * 64], start=True, stop=True),
                              reads=rZs + [R("Bm")], writes=[rpg], inc=(h == 7))
                    tk.op(tk.act, lambda: nc.scalar.copy(Gb, pgv), reads=[rpg], writes=[R("Gb")])
                    pgm, rpgm = bank(6)
                    tk.op(tk.pe, lambda: nc.tensor.matmul(pgm[0:64, :], cst["rowmask"][:, b:b + 1].to_broadcast([128, 64]), lw_tm[:], start=True, stop=True), reads=[R("tm", 0), rc], writes=[rpgm])
                    tk.op(tk.act, lambda: nc.scalar.activation(out=grow, in_=pgm[0:64, :], func=AF.Exp), reads=[rpgm], writes=[R("grow")])
                    pn, rpn = bank(7); pnE, rpnE = bank(0); pnF, rpnF = bank(4)
                    pnv = pn[0:64, :].rearrange("p (h k) -> p h k", h=8)
                    pnEv = pnE[0:64, :].rearrange("p (i two k) -> p i two k", two=2, k=64)
                    pnFv = pnF[0:64, :].rearrange("p (i two k) -> p i two k", two=2, k=64)
                    for h in range(8):
                        o = pnv[:, h, :]
                        tk.op(tk.pe, lambda: nc.tensor.matmul(o, Z6[:, h, 64:128], Bm[:, h * 64:(h + 1) * 64], start=True, stop=False), reads=rZs + [R("Bm")], writes=[rpn], inc=False)
                        tk.op(tk.pe, lambda: nc.tensor.matmul(o, v_tm[:, h * 64:(h + 1) * 64], Km[:, h * 64:(h + 1) * 64], start=False, stop=True), reads=[R("v_tm"), R("Km")], writes=[rpn], inc=(h == 7))
                    for j in range(4):
                        tk.op(tk.pe, lambda: nc.tensor.matmul(pnEv[:, j, 0, :], H0[par][0:64, j, :], Gb[0:64, j, :], start=True, stop=True), reads=[R("H0", par), R("Gb")], writes=[rpnE], inc=(j == 3))
                    for j in range(4):
                        tk.op(tk.pe, lambda: nc.tensor.matmul(pnFv[:, j, 1, :], H0[par][64:128, j, :], Gb[64:128, j, :], start=True, stop=True), reads=[R("H0", par), R("Gb")], writes=[rpnF], inc=(j == 3))
                    Sov = So.rearrange("p (i two) k -> p i two k", two=2)
                    tk.op(tk.dve, lambda: nc.vector.tensor_tensor(out=So, in0=S0[par], in1=grow.rearrange("p (h k) -> p h k", h=8), op=ALU.mult),
                          reads=[R("S0", par), R("grow")], writes=[R("d1")])
                    tk.op(tk.dve, lambda: nc.vector.tensor_tensor(out=So, in0=So, in1=pnv, op=ALU.add), reads=[rpn], writes=[R("d1")])
                    tk.op(tk.dve, lambda: nc.vector.tensor_tensor(out=Sov[:, :, 0, :], in0=Sov[:, :, 0, :], in1=pnEv[:, :, 0, :], op=ALU.add), reads=[rpnE], writes=[R("d1")])
                    tk.op(tk.dve, lambda: nc.vector.tensor_tensor(out=Sov[:, :, 1, :], in0=Sov[:, :, 1, :], in1=pnFv[:, :, 1, :], op=ALU.add), reads=[rpnF], writes=[R("d1")])
                    self.out_toks.append(tk.dma(tk.sp, dr["wkv_s"][b].rearrange("h v k -> v h k"), So, reads=[R("d1")]))
                tk.barrier()

            wo_slots = []
            for si, (t0, S, nb, tl) in enumerate(SEGS):
                sample = nb > 1
                ti = min(t0 // 512, 4)
                if si == 0:
                    self.rmsnorm_seg(t0, S, ti, gcol, rmsb, hseg)
                col_slabs = [(c0, min(512, 3328 - c0)) for c0 in range(0, 3328, 512)]
                pp = 0
                for (w0c, wn) in col_slabs:
                    s_ = self.next_slot()
                    v = self.ring[:, s_, 0:KC * wn].rearrange("p (k f) -> p k f", k=KC)
                    tk.dma(tk.pool, v, dr["w_in_even"][0][:, w0c:w0c + wn].rearrange("(k p) f -> p k f", p=128), writes=[self.rslot[s_]], dsem=self.slot_sem[s_])
                    for q in range(wn // 128):
                        c = w0c // 128 + q
                        if c < 14 or c >= 22:
                            pb = pp % 3; pp += 1
                        elif c < 18:
                            pb = 3
                        else:
                            pb = 4 + (c % 2)
                        p_, rp = bank(pb)
                        for kc in range(KC):
                            tk.op(tk.pe, lambda: nc.tensor.matmul(p_[:, 0:S], v[:, kc, q * 128:(q + 1) * 128], hseg[:, kc, 0:S], start=(kc == 0), stop=(kc == KC - 1)),
                                  reads=[self.rslot[s_], R("hseg")], writes=[rp], inc=(kc == KC - 1))
                        if c < 14:
                            pe_ = pext[c % 2]; rpe = R("pext", c % 2)
                            pv3 = pe_[:, 0:nb * (tl + 1)].rearrange("p (b t) -> p b t", b=nb)
                            if sample:
                                tk.op(tk.dve, lambda: nc.vector.tensor_copy(pv3[:, :, 0], stT[:, c, :]), reads=[rc], writes=[rpe])
                            else:
                                tk.op(tk.dve, lambda: nc.vector.tensor_copy(pv3[:, :, 0], pcar[:, c:c + 1]), reads=[R("pcar")], writes=[rpe])
                            tk.op(tk.act, lambda: nc.scalar.copy(pv3[:, :, 1:tl + 1], p_[:, 0:S].rearrange("p (b t) -> p b t", b=nb)), reads=[rp], writes=[rpe])
                            dt_ = dtmp[c % 2]; rdt = R("dtmp", 0)
                            dv3 = dt_[:, 0:S].rearrange("p (b t) -> p b t", b=nb)
                            tk.op(tk.dve, lambda: nc.vector.tensor_tensor(out=dv3, in0=pv3[:, :, 0:tl], in1=pv3[:, :, 1:tl + 1], op=ALU.subtract), reads=[rpe], writes=[rdt])
                            tk.op(tk.dve, lambda: nc.vector.scalar_tensor_tensor(out=psA[:, c, 0:S].rearrange("p (b t) -> p b t", b=nb), in0=dv3,
                                                                                 scalar=self.pfm[:, P["shift_mu"] + c:P["shift_mu"] + c + 1],
                                                                                 in1=pv3[:, :, 1:tl + 1], op0=ALU.mult, op1=ALU.add),
                                  reads=[rdt, rpe, self.r_const], writes=[R("psA")])
                            if sample:
                                tk.op(tk.act, lambda: nc.scalar.copy(pshs[:, c, :], pv3[:, :, tl]), reads=[rpe], writes=[R("pshs")])
                            else:
                                tk.op(tk.act, lambda: nc.scalar.copy(pcar[:, c:c + 1], pe_[:, S:S + 1]), reads=[rpe], writes=[R("pcar")])
                        elif c < 18:
                            tk.op(tk.act, lambda: nc.scalar.copy(hcv[:, c - 14, 0:S], p_[:, 0:S]), reads=[rp], writes=[R("hcv")])
                        elif c < 22:
                            tk.op(tk.act, lambda: nc.scalar.copy(bgv[:, c - 18, 0:S], p_[:, 0:S]), reads=[rp], writes=[R("bgv")])
                        else:
                            j = c - 22
                            zv = zext[:, j, 0:nb * (tl + 2)].rearrange("p (b t) -> p b t", b=nb)
                            if sample:
                                tk.op(tk.dve, lambda: nc.vector.tensor_copy(zv[:, :, 0:2], cbT[:, j, :, :]), reads=[rc], writes=[R("zext")])
                            tk.op(tk.dve, lambda: nc.vector.tensor_tensor(out=zv[:, :, 2:tl + 2], in0=p_[:, 0:S].rearrange("p (b t) -> p b t", b=nb),
                                                                          in1=hcv[:, j, 0:S].rearrange("p (b t) -> p b t", b=nb), op=ALU.mult),
                                  reads=[rp, R("hcv")], writes=[R("zext")])
                            y_ = ycv[j % 2]; ry = R("ycv", 0)
                            yv = y_[:, 0:S].rearrange("p (b t) -> p b t", b=nb)
                            tk.op(tk.act, lambda: nc.scalar.activation(out=yv, in_=zv[:, :, 0:tl], func=AF.Copy, scale=self.pfm[:, P["cw0"] + j:P["cw0"] + j + 1]),
                                  reads=[R("zext"), self.r_const], writes=[ry])
                            tk.op(tk.dve, lambda: nc.vector.scalar_tensor_tensor(out=yv, in0=zv[:, :, 1:tl + 1], scalar=self.pfm[:, P["cw1"] + j:P["cw1"] + j + 1],
                                                                                 in1=yv, op0=ALU.mult, op1=ALU.add), reads=[R("zext"), self.r_const], writes=[ry])
                            tk.op(tk.dve, lambda: nc.vector.scalar_tensor_tensor(out=yv, in0=zv[:, :, 2:tl + 2], scalar=self.pfm[:, P["cw2"] + j:P["cw2"] + j + 1],
                                                                                 in1=yv, op0=ALU.mult, op1=ALU.add), reads=[R("zext"), self.r_const], writes=[ry])
                            tk.op(tk.dve, lambda: nc.vector.tensor_tensor(out=mT[:, 4 + j, 0:S], in0=y_[:, 0:S], in1=bgv[:, j, 0:S], op=ALU.mult),
                                  reads=[ry, R("bgv")], writes=[R("mT")])
                            if sample:
                                tk.op(tk.act, lambda: nc.scalar.copy(zs[:, j, :, :], zv[:, :, tl:tl + 2]), reads=[R("zext")], writes=[R("zs")])
                            else:
                                tk.op(tk.act, lambda: nc.scalar.copy(zext[:, j, 0:2], zext[:, j, S:S + 2]), reads=[], writes=[R("zext")])
                if do_rwkv and sample:
                    tk.op(tk.dve, lambda: nc.vector.memset(psA[:, 0:12, 64:128], 0.0), writes=[R("psA")])
                    for t_ in (aT, bT, lwT):
                        tk.op(tk.dve, lambda: nc.vector.memset(t_[:, :, 64:128], 0.0), writes=[R("prep")])
                if do_rwkv:
                    self.rwkv_prep(S, psA, aT, bT, lwT, aiT, gT, bonT, t12, sgd, tmpA, tmpB, lora12, g2s, negw0, cone, chalf, bones, rc)
                    for c0 in range(0, S, 128):
                        chunk(c0, sample)
                else:
                    tk.op(tk.dve, lambda: nc.vector.memset(mT[:, 0:4, 0:S], 0.0), writes=[R("mT")])
                if si + 1 < len(SEGS):
                    nt0, nS, _, _ = SEGS[si + 1]
                    self.rmsnorm_seg(nt0, nS, min(nt0 // 512, 4), gcol, rmsb, hseg)
                for half in range(2):
                    s_, v = self.load_w512(dr["w_out_even"][0], half * 512)
                    for q in range(4):
                        dc = half * 4 + q
                        p_, rp = bank(5 + (q % 2))
                        for kc in range(KC):
                            tk.op(tk.pe, lambda: nc.tensor.matmul(p_[:, 0:S], v[:, kc, q * 128:(q + 1) * 128], mT[:, kc, 0:S], start=(kc == 0), stop=(kc == KC - 1)),
                                  reads=[self.rslot[s_], R("mT")], writes=[rp], inc=(kc == KC - 1))
                        tk.op(tk.dve, lambda: nc.vector.tensor_tensor(out=self.xT[:, dc, t0:t0 + S], in0=p_[:, 0:S], in1=self.xT[:, dc, t0:t0 + S], op=ALU.add),
                              reads=[rp, self.rx(ti)], writes=[self.rx(ti)])
                if si == len(SEGS) - 2:
                    self.mix0_prompt_outputs(pcar, zext, Hs[hcur[0]], R("H", hcur[0]), wst, (dtmp[0], hcv))
            self.mix0_sample_outputs(pshs, zs, psA, hcv)
            tk.barrier()

    def rmsnorm_seg(self, t0, S, ti, gcol, b, hseg):
        nc, tk = self.nc, self.tk
        sq, rsq, sd, rs, rsd, rrs = b["sq"], b["rsq"], b["sd"], b["rs"], b["rsd"], b["rrs"]
        p_, rp = self.ps[7], self.rps[7]
        for kc in range(KC):
            s = kc % 2
            tk.op(tk.act, lambda: nc.scalar.activation(out=sq[s][:, 0:S], in_=self.xT[:, kc, t0:t0 + S], func=AF.Square),
                  reads=[self.rx(ti)], writes=[rsq[s]])
            tk.op(tk.pe, lambda: nc.tensor.matmul(p_[:, 0:S], self.onesm[:], sq[s][:, 0:S], start=(kc == 0), stop=(kc == KC - 1)),
                  reads=[rsq[s], self.r_const], writes=[rp], inc=True)
        tk.op(tk.act, lambda: nc.scalar.activation(out=sd[:, 0:S], in_=p_[:, 0:S], func=AF.Ln, bias=self.eps_t[:, 0:1]),
              reads=[rp, self.r_const], writes=[rsd])
        tk.op(tk.act, lambda: nc.scalar.activation(out=rs[0][:, 0:S], in_=sd[:, 0:S], func=AF.Exp, scale=-0.5), reads=[rsd], writes=[rrs[0]])
        for kc in range(KC):
            tk.op(tk.dve, lambda: nc.vector.scalar_tensor_tensor(out=hseg[:, kc, 0:S], in0=self.xT[:, kc, t0:t0 + S],
                                                                 scalar=self.pfm[:, gcol + kc:gcol + kc + 1],
                                                                 in1=rs[0][:, 0:S], op0=ALU.mult, op1=ALU.mult),
                  reads=[self.rx(ti), rrs[0], self.r_const], writes=[tk.R("hseg")])

    def rwkv_prep(self, S, psA, aT, bT, lwT, aiT, gT, bonT, t12, sgd, tmpA, tmpB, lora12, g2s, negw0, cone, chalf, bones, rc):
        nc, tk = self.nc, self.tk
        R = tk.R; P = PFM_IDX
        rA = R("psA"); rp_ = R("prep")
        pf = self.pfm
        tk.op(tk.act, lambda: nc.scalar.activation(out=t12[0:64, 0:S], in_=psA[0:64, 12, 0:S], func=AF.Tanh), reads=[rA], writes=[R("t12")])
        tk.op(tk.act, lambda: nc.scalar.copy(t12[64:128, 0:S], psA[64:128, 12, 0:S]), reads=[rA], writes=[R("t12")])
        tk.op(tk.act, lambda: nc.scalar.activation(out=sgd[:, 0:S], in_=psA[:, 13, 0:S], func=AF.Sigmoid), reads=[rA], writes=[R("sgd")])
        t0_, t1_ = tmpA
        b0_, b1_ = tmpB
        for j in range(4):
            js = slice(j * 128, (j + 1) * 128)
            pw, rpw = self.ps[0], self.rps[0]
            pa, rpa = self.ps[1], self.rps[1]
            pg, rpg = self.ps[2], self.rps[2]
            pss, rpss = self.ps[3], self.rps[3]
            pbn, rpbn = self.ps[4], self.rps[4]
            tk.op(tk.pe, lambda: nc.tensor.matmul(pw[:, 0:S], lora12[0:64, js], t12[0:64, 0:S], start=True, stop=True), reads=[R("t12"), rc], writes=[rpw])
            tk.op(tk.pe, lambda: nc.tensor.matmul(pa[:, 0:S], lora12[64:128, js], t12[64:128, 0:S], start=True, stop=True), reads=[R("t12"), rc], writes=[rpa])
            tk.op(tk.pe, lambda: nc.tensor.matmul(pg[:, 0:S], g2s[:, js], sgd[:, 0:S], start=True, stop=True), reads=[R("sgd"), rc], writes=[rpg])
            tk.op(tk.act, lambda: nc.scalar.activation(out=lwT[:, j, 0:S], in_=pw[:, 0:S], func=AF.Sigmoid, bias=pf[:, P["decay_w0"] + j:P["decay_w0"] + j + 1]),
                  reads=[rpw, self.r_const], writes=[rp_])
            tk.op(tk.dve, lambda: nc.vector.tensor_scalar(lwT[:, j, 0:S], lwT[:, j, 0:S], -math.exp(-0.5), None, op0=ALU.mult), reads=[], writes=[rp_])
            tk.op(tk.act, lambda: nc.scalar.activation(out=aiT[:, 0, 0:S], in_=pa[:, 0:S], func=AF.Sigmoid, bias=pf[:, P["iclr_a0"] + j:P["iclr_a0"] + j + 1]),
                  reads=[rpa, self.r_const], writes=[R("aiT")])
            tk.op(tk.act, lambda: nc.scalar.copy(gT[:, j, 0:S], pg[:, 0:S]), reads=[rpg], writes=[rp_])
            kj = psA[:, 4 + j, 0:S]
            tk.op(tk.dve, lambda: nc.vector.tensor_scalar(t1_[:, 0:S], kj, pf[:, P["k_k"] + j:P["k_k"] + j + 1], None, op0=ALU.mult), reads=[rA, self.r_const], writes=[R("tA", 1)])
            tk.op(tk.act, lambda: nc.scalar.activation(out=b0_[:, 0:S], in_=t1_[:, 0:S], func=AF.Square), reads=[R("tA", 1)], writes=[R("tB", 0)])
            tk.op(tk.pe, lambda: nc.tensor.matmul(pss[:, 0:S], bones[:], b0_[:, 0:S], start=True, stop=True), reads=[R("tB", 0), rc], writes=[rpss])
            tk.op(tk.dve, lambda: nc.vector.tensor_scalar(t0_[:, 0:S], pss[:, 0:S], 1e-18, None, op0=ALU.max), reads=[rpss], writes=[R("tA", 0)])
            tk.op(tk.act, lambda: nc.scalar.activation(out=t0_[:, 0:S], in_=t0_[:, 0:S], func=AF.Ln), reads=[], writes=[R("tA", 0)])
            tk.op(tk.act, lambda: nc.scalar.activation(out=t0_[:, 0:S], in_=t0_[:, 0:S], func=AF.Exp, scale=-0.5), reads=[], writes=[R("tA", 0)])
            tk.op(tk.dve, lambda: nc.vector.tensor_tensor(out=t1_[:, 0:S], in0=t1_[:, 0:S], in1=t0_[:, 0:S], op=ALU.mult), reads=[R("tA", 0)], writes=[R("tA", 1)])
            tk.op(tk.dve, lambda: nc.vector.tensor_scalar(aT[:, j, 0:S], t1_[:, 0:S], -1.0, None, op0=ALU.mult), reads=[R("tA", 1)], writes=[rp_])
            tk.op(tk.dve, lambda: nc.vector.tensor_tensor(out=bT[:, j, 0:S], in0=t1_[:, 0:S], in1=aiT[:, 0, 0:S], op=ALU.mult), reads=[R("tA", 1), R("aiT")], writes=[rp_])
            tk.op(tk.dve, lambda: nc.vector.tensor_scalar(t0_[:, 0:S], aiT[:, 0, 0:S], -1.0, None, op0=ALU.add), reads=[R("aiT")], writes=[R("tA", 0)])
            tk.op(tk.dve, lambda: nc.vector.tensor_scalar(t0_[:, 0:S], t0_[:, 0:S], pf[:, P["k_a"] + j:P["k_a"] + j + 1], None, op0=ALU.mult), reads=[self.r_const], writes=[R("tA", 0)])
            tk.op(tk.dve, lambda: nc.vector.scalar_tensor_tensor(out=kj, in0=t0_[:, 0:S], scalar=1.0, in1=kj, op0=ALU.add, op1=ALU.mult), reads=[R("tA", 0)], writes=[rA])
            tk.op(tk.dve, lambda: nc.vector.scalar_tensor_tensor(out=b1_[:, 0:S], in0=psA[:, j, 0:S], scalar=pf[:, P["r_k"] + j:P["r_k"] + j + 1], in1=kj, op0=ALU.mult, op1=ALU.mult),
                  reads=[rA, self.r_const], writes=[R("tB", 1)])
            tk.op(tk.pe, lambda: nc.tensor.matmul(pbn[:, 0:S], bones[:], b1_[:, 0:S], start=True, stop=True), reads=[R("tB", 1), rc], writes=[rpbn])
            tk.op(tk.dve, lambda: nc.vector.tensor_tensor(out=bonT[:, j, 0:S], in0=pbn[:, 0:S], in1=psA[:, 8 + j, 0:S], op=ALU.mult), reads=[rpbn, rA], writes=[rp_])

    def mix0_prompt_outputs(self, pcar, zext, Hfin, rH, wst, les):
        nc, tk = self.nc, self.tk
        dr = self.dr
        R = tk.R
        dtmp0, hcv = les
        stg14 = dtmp0[0:14, 0:128]
        stg2 = hcv[0:2, :, 0:128]
        p_, rp = self.ps[0], self.rps[0]
        tk.op(tk.pe, lambda: nc.tensor.transpose(p_[0:14, 0:128], pcar[:, 0:14], self.ident[:]), reads=[R("pcar"), self.r_const], writes=[rp])
        tk.op(tk.dve, lambda: nc.vector.tensor_copy(stg14, p_[0:14, 0:128]), reads=[rp], writes=[R("dtmp", 0)])
        self.out_toks.append(tk.dma(tk.sp, dr["sh_p"].rearrange("o (c p) -> (o c) p", p=128), stg14, reads=[R("dtmp", 0)]))
        p_, rp = self.ps[1], self.rps[1]
        for j in range(4):
            tk.op(tk.pe, lambda: nc.tensor.transpose(p_[0:2, j * 128:(j + 1) * 128], zext[:, j, 0:2], self.ident[:]), reads=[R("zext"), self.r_const], writes=[rp], inc=(j == 3))
        tk.op(tk.dve, lambda: nc.vector.tensor_copy(stg2, p_[0:2, :].rearrange("p (q f) -> p q f", q=4)), reads=[rp], writes=[R("hcv")])
        self.out_toks.append(tk.dma(tk.sp, dr["conv_p"].rearrange("r (q f) -> r q f", q=4), stg2, reads=[R("hcv")]))
        p_, rp = self.ps[2], self.rps[2]
        for j in range(4):
            tk.op(tk.pe, lambda: nc.tensor.transpose(p_[0:64, j * 128:(j + 1) * 128], Hfin[:, j, :], self.ident[:]), reads=[rH, self.r_const], writes=[rp], inc=(j == 3))
        tk.op(tk.dve, lambda: nc.vector.tensor_copy(wst, p_[0:64, :].rearrange("p (h k) -> p h k", h=8)), reads=[rp], writes=[R("d1")])
        self.out_toks.append(tk.dma(tk.sp, dr["wkv_p"].rearrange("h v k -> v h k"), wst, reads=[R("d1")]))

    def mix0_sample_outputs(self, pshs, zs, psA, hcv):
        nc, tk = self.nc, self.tk
        dr = self.dr
        R = tk.R
        stg = psA[0:NB, :, 0:128]
        cstg = hcv[0:2 * NB, :, 0:128]
        for c in range(14):
            bk = c // 4
            p_, rp = self.ps[bk], self.rps[bk]
            last = (c % 4 == 3) or c == 13
            tk.op(tk.pe, lambda: nc.tensor.transpose(p_[0:NB, (c % 4) * 128:(c % 4 + 1) * 128], pshs[:, c, :], self.ident[:]), reads=[R("pshs"), self.r_const], writes=[rp], inc=last)
            if last:
                nq = c % 4 + 1
                tk.op(tk.dve, lambda: nc.vector.tensor_copy(stg[:, bk * 4:bk * 4 + nq, :], p_[0:NB, 0:nq * 128].rearrange("p (q f) -> p q f", q=nq)),
                      reads=[rp], writes=[R("psA")])
        self.out_toks.append(tk.dma(tk.sp, dr["sh_s"].rearrange("b (c f) -> b c f", c=14), stg, reads=[R("psA")]))
        p_, rp = self.ps[4], self.rps[4]
        for j in range(4):
            tk.op(tk.pe, lambda: nc.tensor.transpose(p_[0:2 * NB, j * 128:(j + 1) * 128], zs[:, j, :, :], self.ident[:]), reads=[R("zs"), self.r_const], writes=[rp], inc=(j == 3))
        tk.op(tk.dve, lambda: nc.vector.tensor_copy(cstg, p_[0:2 * NB, :].rearrange("p (q f) -> p q f", q=4)), reads=[rp], writes=[R("hcv")])
        self.out_toks.append(tk.dma(tk.sp, dr["conv_s"].rearrange("b r (q f) -> (b r) q f", q=4), cstg, reads=[R("hcv")]))

    def final(self):
        nc, tk = self.nc, self.tk
        with ExitStack() as les:
            b = self.rmsnorm_bufs("fin", les)
            yT = [self.sb(f"yT{i}", [128, KC, 512], F32, les) for i in range(2)]
            ryT = [tk.R("yT", i) for i in range(2)]
            ostg = [self.sb(f"ostg{i}", [128, D], F32, les) for i in range(2)]
            rost = [tk.R("ostg", i) for i in range(2)]
            bi = 0
            for ti, (t0, n) in enumerate(TILES):
                yb = ti % 2
                self.rmsnorm_tile(ti, PFM_IDX["final"], b, dst_fn=lambda kc: yT[yb][:, kc, 0:n], wr=[ryT[yb]])
                nblk = (n + 127) // 128
                for blk in range(nblk):
                    nn = min(128, n - blk * 128)
                    s = bi % 2
                    for half in range(2):
                        pb = (bi * 2 + half) % 6
                        for q in range(4):
                            kc = half * 4 + q
                            tk.op(tk.pe, lambda: nc.tensor.transpose(self.ps[pb][0:nn, q * 128:(q + 1) * 128],
                                                                    yT[yb][:, kc, blk * 128:blk * 128 + nn], self.ident[:]),
                                  reads=[ryT[yb], self.r_const], writes=[self.rps[pb]], inc=(q == 3))
                        dstv = ostg[s][0:nn, half * 512:(half + 1) * 512]
                        if half == 0:
                            tk.op(tk.dve, lambda: nc.vector.tensor_copy(dstv, self.ps[pb][0:nn, :]), reads=[self.rps[pb]], writes=[rost[s]])
                        else:
                            tk.op(tk.act, lambda: nc.scalar.copy(dstv, self.ps[pb][0:nn, :]), reads=[self.rps[pb]], writes=[rost[s]])
                    if ti < 4:
                        dst = self.dr["y_p"][t0 + blk * 128:t0 + blk * 128 + nn, :]
                    else:
                        dst = self.dr["y_s"][0:nn, :]
                    self.out_toks.append(tk.dma(tk.sp if bi % 2 == 0 else tk.act, dst, ostg[s][0:nn, :], reads=[rost[s]]))
                    bi += 1
            for t in self.out_toks:
                tk._wait(tk.sp, t)

    def build(self):
        nc, tk = self.nc, self.tk
        self.declare()
        self.setup()
        self.eps_t = self.sb("eps_t", [128, 1], F32)
        tk.op(tk.dve, lambda: nc.vector.memset(self.eps_t[:], RMS_EPS), writes=[self.r_const])
        self.eps_ln = self.sb("eps_ln", [128, 1], F32)
        tk.op(tk.dve, lambda: nc.vector.memset(self.eps_ln[:], 1e-5), writes=[self.r_const])
        self.load_x()
        dr = self.dr
        for l in range(self.cfg.get("layers", 2)):
            if self.cfg.get("ffn", True):
                self.ffn(dr["f1_wg"][l], dr["f1_wu"][l], dr["f1_wd"][l], PFM_IDX["norms"] + (l * 4 + 0) * 8, f"f1_{l}")
            if l == 0 and self.cfg.get("mix0", True):
                self.mix0()
            if l == 1 and self.cfg.get("mix1", True):
                self.mix1()
            if self.cfg.get("xattn", True):
                self.xattn(l)
            if self.cfg.get("ffn", True):
                self.ffn(dr["f2_wg"][l], dr["f2_wu"][l], dr["f2_wd"][l], PFM_IDX["norms"] + (l * 4 + 3) * 8, f"f2_{l}")
        self.final()


def build_nc(cfg=None):
    cfg = cfg or {}
    nc = bass.Bass("TRN2", target_bir_lowering=False)
    with ExitStack() as es:
        tk = TK(nc, es)
        k = Kern(nc, es, tk, cfg)
        k.build()
    return nc, k


def make_in_maps(inp):
    consts = _consts()
    pfm, _ = _pack_pfm(inp)
    maps = []
    for c in range(NCORES):
        m = {}
        m["xp"] = np.ascontiguousarray(inp["x_prompt"][c])
        m["xs"] = np.ascontiguousarray(inp["x_sample"][c * NB:(c + 1) * NB].reshape(NS, D))
        m["pfm"] = pfm
        m["ident"] = consts["ident"]
        m["onesm"] = consts["onesm"]
        for nm in ("f1_wg", "f1_wu", "f1_wd", "f2_wg", "f2_wu", "f2_wd", "xa_wq", "xa_wk", "xa_wv", "xa_wo"):
            m[nm] = inp[nm]
        m["ones"] = consts["ones"]
        m["triu"] = consts["triu"]; m["mask_blk"] = consts["mask_blk"]
        for nm in ("tri", "sut", "mask2", "maskxt", "trib", "sutb", "mask2b", "maskxtb", "rowmask", "bones64", "bones"):
            m["c_" + nm] = consts[nm]
        for nm in ("w_in_even", "w_out_even", "decay_w2", "iclr_a2", "gate_g2"):
            m[nm] = inp[nm]
        m["st_shift"] = np.ascontiguousarray(inp["state_shift"][0, c * NB:(c + 1) * NB])
        m["st_wkv"] = np.ascontiguousarray(inp["state_wkv"][0, c * NB:(c + 1) * NB])
        m["st_conv"] = np.ascontiguousarray(inp["state_conv"][0, c * NB:(c + 1) * NB])
        for nm in ("w_in_odd", "w_out_odd", "gm_ln_w", "gm_ln_b", "gm_ws", "gm_bs"):
            m[nm] = inp[nm]
        m["memp"] = np.ascontiguousarray(inp["mem_prompt"][c])
        m["ck"] = np.ascontiguousarray(inp["cache_mem_k"][:, c * NB:(c + 1) * NB].reshape(2, NB, 256, D))
        m["cv"] = np.ascontiguousarray(inp["cache_mem_v"][:, c * NB:(c + 1) * NB].reshape(2, NB, 256, D))
        maps.append(m)
    return maps


def kernel(**inputs):
    inp = {k: np.asarray(v) for k, v in inputs.items()}
    nc, k = build_nc()
    maps = make_in_maps(inp)
    res = run_bass_kernel_spmd(nc, maps, core_ids=list(range(NCORES)))
    r = res.results
    C = range(NCORES)
    f = lambda a: np.ascontiguousarray(np.asarray(a, dtype=np.float32))
    y_p = np.stack([r[c]["y_p"] for c in C], axis=0)
    y_s = np.concatenate([r[c]["y_s"].reshape(NB, 4, D) for c in C], axis=0)
    sh_p = np.stack([r[c]["sh_p"].reshape(PROJ_A) for c in C], axis=0)[None]
    wkv_p = np.stack([r[c]["wkv_p"] for c in C], axis=0)[None]
    conv_p = np.stack([r[c]["conv_p"] for c in C], axis=0)[None]
    mk_p = np.stack([r[c]["mk_p"].reshape(2, 256, 4, 256) for c in C], axis=1)
    mv_p = np.stack([r[c]["mv_p"].reshape(2, 256, 4, 256) for c in C], axis=1)
    sh_s = np.concatenate([r[c]["sh_s"] for c in C], axis=0)[None]
    wkv_s = np.concatenate([r[c]["wkv_s"] for c in C], axis=0)[None]
    conv_s = np.concatenate([r[c]["conv_s"] for c in C], axis=0)[None]
    gv_s = np.concatenate([r[c]["gv_s"].reshape(NB, 4, D) for c in C], axis=0)[None]
    return tuple(f(a) for a in (y_p, y_s, sh_p, wkv_p, conv_p, mk_p, mv_p, sh_s, wkv_s, conv_s, gv_s))
```
